# Optimizing a Trainium2 kernel written in Bass

```python
import math
import jax, jax.numpy as jnp
from jax import lax
import numpy as np

D_MODEL = 1024
BATCH = 8
SEQ = 8192
DEPTH = 4
DEC_BATCH = 8
DEC_SEQ = 2048
PAST_LEN = 128

N_META = 16
GRID_W = 64
N_MIXERS = 2
N_HYENA_LAYERS = (DEPTH + N_MIXERS - 1) // N_MIXERS
N_ATTN_LAYERS = DEPTH // N_MIXERS
HEAD_DIM = 128
N_HEADS = D_MODEL // HEAD_DIM
N_KV_HEADS = N_HEADS // 4
GROUP = N_HEADS // N_KV_HEADS
Q_BLOCK = 128
ROPE_THETA = 10000.0
AXIS_ROT = HEAD_DIM // 2
FILTER_EMB_DIM = 33
FILTER_BANDS = (FILTER_EMB_DIM - 1) // 2
FILTER_HIDDEN = 64
DECAY_TARGET = 1e-2
FAST_DECAY_PCT = 0.3
SLOW_DECAY_PCT = 1.5
DECAY_SHIFT = 0.05
FILTER_NORM_EPS = 1e-6
D_FF = 2816
DEEPNORM_ALPHA = (2 * DEPTH) ** 0.25
DEEPNORM_BETA = (8 * DEPTH) ** -0.25
LN_EPS = 1e-5
RMS_EPS = 1e-6

kernel_name = "hyena_gqa_axial_rope_deepnorm_meta_encoder"


def layer_norm(x, g, b):
    xf = x.astype(jnp.float32)
    mu = jnp.mean(xf, axis=-1, keepdims=True)
    xc = xf - mu
    var = jnp.mean(xc * xc, axis=-1, keepdims=True)
    y = xc * lax.rsqrt(var + LN_EPS) * g.astype(jnp.float32) + b.astype(jnp.float32)
    return y.astype(x.dtype)


def rms_norm_f32(x, g):
    xf = x.astype(jnp.float32)
    return xf * lax.rsqrt(jnp.mean(xf * xf, axis=-1, keepdims=True) + RMS_EPS) * g.astype(jnp.float32)


def dwconv3(x, w, b):
    xp = jnp.pad(x, ((0, 0), (1, 1), (0, 0)))
    return xp[:, :-2] * w[0] + xp[:, 1:-1] * w[1] + xp[:, 2:] * w[2] + b


def hyena_filter_spectrum(L, w1, b1, w2, b2, w3, b3, freq):
    f32 = jnp.float32
    pos = jnp.arange(L, dtype=f32)
    t = (pos / (L - 1))[:, None]
    w = (2.0 * math.pi / L) * pos[:, None]
    bands = jnp.linspace(1e-4, FILTER_BANDS - 1, FILTER_BANDS, dtype=f32)
    z = jnp.concatenate([t, jnp.cos(w * bands), -jnp.sin(w * bands)], axis=-1)
    fr = freq.astype(f32)
    h = jnp.sin(fr * (z @ w1.astype(f32) + b1.astype(f32)))
    h = jnp.sin(fr * (h @ w2.astype(f32) + b2.astype(f32)))
    h = h @ w3.astype(f32) + b3.astype(f32)
    max_decay = math.log(1.0 / DECAY_TARGET) / FAST_DECAY_PCT
    min_decay = math.log(1.0 / DECAY_TARGET) / SLOW_DECAY_PCT
    deltas = jnp.linspace(min_decay, max_decay, D_MODEL, dtype=f32)
    window = jnp.exp(-t * deltas) + DECAY_SHIFT
    h = h * jnp.concatenate([window, window], axis=-1)
    h = h / (jnp.sum(jnp.abs(h), axis=0, keepdims=True) + FILTER_NORM_EPS)
    hf, hb = h[:, :D_MODEL], h[:, D_MODEL:]
    k = jnp.concatenate([hf, jnp.zeros((1, D_MODEL), f32), hb[1:][::-1]], axis=0)
    return jnp.fft.rfft(k, axis=0)


def hyena_mixer(x, w_in, conv_w, conv_b, fw1, fb1, fw2, fb2, fw3, fb3, ffreq, skip, w_out):
    B, L, _ = x.shape
    u = dwconv3(x @ w_in, conv_w, conv_b)
    x0, x1, v = jnp.split(u, 3, axis=-1)
    v = (v * x1).astype(jnp.float32)
    kf = hyena_filter_spectrum(L, fw1, fb1, fw2, fb2, fw3, fb3, ffreq)
    y = jnp.fft.irfft(jnp.fft.rfft(v, n=2 * L, axis=1) * kf[None], n=2 * L, axis=1)[:, :L]
    y = y + v * skip.astype(jnp.float32)
    y = y.astype(x.dtype) * x0
    return y @ w_out


def axial_rope_tables(n_tokens):
    rows = n_tokens // GRID_W
    f32 = jnp.float32
    row = jnp.concatenate([jnp.full((N_META,), -1.0, f32),
                           jnp.repeat(jnp.arange(rows, dtype=f32), GRID_W)])
    col = jnp.concatenate([jnp.arange(N_META, dtype=f32),
                           jnp.tile(jnp.arange(GRID_W, dtype=f32), rows)])
    inv_freq = ROPE_THETA ** (-jnp.arange(0, AXIS_ROT, 2, dtype=f32) / AXIS_ROT)
    ang = jnp.concatenate([row[:, None] * inv_freq, col[:, None] * inv_freq], axis=-1)
    return jnp.cos(ang), jnp.sin(ang)


def apply_rope(x, cos, sin):
    half = HEAD_DIM // 2
    x1, x2 = x[..., :half], x[..., half:]
    c = cos[None, :, None, :]
    s = sin[None, :, None, :]
    return jnp.concatenate([x1 * c - x2 * s, x2 * c + x1 * s], axis=-1)


def attend(q, k, v):
    s = jnp.einsum('bqhgd,bkhd->bhgqk', q, k).astype(jnp.float32) * (HEAD_DIM ** -0.5)
    p = jax.nn.softmax(s, axis=-1).astype(v.dtype)
    return jnp.einsum('bhgqk,bkhd->bqhgd', p, v)


def attention_mixer(x, w_qkv, q_gain, k_gain, w_out, cos, sin):
    B, L, _ = x.shape
    nq = N_HEADS * HEAD_DIM
    nk = N_KV_HEADS * HEAD_DIM
    qkv = x @ w_qkv
    q = qkv[..., :nq].reshape(B, L, N_HEADS, HEAD_DIM)
    k = qkv[..., nq:nq + nk].reshape(B, L, N_KV_HEADS, HEAD_DIM)
    v = qkv[..., nq + nk:].reshape(B, L, N_KV_HEADS, HEAD_DIM)
    q = apply_rope(rms_norm_f32(q, q_gain), cos, sin).astype(x.dtype)
    k = apply_rope(rms_norm_f32(k, k_gain), cos, sin).astype(x.dtype)
    q = q.reshape(B, L, N_KV_HEADS, GROUP, HEAD_DIM)
    o_meta = attend(q[:, :N_META], k, v)
    n = L - N_META
    nblk = n // Q_BLOCK
    qb = q[:, N_META:].reshape(B, nblk, Q_BLOCK, N_KV_HEADS, GROUP, HEAD_DIM).transpose(1, 0, 2, 3, 4, 5)
    ob = lax.map(lambda qi: attend(qi, k, v), qb)
    o_real = ob.transpose(1, 0, 2, 3, 4, 5).reshape(B, n, N_KV_HEADS, GROUP, HEAD_DIM)
    o = jnp.concatenate([o_meta, o_real], axis=1).reshape(B, L, nq)
    return o @ w_out


def conv_ffn(x, w_in, conv_w, conv_b, w_out):
    h = x @ w_in
    g, a = jnp.split(h, 2, axis=-1)
    g = dwconv3(g, conv_w, conv_b)
    return (jax.nn.gelu(g, approximate=False) * a) @ w_out


def run_trunk(x, meta_tokens, hy_w_in, hy_conv_w, hy_conv_b, hy_filt_w1, hy_filt_b1, hy_filt_w2,
              hy_filt_b2, hy_filt_w3, hy_filt_b3, hy_filt_freq, hy_skip, hy_w_out, at_w_qkv,
              at_q_gain, at_k_gain, at_w_out, ln1_g, ln1_b, ln2_g, ln2_b, ffn_w_in, ffn_conv_w,
              ffn_conv_b, ffn_w_out):
    B, n, _ = x.shape
    meta = jnp.broadcast_to(meta_tokens.astype(x.dtype)[None], (B, N_META, D_MODEL))
    h = jnp.concatenate([meta, x], axis=1)
    cos, sin = axial_rope_tables(n)
    for i in range(DEPTH):
        j = i // N_MIXERS
        if i % N_MIXERS == 0:
            m = hyena_mixer(h, hy_w_in[j], hy_conv_w[j], hy_conv_b[j], hy_filt_w1[j], hy_filt_b1[j],
                            hy_filt_w2[j], hy_filt_b2[j], hy_filt_w3[j], hy_filt_b3[j],
                            hy_filt_freq[j], hy_skip[j], hy_w_out[j])
        else:
            m = attention_mixer(h, at_w_qkv[j], at_q_gain[j], at_k_gain[j], at_w_out[j], cos, sin)
        h = layer_norm(DEEPNORM_ALPHA * h + m, ln1_g[i], ln1_b[i])
        f = conv_ffn(h, ffn_w_in[i], ffn_conv_w[i], ffn_conv_b[i], ffn_w_out[i])
        h = layer_norm(DEEPNORM_ALPHA * h + f, ln2_g[i], ln2_b[i])
    return h[:, N_META:]


def setup_inputs(seed: int = 0) -> dict:
    key = jax.random.key(seed)
    ks = iter(jax.random.split(key, 40))
    f32 = jnp.float32
    D = D_MODEL
    NH, NA = N_HYENA_LAYERS, N_ATTN_LAYERS
    qkv_w = (N_HEADS + 2 * N_KV_HEADS) * HEAD_DIM

    def nrm(shape, scale):
        return jax.random.normal(next(ks), shape, f32) * scale

    return {
        "x_prompt": nrm((BATCH, SEQ, D), 1.0),
        "x_sample": nrm((DEC_BATCH, DEC_SEQ, D), 1.0),
        "meta_tokens": nrm((N_META, D), 1.0),
        "hy_w_in": nrm((NH, D, 3 * D), D ** -0.5),
        "hy_conv_w": nrm((NH, 3, 3 * D), 3 ** -0.5),
        "hy_conv_b": nrm((NH, 3 * D), 0.02),
        "hy_filt_w1": nrm((NH, FILTER_EMB_DIM, FILTER_HIDDEN), FILTER_EMB_DIM ** -0.5),
        "hy_filt_b1": nrm((NH, FILTER_HIDDEN), 0.02),
        "hy_filt_w2": nrm((NH, FILTER_HIDDEN, FILTER_HIDDEN), FILTER_HIDDEN ** -0.5),
        "hy_filt_b2": nrm((NH, FILTER_HIDDEN), 0.02),
        "hy_filt_w3": nrm((NH, FILTER_HIDDEN, 2 * D), FILTER_HIDDEN ** -0.5),
        "hy_filt_b3": nrm((NH, 2 * D), 0.02),
        "hy_filt_freq": 1.0 + nrm((NH, FILTER_HIDDEN), 0.05),
        "hy_skip": nrm((NH, D), 0.5),
        "hy_w_out": nrm((NH, D, D), DEEPNORM_BETA * D ** -0.5),
        "at_w_qkv": nrm((NA, D, qkv_w), D ** -0.5),
        "at_q_gain": 1.0 + nrm((NA, HEAD_DIM), 0.05),
        "at_k_gain": 1.0 + nrm((NA, HEAD_DIM), 0.05),
        "at_w_out": nrm((NA, N_HEADS * HEAD_DIM, D), DEEPNORM_BETA * (N_HEADS * HEAD_DIM) ** -0.5),
        "ln1_g": 1.0 + nrm((DEPTH, D), 0.05),
        "ln1_b": nrm((DEPTH, D), 0.02),
        "ln2_g": 1.0 + nrm((DEPTH, D), 0.05),
        "ln2_b": nrm((DEPTH, D), 0.02),
        "ffn_w_in": nrm((DEPTH, D, 2 * D_FF), D ** -0.5),
        "ffn_conv_w": nrm((DEPTH, 3, D_FF), 3 ** -0.5),
        "ffn_conv_b": nrm((DEPTH, D_FF), 0.02),
        "ffn_w_out": nrm((DEPTH, D_FF, D), DEEPNORM_BETA * D_FF ** -0.5),
    }


def reference(x_prompt, x_sample, meta_tokens, hy_w_in, hy_conv_w, hy_conv_b, hy_filt_w1, hy_filt_b1,
              hy_filt_w2, hy_filt_b2, hy_filt_w3, hy_filt_b3, hy_filt_freq, hy_skip, hy_w_out,
              at_w_qkv, at_q_gain, at_k_gain, at_w_out, ln1_g, ln1_b, ln2_g, ln2_b, ffn_w_in,
              ffn_conv_w, ffn_conv_b, ffn_w_out):
    weights = (meta_tokens, hy_w_in, hy_conv_w, hy_conv_b, hy_filt_w1, hy_filt_b1, hy_filt_w2,
               hy_filt_b2, hy_filt_w3, hy_filt_b3, hy_filt_freq, hy_skip, hy_w_out, at_w_qkv,
               at_q_gain, at_k_gain, at_w_out, ln1_g, ln1_b, ln2_g, ln2_b, ffn_w_in, ffn_conv_w,
               ffn_conv_b, ffn_w_out)
    y_prompt = run_trunk(x_prompt, *weights)
    y_sample = run_trunk(x_sample, *weights)
    return (y_prompt, y_sample)
```

```python
import contextlib
import math
import os

import numpy as np
import concourse.bass as bass
import concourse.mybir as mybir
from concourse.bass_utils import run_bass_kernel_spmd

F32 = mybir.dt.float32
BF16 = mybir.dt.bfloat16
AF = mybir.ActivationFunctionType
ALU = mybir.AluOpType
AX = mybir.AxisListType

D = 1024
DFF = 2816
NFF = DFF // 128
DEPTH = 4
NMETA = 16
ALPHA = (2 * DEPTH) ** 0.25
LN_EPS = 1e-5
RMS_EPS = 1e-6
HD = 128
NH = 8
NKV = 2
PI = math.pi
FDT = BF16
CG = 4


class Stream:
    __slots__ = ("name", "sem", "ops", "step", "is_pe")

    def __init__(self, name, sem, step, is_pe=False):
        self.name, self.sem, self.ops, self.step, self.is_pe = name, sem, [], step, is_pe


class Op:
    __slots__ = ("cs", "fn", "deps", "needed", "count", "raw")

    def __init__(self, cs, fn, deps):
        self.cs, self.fn, self.needed, self.count = cs, fn, False, 0
        self.deps, self.raw = deps


class Buf:
    __slots__ = ("w", "r", "ld", "st")

    def __init__(self):
        self.w, self.r, self.ld, self.st = None, [], None, None


class _Rec:
    def __init__(self):
        self.call = None

    def __getattr__(self, name):
        def f(*a, **k):
            assert self.call is None, "op closure must make exactly one engine call"
            self.call = (name, a, k)
            return self
        return f


def _bind(fn):
    rec = _Rec()
    fn(rec)
    name, a, k = rec.call
    return lambda h: getattr(h, name)(*a, **k)


class Prog:
    ENGS = ("pe", "dve", "act", "pool", "sp")

    def __init__(self, nc, es, same_sync=True):
        self.nc, self.es, self.same_sync = nc, es, same_sync
        self.cs = {}
        self.issue = {e: [] for e in self.ENGS}
        for e in self.ENGS:
            self.cs[e] = Stream(e, es.enter_context(nc.semaphore("sem_" + e)), 1, e == "pe")
        self.dma_streams = []
        self.free_streams = []
        self.rr = 0
        self.store_q = "pool"

    def buf(self):
        return Buf()

    def release(self, bufs):
        for b in bufs:
            for s in (b.ld, b.st):
                if s is not None:
                    self.free_streams.append(s)
            b.ld = b.st = None

    def _dstream(self):
        if self.free_streams:
            return self.free_streams.pop()
        s = Stream("d%d" % len(self.dma_streams),
                   self.es.enter_context(self.nc.semaphore("dsem%d" % len(self.dma_streams))), 16)
        self.dma_streams.append(s)
        return s

    @staticmethod
    def _deps(reads, writes):
        deps = []
        raw = set()
        for b in reads:
            if b.w is not None:
                deps.append(b.w)
                raw.add(id(b.w))
        for b in writes:
            if b.w is not None:
                deps.append(b.w)
            deps.extend(b.r)
        return deps, raw

    @staticmethod
    def _commit(op, reads, writes):
        for b in reads:
            b.r.append(op)
        for b in writes:
            b.w = op
            b.r = []

    def op(self, eng, fn, reads=(), writes=()):
        cs = self.cs[eng]
        o = Op(cs, _bind(fn), self._deps(reads, writes))
        cs.ops.append(o)
        self.issue[eng].append(o)
        self._commit(o, reads, writes)
        return o

    def dma(self, fn, reads=(), writes=(), queue=None):
        if queue is None:
            queue = "sp" if writes else self.store_q
        if writes:
            track = writes[0]
            if track.ld is None:
                track.ld = self._dstream()
            cs = track.ld
        else:
            track = reads[0]
            if track.st is None:
                track.st = self._dstream()
            cs = track.st
        o = Op(cs, _bind(fn), self._deps(reads, writes))
        o.needed = True
        cs.ops.append(o)
        self.issue[queue].append(o)
        self._commit(o, reads, writes)
        return o

    def barrier(self):
        lasts = []
        for s in list(self.cs.values()) + self.dma_streams:
            if s.ops:
                lasts.append(s.ops[-1])
        for e in self.ENGS:
            o = Op(self.cs[e], None, (list(lasts), set(id(x) for x in lasts)))
            self.cs[e].ops.append(o)
            self.issue[e].append(o)

    def finalize_counts(self):
        for e in self.ENGS:
            own = self.cs[e]
            for o in self.issue[e]:
                for d in o.deps:
                    if d.cs is own and (own.is_pe or not self.same_sync or id(d) not in o.raw):
                        continue
                    d.needed = True
        for s in list(self.cs.values()) + self.dma_streams:
            c = 0
            for o in s.ops:
                if o.needed and o.fn is not None:
                    c += s.step
                o.count = c

    def emit(self, eng, h):
        own = self.cs[eng]
        seen = {}
        for o in self.issue[eng]:
            need = {}
            for d in o.deps:
                if d.cs is own and (own.is_pe or not self.same_sync or id(d) not in o.raw):
                    continue
                if d.count == 0 or seen.get(d.cs.name, 0) >= d.count:
                    continue
                if need.get(d.cs.name, (None, 0))[1] < d.count:
                    need[d.cs.name] = (d.cs, d.count)
            for nm, (s, c) in need.items():
                h.wait_ge(s.sem, c)
                seen[nm] = c
            if o.fn is None:
                continue
            ins = o.fn(h)
            if o.needed:
                ins.then_inc(o.cs.sem, o.cs.step)

    def run_block(self):
        self.barrier()
        self.finalize_counts()
        with self.nc.Block() as block:
            @block.tensor
            def _(h):
                self.emit("pe", h)

            @block.vector
            def _(h):
                self.emit("dve", h)

            @block.scalar
            def _(h):
                self.emit("act", h)

            @block.gpsimd
            def _(h):
                self.emit("pool", h)

            @block.sync
            def _(h):
                self.emit("sp", h)


def cols(v):
    v = np.asarray(v, np.float32)
    return np.ascontiguousarray(v.reshape(-1, 128).T)


class Group:
    def __init__(self, name, n_tok):
        self.name = name
        self.n = n_tok
        self.L = n_tok + NMETA
        self.M = 128 if n_tok > 4096 else 64
        self.N = self.M * self.M
        self.H = self.N // 2
        self.Hh = self.M // 2
        self.nk = self.Hh + 1
        self.nin = self.L // self.M + 1
        self.tail = self.L - (self.nin - 1) * self.M
        self.Lp = self.nin * self.M
        self.CG = 4 if self.M == 128 else 8
        assert self.L - self.H == 16 and self.tail == 16


def fft_consts(g):
    M, N, Hh, nk, nin = g.M, g.N, g.Hh, g.nk, g.nin
    c = {}
    n1 = np.arange(M)[:, None].astype(np.float64)
    k1 = np.arange(nk)[None, :].astype(np.float64)
    th = 2 * np.pi * n1 * k1 / M
    f1 = np.zeros((M, 2 * Hh))
    f1[:, :nk] = np.cos(th)
    f1[:, nk:] = -np.sin(th[:, 1:Hh])
    c["f1"] = f1
    n2 = np.arange(M)[:, None].astype(np.float64)
    k2 = np.arange(M)[None, :].astype(np.float64)
    th2 = 2 * np.pi * n2 * k2 / M
    c["c2"] = np.cos(th2)
    c["s2"] = np.sin(th2)
    c["ns2"] = -np.sin(th2)
    c["if2a"] = np.concatenate([np.cos(th2), np.sin(th2)], axis=1)
    c["if2b"] = np.concatenate([-np.sin(th2), np.cos(th2)], axis=1)
    tht = 2 * np.pi * n2 * k1 / N
    c["twc"] = np.cos(tht)
    c["tws"] = np.sin(tht)
    c["ntws"] = -np.sin(tht)
    c["twct"] = np.cos(tht).T.copy()
    c["twst"] = np.sin(tht).T.copy()
    c["ntwst"] = -np.sin(tht).T.copy()
    wk = np.full((nk, 1), 2.0)
    wk[0] = 1.0
    wk[Hh] = 1.0
    k1c = np.arange(nk)[:, None].astype(np.float64)
    n1r = np.arange(nin)[None, :].astype(np.float64)
    thg = 2 * np.pi * k1c * n1r / M
    c["gc"] = wk * np.cos(thg) / N
    c["gs"] = -wk * np.sin(thg) / N
    return {k: np.ascontiguousarray(v, dtype=np.float32) for k, v in c.items()}


def filter_grids(g):
    N, H, L = g.N, g.H, g.L
    j = np.arange(N)
    pos_c = np.where(j < H, j, N - j).astype(np.float64)
    e_pos = np.concatenate([[0], np.arange(H - 15, H + 16)]).astype(np.float64)
    e2 = np.concatenate([[0], e_pos[1:][::-1]])
    pos_e = np.concatenate([e_pos, e2])
    pos = np.concatenate([pos_c, pos_e])
    t = pos / (L - 1)
    w = (2.0 * np.pi / L) * pos
    bands = np.linspace(1e-4, 15, 16)
    z = np.concatenate([t[:, None], np.cos(w[:, None] * bands), -np.sin(w[:, None] * bands)], axis=1)
    zt = np.ascontiguousarray(z.T, dtype=np.float32)
    tb = np.ascontiguousarray(np.broadcast_to(t[None, :].astype(np.float32), (128, N + 64)))
    return zt, tb


def rope_tables(g):
    n = g.n
    rows = n // 64
    row = np.concatenate([np.full((NMETA,), -1.0), np.repeat(np.arange(rows), 64)]).astype(np.float32)
    col = np.concatenate([np.arange(NMETA), np.tile(np.arange(64), rows)]).astype(np.float32)
    inv = (np.float32(10000.0) ** (-np.arange(0, 64, 2, dtype=np.float32) / np.float32(64))).astype(np.float32)
    ang = np.concatenate([row[:, None] * inv, col[:, None] * inv], axis=1)
    cs = np.cos(ang).T
    sn = np.sin(ang).T
    cos_t = np.concatenate([cs, cs], axis=0)
    sin_t = np.concatenate([-sn, sn], axis=0)
    return np.ascontiguousarray(cos_t, np.float32), np.ascontiguousarray(sin_t, np.float32)


def tiles_even(L, tmax):
    n = -(-L // tmax)
    T = -(-L // n)
    out = []
    t0 = 0
    while t0 < L:
        out.append((t0, min(T, L - t0)))
        t0 += T
    return out


class VecMap:
    def __init__(self):
        self.off = {}
        self.n = 0
        self.parts = []

    def add(self, name, arr):
        a = np.asarray(arr, np.float32)
        if a.shape[0] != 128:
            a = np.concatenate([a, np.zeros((128 - a.shape[0], a.shape[1]), np.float32)], axis=0)
        self.off[name] = self.n
        self.n += a.shape[1]
        self.parts.append(a)

    def build(self):
        return np.ascontiguousarray(np.concatenate(self.parts, axis=1))


def build_vecs(inp):
    vm = VecMap()
    for i in range(DEPTH):
        vm.add("ln1g%d" % i, cols(inp["ln1_g"][i]))
        vm.add("ln1b%d" % i, cols(inp["ln1_b"][i]))
        vm.add("ln2g%d" % i, cols(inp["ln2_g"][i]))
        vm.add("ln2b%d" % i, cols(inp["ln2_b"][i]))
        for tap in range(3):
            vm.add("fcw%d_%d" % (i, tap), cols(inp["ffn_conv_w"][i, tap]))
        vm.add("fcb%d" % i, cols(inp["ffn_conv_b"][i]))
    for j in range(2):
        for tap in range(3):
            vm.add("hcw%d_%d" % (j, tap), cols(inp["hy_conv_w"][j, tap]))
        vm.add("hcb%d" % j, cols(inp["hy_conv_b"][j]))
        vm.add("skip%d" % j, cols(inp["hy_skip"][j]))
        vm.add("fb3_%d" % j, cols(inp["hy_filt_b3"][j]))
        vm.add("fb1_%d" % j, np.asarray(inp["hy_filt_b1"][j], np.float32)[:, None])
        vm.add("fb2_%d" % j, np.asarray(inp["hy_filt_b2"][j], np.float32)[:, None])
        vm.add("ffr_%d" % j, np.asarray(inp["hy_filt_freq"][j], np.float32)[:, None])
        qg = np.asarray(inp["at_q_gain"][j], np.float32)
        kg = np.asarray(inp["at_k_gain"][j], np.float32)
        vm.add("qg%d" % j, qg[:, None])
        vm.add("kg%d" % j, kg[:, None])
    max_decay = math.log(1.0 / 1e-2) / 0.3
    min_decay = math.log(1.0 / 1e-2) / 1.5
    deltas = np.linspace(min_decay, max_decay, D, dtype=np.float32)
    vm.add("ndelta", cols(-deltas))
    return vm, vm.build()


class Builder:
    def __init__(self, groups, vm, dbg=None):
        self.groups = groups
        self.vm = vm
        self.dbg = dbg or {}
        self.nc = bass.Bass("TRN2", target_bir_lowering=False)
        self.es = contextlib.ExitStack()
        self.P = None
        self.dram = {}

    def din(self, name, shape, dt=F32):
        t = self.nc.dram_tensor(name, list(shape), dt, kind="ExternalInput").ap()
        self.dram[name] = t
        return t

    def dout(self, name, shape, dt=F32):
        t = self.nc.dram_tensor(name, list(shape), dt, kind="ExternalOutput").ap()
        self.dram[name] = t
        return t

    def dscr(self, name, shape, dt=F32):
        kind = "ExternalOutput" if name in self.dbg.get("outs", ()) else "Internal"
        t = self.nc.dram_tensor(name, list(shape), dt, kind=kind).ap()
        self.dram[name] = t
        return t

    def sb(self, st, name, shape, dt=F32):
        self.uid = getattr(self, "uid", 0) + 1
        return st.enter_context(self.nc.sbuf_tensor("sb%d_%s" % (self.uid, name), list(shape), dt))

    def vcol(self, name, k=0, p=128):
        o = self.vm.off[name] + k
        return self.vecs[0:p, o:o + 1]

    def build(self):
        nc, es = self.nc, self.es
        with es:
            self.P = P = Prog(nc, es, same_sync=bool(self.dbg.get("same_sync", True)))
            self.declare_io()
            self.bank = [es.enter_context(nc.psum_tensor("bank%d" % i, [128, 512], F32)) for i in range(8)]
            self.bbuf = [P.buf() for _ in range(8)]
            self.vecs = self.sb(es, "vecs", [128, self.vm.n])
            self.bvecs = P.buf()
            P.dma(lambda h: h.dma_start(out=self.vecs[:], in_=self.dram["vecs"][:, :]), writes=[self.bvecs])
            self.ident = self.sb(es, "ident", [128, 128])
            self.bident = P.buf()
            P.dma(lambda h: h.dma_start(out=self.ident[:], in_=self.dram["ident"][:, :]), writes=[self.bident])
            self.onesD = self.sb(es, "onesD", [128, 128], BF16)
            self.onesH = self.sb(es, "onesH", [128, 128], BF16)
            self.ones1 = self.sb(es, "ones1", [128, 128], BF16)
            self.bones = P.buf()
            P.op("dve", lambda h: h.memset(self.onesD[:], 1.0 / D), writes=[self.bones])
            P.op("dve", lambda h: h.memset(self.onesH[:], 1.0 / HD), writes=[self.bones])
            P.op("dve", lambda h: h.memset(self.ones1[:], 1.0), writes=[self.bones])
            self.zero = self.sb(es, "zero", [128, 8], BF16)
            self.bzero = P.buf()
            P.op("dve", lambda h: h.memset(self.zero[:], 0.0), writes=[self.bzero])
            stop = self.dbg.get("stop", "end")
            self.phase_prep_weights()
            for g in self.groups:
                if stop != "setup":
                    self.phase_input(g)
            P.barrier()
            done = (stop in ("input", "setup"))
            for i in range(DEPTH):
                if done:
                    break
                for g in self.groups:
                    if i % 2 == 0:
                        self.layer_hyena(i, g)
                    else:
                        self.layer_attn(i, g)
                    if stop == "mix%d" % i:
                        continue
                    self.phase_ffn(i, g)
                if stop in ("mix%d" % i, "ffn%d" % i):
                    done = True
            if not done:
                for g in self.groups:
                    self.phase_output(g)
            P.run_block()
        return nc

    def declare_io(self):
        for g in self.groups:
            self.din("x_" + g.name, [g.n, D])
            self.dout("y_" + g.name, [g.n, D])
            self.din("cos_" + g.name, [128, g.L])
            self.din("sin_" + g.name, [128, g.L])
            self.din("zt_" + g.name, [33, g.N + 64])
            self.din("tb_" + g.name, [128, g.N + 64])
            for k in ("f1", "c2", "s2", "ns2", "if2a", "if2b", "twc", "tws", "ntws", "twct", "twst", "ntwst", "gc", "gs"):
                shp = {"f1": [g.M, 2 * g.Hh], "c2": [g.M, g.M], "s2": [g.M, g.M], "ns2": [g.M, g.M],
                       "if2a": [g.M, 2 * g.M], "if2b": [g.M, 2 * g.M], "twc": [g.M, g.nk], "tws": [g.M, g.nk],
                       "ntws": [g.M, g.nk], "twct": [g.nk, g.M], "twst": [g.nk, g.M], "ntwst": [g.nk, g.M],
                       "gc": [g.nk, g.nin], "gs": [g.nk, g.nin]}[k]
                self.din("%s_%s" % (k, g.name), shp)
            for pp in range(2):
                self.dscr("hf%d_%s" % (pp, g.name), [D, g.L])
                self.dscr("hb%d_%s" % (pp, g.name), [D, g.L + 2], BF16)
            self.dscr("x0_" + g.name, [D, g.L])
            self.dscr("vg_" + g.name, [D, g.L])
            self.dscr("yc_" + g.name, [D, g.Lp])
            self.dscr("kun_" + g.name, [D, g.N])
            self.dscr("kf_" + g.name, [D // g.CG, g.M, g.CG * 2 * g.nk])
            self.dscr("rn_" + g.name, [2, D])
            self.dscr("qT_" + g.name, [D, g.L], BF16)
            self.dscr("xo_" + g.name, [D, g.L], BF16)
        self.din("vecs", [128, self.vm.n])
        self.din("ident", [128, 128])
        self.din("meta_tokens", [NMETA, D])
        self.din("hy_w_in", [2, D, 3 * D])
        self.din("hy_w_out", [2, D, D])
        self.din("hy_filt_w1", [2, 33, 64])
        self.din("hy_filt_w2", [2, 64, 64])
        self.din("hy_filt_w3", [2, 64, 2 * D])
        self.din("at_w_qkv", [2, D, 1536])
        self.din("at_w_out", [2, D, D])
        self.din("ffn_w_in", [DEPTH, D, 2 * DFF])
        self.din("ffn_w_out", [DEPTH, DFF, D])
        self.dscr("b_hy_w_in", [2, D, 3 * D], BF16)
        self.dscr("b_hy_w_out", [2, D, D], BF16)
        self.dscr("b_at_w_qkv", [2, D, 1536], BF16)
        self.dscr("b_at_w_out", [2, D, D], BF16)
        self.dscr("b_ffn_w1", [DEPTH, NFF, 128, 2, 8, 128], BF16)
        self.dscr("b_ffn_w_out", [DEPTH, DFF, D], BF16)

    def phase_prep_weights(self):
        P, nc = self.P, self.nc
        if self.dbg.get("skip_prep"):
            return
        with contextlib.ExitStack() as st:
            NS = 4
            CW = 3072
            src = [self.sb(st, "pw_s%d" % i, [128, CW]) for i in range(NS)]
            dst = [self.sb(st, "pw_d%d" % i, [128, CW], BF16) for i in range(NS)]
            bs = [P.buf() for _ in range(NS)]
            bd = [P.buf() for _ in range(NS)]
            cnt = [0]
            engs = ("dve", "act")

            def job(src_ap, ncols, dst_ap_fn):
                i = cnt[0] % NS
                e = engs[cnt[0] % 2]
                cnt[0] += 1
                P.dma(lambda h: h.dma_start(out=src[i][:, 0:ncols], in_=src_ap), writes=[bs[i]])
                if e == "act":
                    P.op("act", lambda h: h.copy(out=dst[i][:, 0:ncols], in_=src[i][:, 0:ncols]), reads=[bs[i]], writes=[bd[i]])
                else:
                    P.op(e, lambda h: h.tensor_copy(out=dst[i][:, 0:ncols], in_=src[i][:, 0:ncols]), reads=[bs[i]], writes=[bd[i]])
                o, s = dst_ap_fn(dst[i])
                P.dma(lambda h: h.dma_start(out=o, in_=s), reads=[bd[i]])

            def plain(name, lead, K, Ncol):
                s = self.dram[name]
                d = self.dram["b_" + name]
                for l in range(lead):
                    for k in range(K // 128):
                        for c0 in range(0, Ncol, CW):
                            w = min(CW, Ncol - c0)
                            job(s[l, k * 128:(k + 1) * 128, c0:c0 + w], w,
                                (lambda l=l, k=k, c0=c0, w=w: (lambda t: (d[l, k * 128:(k + 1) * 128, c0:c0 + w], t[:, 0:w])))())
            plain("hy_w_in", 2, D, 3 * D)
            plain("hy_w_out", 2, D, D)
            plain("at_w_qkv", 2, D, 1536)
            plain("at_w_out", 2, D, D)
            plain("ffn_w_out", DEPTH, DFF, D)
            s = self.dram["ffn_w_in"]
            d = self.dram["b_ffn_w1"]
            for l in range(DEPTH):
                for k in range(8):
                    for half in range(2):
                        def mk(l=l, k=k, half=half):
                            def f(t):
                                o = d[l, :, :, half, k, :].rearrange("n p c -> p n c")
                                return o, t[:, 0:DFF].rearrange("p (n c) -> p n c", c=128)
                            return f
                        job(s[l, k * 128:(k + 1) * 128, half * DFF:(half + 1) * DFF], DFF, mk())
            P.barrier()
            P.release(bs + bd)

    def phase_input(self, g):
        P = self.P
        hf = self.dram["hf0_" + g.name]
        hb = self.dram["hb0_" + g.name]
        x = self.dram["x_" + g.name]
        with contextlib.ExitStack() as st:
            xt = [self.sb(st, "in_x%d" % i, [128, 4, D]) for i in range(2)]
            bx = [P.buf() for _ in range(2)]
            of = [self.sb(st, "in_of%d" % i, [128, 8, 512]) for i in range(2)]
            ob = [self.sb(st, "in_ob%d" % i, [128, 8, 512], BF16) for i in range(2)]
            bof = [P.buf() for _ in range(2)]
            bob = [P.buf() for _ in range(2)]
            for pp in range(0 if not self.dbg.get("no_halo") else 2, 2):
                hbp = self.dram["hb%d_%s" % (pp, g.name)].rearrange("(k p) t -> p k t", p=128)
                for c in (0, g.L + 1):
                    P.dma(lambda h, hbp=hbp, c=c: h.dma_start(out=hbp[:, :, c:c + 1], in_=self.zero[:, 0:8].rearrange("p (k o) -> p k o", o=1), allow_slow_non_contiguous=True),
                          reads=[self.bzero])
            jobs = [("meta", 0, NMETA)] + [("x", t0, 512) for t0 in range(0, g.n, 512)]
            if self.dbg.get("in_nometa"):
                jobs = jobs[1:]
            for ji, (kind, t0, T) in enumerate(jobs):
                s = ji % 2
                nch = (T + 127) // 128
                if kind == "meta":
                    P.dma(lambda h, s=s: h.dma_start(out=xt[s][0:NMETA, 0, :], in_=self.dram["meta_tokens"][:, :]), writes=[bx[s]])
                    dcol = 0
                else:
                    P.dma(lambda h, s=s, t0=t0: h.dma_start(out=xt[s][:, :, :], in_=x[t0:t0 + 512, :].rearrange("(j p) d -> p j d", p=128)), writes=[bx[s]])
                    dcol = NMETA + t0
                for k in range(8):
                    bk = k % 2
                    for j in range(nch):
                        tp = min(128, T - j * 128)
                        P.op("pe", lambda h, s=s, k=k, j=j, tp=tp, bk=bk: h.transpose(self.bank[bk][:, j * 128:j * 128 + tp], xt[s][0:tp, j, k * 128:(k + 1) * 128], self.ident[0:tp, 0:tp]),
                             reads=[bx[s], self.bident], writes=[self.bbuf[bk]])
                    P.op("dve", lambda h, s=s, k=k, bk=bk, T=T: h.tensor_copy(out=of[s][:, k, 0:T], in_=self.bank[bk][:, 0:T]), reads=[self.bbuf[bk]], writes=[bof[s]])
                    P.op("act", lambda h, s=s, k=k, T=T: h.copy(out=ob[s][:, k, 0:T], in_=of[s][:, k, 0:T]), reads=[bof[s]], writes=[bob[s]])
                if not self.dbg.get("in_nohf"):
                    P.dma(lambda h, s=s, T=T, dcol=dcol: h.dma_start(out=hf.rearrange("(k p) t -> p k t", p=128)[:, :, dcol:dcol + T], in_=of[s][:, :, 0:T]), reads=[bof[s]])
                if not self.dbg.get("in_nohb"):
                    P.dma(lambda h, s=s, T=T, dcol=dcol: h.dma_start(out=hb.rearrange("(k p) t -> p k t", p=128)[:, :, 1 + dcol:1 + dcol + T], in_=ob[s][:, :, 0:T]), reads=[bob[s]])
            P.barrier()
            P.release(bx + bof + bob)

    def res_ln(self, st_bufs, i, which, g, t0, T, out_pp, acc_fn, defer=False):
        P = self.P
        S = st_bufs
        hres, bhres = S["hres"], S["bhres"]
        r, br = S["r"], S["br"]
        rb, brb = S["rb"], S["brb"]
        r2, br2 = S["r2"], S["br2"]
        ob, bob = S["ob"], S["bob"]
        mean, bmean = S["mean"], S["bmean"]
        rstd, brstd = S["rstd"], S["brstd"]
        gname = "ln%dg%d" % (which, i)
        bname = "ln%db%d" % (which, i)
        for k in range(8):
            bk = S["obanks"][k % len(S["obanks"])]
            acc_fn(k, bk)
            P.op("dve", lambda h, k=k, bk=bk: h.scalar_tensor_tensor(out=r[:, k, 0:T], in0=hres[:, k, 0:T], scalar=float(ALPHA), in1=self.bank[bk][:, 0:T], op0=ALU.mult, op1=ALU.add),
                 reads=[bhres, self.bbuf[bk]], writes=[br[k]])
            P.op("act", lambda h, k=k: h.copy(out=rb[:, k, 0:T], in_=r[:, k, 0:T]), reads=[br[k]], writes=[brb[k]])
            P.op("act", lambda h, k=k: h.activation(out=r2[:, k, 0:T], in_=r[:, k, 0:T], func=AF.Square), reads=[br[k]], writes=[br2[k]])
        bm, be = S["sbanks"]
        hf = self.dram["hf%d_%s" % (out_pp, g.name)].rearrange("(k p) t -> p k t", p=128)
        hb = self.dram["hb%d_%s" % (out_pp, g.name)].rearrange("(k p) t -> p k t", p=128)

        def part_b():
            for k in range(8):
                P.op("pe", lambda h, k=k: h.matmul(self.bank[bm][:, 0:T], self.onesD[:], rb[:, k, 0:T], start=(k == 0), stop=(k == 7)),
                     reads=[brb[k], self.bones], writes=[self.bbuf[bm]])
            for k in range(8):
                P.op("pe", lambda h, k=k: h.matmul(self.bank[be][:, 0:T], self.onesD[:], r2[:, k, 0:T], start=(k == 0), stop=(k == 7)),
                     reads=[br2[k], self.bones], writes=[self.bbuf[be]])
            P.op("act", lambda h: h.copy(out=mean[:, 0:T], in_=self.bank[bm][:, 0:T]), reads=[self.bbuf[bm]], writes=[bmean])
            P.op("dve", lambda h: h.tensor_tensor(out=rstd[:, 0:T], in0=mean[:, 0:T], in1=mean[:, 0:T], op=ALU.mult), reads=[bmean], writes=[brstd])
            P.op("dve", lambda h: h.scalar_tensor_tensor(out=rstd[:, 0:T], in0=self.bank[be][:, 0:T], scalar=float(LN_EPS), in1=rstd[:, 0:T], op0=ALU.add, op1=ALU.subtract), reads=[self.bbuf[be], brstd], writes=[brstd])
            P.op("act", lambda h: h.activation(out=rstd[:, 0:T], in_=rstd[:, 0:T], func=AF.Sqrt), reads=[brstd], writes=[brstd])
            P.op("dve", lambda h: h.reciprocal(out=rstd[:, 0:T], in_=rstd[:, 0:T]), reads=[brstd], writes=[brstd])

        def part_c(k):
            P.op("pool", lambda h: h.tensor_tensor(out=r[:, k, 0:T], in0=r[:, k, 0:T], in1=mean[:, 0:T], op=ALU.subtract), reads=[br[k], bmean], writes=[br[k]])
            P.op("dve", lambda h: h.tensor_tensor(out=r[:, k, 0:T], in0=r[:, k, 0:T], in1=rstd[:, 0:T], op=ALU.mult), reads=[br[k], brstd], writes=[br[k]])
            P.op("act", lambda h: h.activation(out=ob[:, k, 0:T], in_=r[:, k, 0:T], func=AF.Identity, scale=self.vcol(gname, k), bias=self.vcol(bname, k)),
                 reads=[br[k], self.bvecs], writes=[bob])
            P.op("act", lambda h: h.activation(out=r[:, k, 0:T], in_=r[:, k, 0:T], func=AF.Identity, scale=self.vcol(gname, k), bias=self.vcol(bname, k)),
                 reads=[br[k], self.bvecs], writes=[br[k]])

        def part_d():
            P.dma(lambda h: h.dma_start(out=hf[:, :, t0:t0 + T], in_=r[:, :, 0:T]), reads=[S["brall"]] + br)
            P.dma(lambda h: h.dma_start(out=hb[:, :, 1 + t0:1 + t0 + T], in_=ob[:, :, 0:T]), reads=[bob])

        pieces = [part_b] + [(lambda k=k: part_c(k)) for k in range(8)] + [part_d]
        if defer:
            return pieces
        for p_ in pieces:
            p_()
        return []

    def alloc_res_ln(self, st, pfx, obanks, sbanks):
        P = self.P
        S = {}
        S["hres"] = self.sb(st, pfx + "hres", [128, 8, 512])
        S["bhres"] = P.buf()
        S["r"] = self.sb(st, pfx + "r", [128, 8, 512])
        S["br"] = [P.buf() for _ in range(8)]
        S["brall"] = P.buf()
        S["rb"] = self.sb(st, pfx + "rb", [128, 8, 512], BF16)
        S["brb"] = [P.buf() for _ in range(8)]
        S["r2"] = self.sb(st, pfx + "r2", [128, 8, 512], BF16)
        S["br2"] = [P.buf() for _ in range(8)]
        S["ob"] = self.sb(st, pfx + "ob", [128, 8, 512], BF16)
        S["bob"] = P.buf()
        S["mean"] = self.sb(st, pfx + "mean", [128, 512])
        S["bmean"] = P.buf()
        S["rstd"] = self.sb(st, pfx + "rstd", [128, 512])
        S["brstd"] = P.buf()
        S["obanks"] = obanks
        S["sbanks"] = sbanks
        return S

    def res_ln_bufs(self, S):
        return [S["bhres"], S["brall"], S["bob"], S["bmean"], S["brstd"]] + S["br"] + S["brb"] + S["br2"]

    def phase_ffn(self, i, g):
        P = self.P
        hbin = self.dram["hb1_" + g.name].rearrange("(k p) t -> p k t", p=128)
        hfin = self.dram["hf1_" + g.name].rearrange("(k p) t -> p k t", p=128)
        w1d = self.dram["b_ffn_w1"]
        w2d = self.dram["b_ffn_w_out"]
        with contextlib.ExitStack() as st:
            w2 = self.sb(st, "f_w2", [128, NFF, D], BF16)
            bw2 = P.buf()
            P.dma(lambda h: h.dma_start(out=w2[:, 0:11, :], in_=w2d[i, 0:11 * 128, :].rearrange("(n p) d -> p n d", p=128)), writes=[bw2])
            bw2b = P.buf()
            P.dma(lambda h: h.dma_start(out=w2[:, 11:22, :], in_=w2d[i, 11 * 128:22 * 128, :].rearrange("(n p) d -> p n d", p=128)), writes=[bw2b])
            NW = 4
            w1 = [self.sb(st, "f_w1_%d" % s, [128, 2, 8, 128], BF16) for s in range(NW)]
            bw1 = [P.buf() for _ in range(NW)]
            hb = [self.sb(st, "f_hb%d" % s, [128, 8, 512], BF16) for s in range(2)]
            bhb = [P.buf() for _ in range(2)]
            act = self.sb(st, "f_act", [128, NFF, 512], BF16)
            bact = [P.buf() for _ in range(NFF)]
            tmp = [self.sb(st, "f_tmp%d" % s, [128, 512]) for s in range(2)]
            btmp = [P.buf() for _ in range(2)]
            gl = [self.sb(st, "f_gl%d" % s, [128, 512]) for s in range(2)]
            bgl = [P.buf() for _ in range(2)]
            asb = [self.sb(st, "f_asb%d" % s, [128, 512]) for s in range(2)]
            basb = [P.buf() for _ in range(2)]
            S = self.alloc_res_ln(st, "f_", obanks=[4, 5], sbanks=[4, 5])
            abanks = [2, 3, 6, 7]
            tiles = tiles_even(g.L, 510)
            wi = 0
            pend_ln = []
            for ti, (t0, T) in enumerate(tiles):
                s = ti % 2
                P.dma(lambda h, s=s, t0=t0, T=T: h.dma_start(out=hb[s][:, :, 0:T + 2], in_=hbin[:, :, t0:t0 + T + 2]), writes=[bhb[s]])
                P.dma(lambda h, t0=t0, T=T: h.dma_start(out=S["hres"][:, :, 0:T], in_=hfin[:, :, t0:t0 + T]), writes=[S["bhres"]])
                for n in range(NFF):
                    if n >= 1 and pend_ln:
                        pend_ln.pop(0)()
                    ws = wi % NW
                    wi += 1
                    P.dma(lambda h, ws=ws, n=n: h.dma_start(out=w1[ws][:], in_=w1d[i, n]), writes=[bw1[ws]])
                    bg = n % 2
                    ba = abanks[n % 4]
                    for k in range(8):
                        P.op("pe", lambda h, ws=ws, k=k, s=s, T=T, bg=bg: h.matmul(self.bank[bg][:, 0:T + 2], w1[ws][:, 0, k, :], hb[s][:, k, 0:T + 2], start=(k == 0), stop=(k == 7)),
                             reads=[bw1[ws], bhb[s]], writes=[self.bbuf[bg]])
                    for k in range(8):
                        P.op("pe", lambda h, ws=ws, k=k, s=s, T=T, ba=ba: h.matmul(self.bank[ba][:, 0:T], w1[ws][:, 1, k, :], hb[s][:, k, 1:T + 1], start=(k == 0), stop=(k == 7)),
                             reads=[bw1[ws], bhb[s]], writes=[self.bbuf[ba]])
                    q = n % 2
                    P.op("dve", lambda h, n=n, q=q, bg=bg, T=T: h.tensor_scalar(tmp[q][:, 0:T], self.bank[bg][:, 1:T + 1], self.vcol("fcw%d_1" % i, n), self.vcol("fcb%d" % i, n), ALU.mult, ALU.add),
                         reads=[self.bbuf[bg], self.bvecs], writes=[btmp[q]])
                    P.op("dve", lambda h, n=n, q=q, bg=bg, T=T: h.scalar_tensor_tensor(out=tmp[q][:, 0:T], in0=self.bank[bg][:, 0:T], scalar=self.vcol("fcw%d_0" % i, n), in1=tmp[q][:, 0:T], op0=ALU.mult, op1=ALU.add),
                         reads=[self.bbuf[bg], btmp[q]], writes=[btmp[q]])
                    P.op("dve", lambda h, n=n, q=q, bg=bg, T=T: h.scalar_tensor_tensor(out=tmp[q][:, 0:T], in0=self.bank[bg][:, 2:T + 2], scalar=self.vcol("fcw%d_2" % i, n), in1=tmp[q][:, 0:T], op0=ALU.mult, op1=ALU.add),
                         reads=[self.bbuf[bg], btmp[q]], writes=[btmp[q]])
                    P.op("act", lambda h, q=q, T=T: h.activation(out=gl[q][:, 0:T], in_=tmp[q][:, 0:T], func=AF.Gelu), reads=[btmp[q]], writes=[bgl[q]])
                    P.op("act", lambda h, q=q, ba=ba, T=T: h.copy(out=asb[q][:, 0:T], in_=self.bank[ba][:, 0:T]), reads=[self.bbuf[ba]], writes=[basb[q]])
                    P.op("pool", lambda h, n=n, q=q, T=T: h.tensor_tensor(out=act[:, n, 0:T], in0=asb[q][:, 0:T], in1=gl[q][:, 0:T], op=ALU.mult),
                         reads=[basb[q], bgl[q]], writes=[bact[n]])

                def acc(k, bk, T=T):
                    for n in range(NFF):
                        P.op("pe", lambda h, n=n, k=k, bk=bk: h.matmul(self.bank[bk][:, 0:T], w2[:, n, k * 128:(k + 1) * 128], act[:, n, 0:T], start=(n == 0), stop=(n == NFF - 1)),
                             reads=[bact[n], bw2, bw2b], writes=[self.bbuf[bk]])
                while pend_ln:
                    pend_ln.pop(0)()
                pend_ln = self.res_ln(S, i, 2, g, t0, T, 0, acc, defer=True)
            while pend_ln:
                pend_ln.pop(0)()
            P.barrier()
            P.release([bw2, bw2b] + bw1 + bhb + self.res_ln_bufs(S))

    def phase_proj(self, i, g, wname, j, prologue=None):
        P = self.P
        wd = self.dram[wname]
        xo = self.dram["xo_" + g.name].rearrange("(k p) t -> p k t", p=128)
        hfin = self.dram["hf0_" + g.name].rearrange("(k p) t -> p k t", p=128)
        with contextlib.ExitStack() as st:
            w = self.sb(st, "p_w", [128, 8, D], BF16)
            bw = P.buf()
            P.dma(lambda h: h.dma_start(out=w[:], in_=wd[j].rearrange("(k p) d -> p k d", p=128)), writes=[bw])
            xin = [self.sb(st, "p_x%d" % s, [128, 8, 512], BF16) for s in range(2)]
            bxin = [P.buf() for _ in range(2)]
            S = self.alloc_res_ln(st, "p_", obanks=[0, 1, 2, 3], sbanks=[6, 7])
            extra = prologue("alloc", st) if prologue else None
            tiles = tiles_even(g.L, 512) if not prologue else [(t0, min(512, g.L - t0)) for t0 in range(0, g.L, 512)]
            for ti, (t0, T) in enumerate(tiles):
                s = ti % 2
                if prologue:
                    prologue("tile", (ti, t0, T, xin[s], bxin[s], extra))
                else:
                    P.dma(lambda h, s=s, t0=t0, T=T: h.dma_start(out=xin[s][:, :, 0:T], in_=xo[:, :, t0:t0 + T]), writes=[bxin[s]])
                P.dma(lambda h, t0=t0, T=T: h.dma_start(out=S["hres"][:, :, 0:T], in_=hfin[:, :, t0:t0 + T]), writes=[S["bhres"]])

                def acc(k, bk, T=T, s=s):
                    for kk in range(8):
                        P.op("pe", lambda h, kk=kk, k=k, bk=bk: h.matmul(self.bank[bk][:, 0:T], w[:, kk, k * 128:(k + 1) * 128], xin[s][:, kk, 0:T], start=(kk == 0), stop=(kk == 7)),
                             reads=[bxin[s], bw], writes=[self.bbuf[bk]])
                self.res_ln(S, i, 1, g, t0, T, 1, acc)
            P.barrier()
            rel = [bw] + bxin + self.res_ln_bufs(S)
            if prologue:
                rel += prologue("bufs", extra)
            P.release(rel)

    def layer_attn(self, i, g):
        P = self.P
        j = i // 2
        L = g.L
        hbin = self.dram["hb0_" + g.name].rearrange("(k p) t -> p k t", p=128)
        wq = self.dram["b_at_w_qkv"]
        qT = self.dram["qT_" + g.name].rearrange("(k p) t -> p k t", p=128)
        xo = self.dram["xo_" + g.name].rearrange("(k p) t -> p k t", p=128)
        cosd = self.dram["cos_" + g.name]
        sind = self.dram["sin_" + g.name]
        tiles = [(t0, min(512, L - t0)) for t0 in range(0, L, 512)]
        nkc = (L + 127) // 128
        with contextlib.ExitStack() as st:
            kT = self.sb(st, "a_kT", [128, NKV, L], BF16)
            bkT = [[P.buf() for _ in tiles] for _ in range(NKV)]
            vv = self.sb(st, "a_v", [128, nkc, NKV * HD], BF16)
            bvv = [P.buf() for _ in range(nkc)]
            with contextlib.ExitStack() as s1:
                w = self.sb(s1, "a_w", [128, 8, 1536], BF16)
                bw = P.buf()
                P.dma(lambda h: h.dma_start(out=w[:], in_=wq[j].rearrange("(k p) d -> p k d", p=128)), writes=[bw])
                hb = [self.sb(s1, "a_hb%d" % s, [128, 8, 512], BF16) for s in range(2)]
                bhb = [P.buf() for _ in range(2)]
                cs = [self.sb(s1, "a_cos%d" % s, [128, 512]) for s in range(2)]
                sn = [self.sb(s1, "a_sin%d" % s, [128, 512]) for s in range(2)]
                bcs = [P.buf() for _ in range(2)]
                bsn = [P.buf() for _ in range(2)]
                sq = [self.sb(s1, "a_sq%d" % s, [128, 512], BF16) for s in range(2)]
                bsq = [P.buf() for _ in range(2)]
                rs = [self.sb(s1, "a_rs%d" % s, [128, 512]) for s in range(2)]
                brs = [P.buf() for _ in range(2)]
                aa = [self.sb(s1, "a_aa%d" % s, [128, 512]) for s in range(2)]
                baa = [P.buf() for _ in range(2)]
                ar = [self.sb(s1, "a_ar%d" % s, [128, 512]) for s in range(2)]
                bar = [P.buf() for _ in range(2)]
                qo = [self.sb(s1, "a_qo%d" % s, [128, 8, 512], BF16) for s in range(2)]
                bqo = [P.buf() for _ in range(2)]
                vst = [self.sb(s1, "a_vst%d" % s, [128, 256]) for s in range(2)]
                hc = 0
                for ti, (t0, T) in enumerate(tiles):
                    s = ti % 2
                    P.dma(lambda h, s=s, t0=t0, T=T: h.dma_start(out=hb[s][:, :, 0:T], in_=hbin[:, :, 1 + t0:1 + t0 + T]), writes=[bhb[s]])
                    P.dma(lambda h, s=s, t0=t0, T=T: h.dma_start(out=cs[s][:, 0:T], in_=cosd[:, t0:t0 + T]), writes=[bcs[s]])
                    P.dma(lambda h, s=s, t0=t0, T=T: h.dma_start(out=sn[s][:, 0:T], in_=sind[:, t0:t0 + T]), writes=[bsn[s]])
                    for hh in range(NH + NKV):
                        q = hc % 2
                        hc += 1
                        bq = (hc % 2) * 2
                        bs_ = bq + 1
                        gn = ("qg%d" % j) if hh < NH else ("kg%d" % j)
                        for k in range(8):
                            P.op("pe", lambda h, k=k, hh=hh, s=s, T=T, bq=bq: h.matmul(self.bank[bq][:, 0:T], w[:, k, hh * 128:(hh + 1) * 128], hb[s][:, k, 0:T], start=(k == 0), stop=(k == 7)),
                                 reads=[bw, bhb[s]], writes=[self.bbuf[bq]])
                        P.op("dve", lambda h, q=q, bq=bq, T=T: h.tensor_copy(out=aa[q][:, 0:T], in_=self.bank[bq][:, 0:T]), reads=[self.bbuf[bq]], writes=[baa[q]])
                        P.op("act", lambda h, q=q, T=T: h.activation(out=sq[q][:, 0:T], in_=aa[q][:, 0:T], func=AF.Square), reads=[baa[q]], writes=[bsq[q]])
                        P.op("pe", lambda h, q=q, bs_=bs_, T=T: h.matmul(self.bank[bs_][:, 0:T], self.onesH[:], sq[q][:, 0:T], start=True, stop=True),
                             reads=[bsq[q], self.bones], writes=[self.bbuf[bs_]])
                        P.op("dve", lambda h, q=q, bs_=bs_, T=T: h.tensor_scalar(rs[q][:, 0:T], self.bank[bs_][:, 0:T], float(RMS_EPS), None, ALU.add),
                             reads=[self.bbuf[bs_]], writes=[brs[q]])
                        P.op("act", lambda h, q=q, T=T: h.activation(out=rs[q][:, 0:T], in_=rs[q][:, 0:T], func=AF.Sqrt), reads=[brs[q]], writes=[brs[q]])
                        P.op("dve", lambda h, q=q, T=T: h.reciprocal(out=rs[q][:, 0:T], in_=rs[q][:, 0:T]), reads=[brs[q]], writes=[brs[q]])
                        P.op("dve", lambda h, q=q, bq=bq, T=T, gn=gn: h.scalar_tensor_tensor(out=aa[q][:, 0:T], in0=aa[q][:, 0:T], scalar=self.vcol(gn), in1=rs[q][:, 0:T], op0=ALU.mult, op1=ALU.mult),
                             reads=[baa[q], bsq[q], brs[q], self.bvecs], writes=[baa[q]])
                        P.op("pool", lambda h, q=q, T=T: h.tensor_copy(out=ar[q][0:64, 0:T], in_=aa[q][64:128, 0:T]), reads=[baa[q]], writes=[bar[q]])
                        P.op("pool", lambda h, q=q, T=T: h.tensor_copy(out=ar[q][64:128, 0:T], in_=aa[q][0:64, 0:T]), reads=[baa[q]], writes=[bar[q]])
                        P.op("pool", lambda h, q=q, s=s, T=T: h.tensor_tensor(out=ar[q][:, 0:T], in0=ar[q][:, 0:T], in1=sn[s][:, 0:T], op=ALU.mult), reads=[bar[q], bsn[s]], writes=[bar[q]])
                        P.op("dve", lambda h, q=q, s=s, T=T: h.tensor_tensor(out=aa[q][:, 0:T], in0=aa[q][:, 0:T], in1=cs[s][:, 0:T], op=ALU.mult), reads=[baa[q], bcs[s]], writes=[baa[q]])
                        if hh < NH:
                            P.op("dve", lambda h, q=q, s=s, hh=hh, T=T: h.tensor_tensor(out=qo[s][:, hh, 0:T], in0=aa[q][:, 0:T], in1=ar[q][:, 0:T], op=ALU.add),
                                 reads=[baa[q], bar[q]], writes=[bqo[s]])
                        else:
                            P.op("dve", lambda h, q=q, hh=hh, t0=t0, T=T: h.tensor_tensor(out=kT[:, hh - NH, t0:t0 + T], in0=aa[q][:, 0:T], in1=ar[q][:, 0:T], op=ALU.add),
                                 reads=[baa[q], bar[q]], writes=[bkT[hh - NH][ti]])
                    P.dma(lambda h, s=s, t0=t0, T=T: h.dma_start(out=qT[:, :, t0:t0 + T], in_=qo[s][:, :, 0:T]), reads=[bqo[s]])
                    for c in range((T + 127) // 128):
                        tp = min(128, T - c * 128)
                        kc = t0 // 128 + c
                        bv = 4 + (kc % 2)
                        for k in range(8):
                            P.op("pe", lambda h, k=k, s=s, c=c, tp=tp, bv=bv: h.matmul(self.bank[bv][0:tp, 0:256], hb[s][:, k, c * 128:c * 128 + tp], w[:, k, 1280:1536], start=(k == 0), stop=(k == 7)),
                                 reads=[bw, bhb[s]], writes=[self.bbuf[bv]])
                        P.op("act", lambda h, kc=kc, tp=tp, bv=bv: h.copy(out=vv[0:tp, kc, :], in_=self.bank[bv][0:tp, 0:256]), reads=[self.bbuf[bv]], writes=[bvv[kc]])
                P.barrier()
                P.release([bw] + bhb + bcs + bsn + bqo)
            with contextlib.ExitStack() as s2:
                NQ = 3
                qt = [self.sb(s2, "a_qt%d" % s, [128, 512], BF16) for s in range(NQ)]
                bqt = [P.buf() for _ in range(NQ)]
                NP = 4
                pt = [self.sb(s2, "a_pt%d" % s, [128, 512], BF16) for s in range(NP)]
                bpt = [P.buf() for _ in range(NP)]
                rd = [self.sb(s2, "a_rd%d" % s, [128, 512]) for s in range(2)]
                brd = [P.buf() for _ in range(2)]
                ot = [self.sb(s2, "a_ot%d" % s, [128, 512], BF16) for s in range(2)]
                bot = [P.buf() for _ in range(2)]
                scale = float(HD ** -0.5)
                it = 0
                pc = 0
                allk = [b for row in bkT for b in row]
                for ti, (t0, T) in enumerate(tiles):
                    for hh in range(NH):
                        kv = hh // 4
                        qs = it % NQ
                        o2 = it % 2
                        it += 1
                        bo = 4 + o2 * 2
                        bd = bo + 1
                        P.dma(lambda h, qs=qs, hh=hh, t0=t0, T=T: h.dma_start(out=qt[qs][:, 0:T], in_=qT[:, hh, t0:t0 + T]), writes=[bqt[qs]])
                        LOOK = 2
                        slots = {}
                        for kq in range(nkc + LOOK):
                            if kq < nkc:
                                kc = kq
                                kp = min(128, L - kc * 128)
                                sbk = pc % 4
                                ps = pc % NP
                                pc += 1
                                slots[kc] = ps
                                P.op("pe", lambda h, kc=kc, kp=kp, kv=kv, qs=qs, sbk=sbk, T=T: h.matmul(self.bank[sbk][0:kp, 0:T], kT[:, kv, kc * 128:kc * 128 + kp], qt[qs][:, 0:T], start=True, stop=True),
                                     reads=[bqt[qs], bkT[kv][kc // 4]], writes=[self.bbuf[sbk]])
                                P.op("act", lambda h, kp=kp, sbk=sbk, ps=ps, T=T: h.activation(out=pt[ps][0:kp, 0:T], in_=self.bank[sbk][0:kp, 0:T], func=AF.Exp, scale=scale),
                                     reads=[self.bbuf[sbk]], writes=[bpt[ps]])
                            if kq >= LOOK:
                                kc = kq - LOOK
                                kp = min(128, L - kc * 128)
                                ps = slots[kc]
                                P.op("pe", lambda h, kc=kc, kp=kp, kv=kv, ps=ps, bo=bo, T=T: h.matmul(self.bank[bo][:, 0:T], vv[0:kp, kc, kv * HD:(kv + 1) * HD], pt[ps][0:kp, 0:T], start=(kc == 0), stop=(kc == nkc - 1)),
                                     reads=[bpt[ps], bvv[kc]], writes=[self.bbuf[bo]])
                                P.op("pe", lambda h, kc=kc, kp=kp, ps=ps, bd=bd, T=T: h.matmul(self.bank[bd][:, 0:T], self.ones1[0:kp, :], pt[ps][0:kp, 0:T], start=(kc == 0), stop=(kc == nkc - 1)),
                                     reads=[bpt[ps], self.bones], writes=[self.bbuf[bd]])
                        P.op("dve", lambda h, o2=o2, bd=bd, T=T: h.reciprocal(out=rd[o2][:, 0:T], in_=self.bank[bd][:, 0:T]), reads=[self.bbuf[bd]], writes=[brd[o2]])
                        P.op("dve", lambda h, o2=o2, bo=bo, T=T: h.tensor_tensor(out=ot[o2][:, 0:T], in0=self.bank[bo][:, 0:T], in1=rd[o2][:, 0:T], op=ALU.mult),
                             reads=[self.bbuf[bo], brd[o2]], writes=[bot[o2]])
                        P.dma(lambda h, o2=o2, hh=hh, t0=t0, T=T: h.dma_start(out=xo[:, hh, t0:t0 + T], in_=ot[o2][:, 0:T]), reads=[bot[o2]])
                P.barrier()
                P.release(bqt + bot)
        self.phase_proj(i, g, "b_at_w_out", j)

    def load_const(self, st, name, shape, dt=F32):
        P = self.P
        t = self.sb(st, "c_" + name, shape)
        b = P.buf()
        P.dma(lambda h: h.dma_start(out=t[:], in_=self.dram[name][:, :]), writes=[b])
        if dt == F32:
            return t, b
        self.const_rel = getattr(self, "const_rel", []) + [b]
        t2 = self.sb(st, "cb_" + name, shape, dt)
        b2 = P.buf()
        P.op("dve", lambda h: h.tensor_copy(out=t2[:], in_=t[:]), reads=[b], writes=[b2])
        return t2, b2

    def fft_pass(self, g, st, C, src, nrows, mode, extra):
        P = self.P
        M, Hh, nk, nin = g.M, g.Hh, g.nk, g.nin
        CG = g.CG
        kf = self.dram["kf_" + g.name]
        yc = self.dram["yc_" + g.name]
        W2 = 2 * nk
        NS = 4
        G = D // CG
        xt = [self.sb(st, "ff_x%d" % s, [M, CG, M], FDT) for s in range(NS)]
        bxc = [P.buf() for _ in range(NS)]
        x32 = [self.sb(st, "ff_x32_%d" % s, [M, CG, M]) for s in range(NS)]
        bxt = [P.buf() for _ in range(NS)]
        bxt2 = [P.buf() for _ in range(NS)]
        B = [self.sb(st, "ff_B%d" % s, [M, CG, W2], FDT) for s in range(NS)]
        bB = [P.buf() for _ in range(NS)]
        t1 = [self.sb(st, "ff_t1%d" % s, [M, CG, W2], FDT) for s in range(NS)]
        bt1 = [P.buf() for _ in range(NS)]
        if mode == "data":
            for s in range(NS):
                P.op("pool", lambda h, s=s: h.memset(x32[s][:], 0.0), writes=[bxt[s], bxt2[s]])
            kt = [self.sb(st, "ff_k%d" % s, [M, CG, W2]) for s in range(NS)]
            bkt = [P.buf() for _ in range(NS)]
            Y = [self.sb(st, "ff_Y%d" % s, [M, CG, W2], FDT) for s in range(NS)]
            bY = [P.buf() for _ in range(NS)]
            Dd = [self.sb(st, "ff_D%d" % s, [nk, CG, 2 * M], FDT) for s in range(NS)]
            bD = [P.buf() for _ in range(NS)]
            t2 = [self.sb(st, "ff_t2%d" % s, [nk, CG, 2 * M], FDT) for s in range(NS)]
            bt2 = [P.buf() for _ in range(NS)]
            yo = [self.sb(st, "ff_yo%d" % s, [nin, CG, M]) for s in range(NS)]
            byo = [P.buf() for _ in range(NS)]
        else:
            sc = extra["sc"]
            bsc = extra["bsc"]
            ko = [self.sb(st, "ff_ko%d" % s, [M, CG, W2]) for s in range(NS)]
            bko = [P.buf() for _ in range(NS)]

        def bc(tab, p, lo, hi):
            a = tab[0:p, lo:hi]
            return a.unsqueeze(1).to_broadcast([p, CG, hi - lo])

        cpb = 512 // (2 * M)
        ib = [4, 5][:CG // cpb]
        assert CG // cpb <= 2 and CG * 2 * Hh <= 512 and CG * M <= 512

        def stage_a(gi):
            s = gi % NS
            c0 = gi * CG
            ba = gi % 2
            if mode == "data":
                P.dma(lambda h: h.dma_start(out=x32[s][0:nin - 1, :, :], in_=src[c0:c0 + CG, 0:(nin - 1) * M].rearrange("c (a b) -> a c b", b=M)), writes=[bxt[s]])
                P.dma(lambda h: h.dma_start(out=x32[s][nin - 1:nin, :, 0:g.tail], in_=src[c0:c0 + CG, (nin - 1) * M:g.L].unsqueeze(0)), writes=[bxt2[s]])
                P.dma(lambda h: h.dma_start(out=kt[s][:], in_=kf[gi].rearrange("m (c w) -> m c w", c=CG)), writes=[bkt[s]])
                P.op("act", lambda h: h.copy(out=xt[s][0:nrows, :, :], in_=x32[s][0:nrows, :, :]), reads=[bxt[s], bxt2[s]], writes=[bxc[s]])
                rdeps = [bxc[s]]
            else:
                P.dma(lambda h: h.dma_start(out=x32[s][:, :, :], in_=src[c0:c0 + CG, :].rearrange("c (a b) -> a c b", b=M)), writes=[bxt[s]])
                P.op("pool", lambda h: h.tensor_tensor(out=xt[s][:], in0=x32[s][:], in1=sc[0:M, c0:c0 + CG].unsqueeze(2).to_broadcast([M, CG, M]), op=ALU.mult),
                     reads=[bxt[s], bsc], writes=[bxc[s]])
                rdeps = [bxc[s]]
            for c in range(CG):
                P.op("pe", lambda h, c=c: h.matmul(self.bank[ba][0:M, c * 2 * Hh:(c + 1) * 2 * Hh], xt[s][0:nrows, c, :], C["f1"][0][0:nrows, :], start=True, stop=True),
                     reads=rdeps + [C["f1"][1]], writes=[self.bbuf[ba]])
            A = self.bank[ba][0:M, 0:CG * 2 * Hh].rearrange("p (c w) -> p c w", c=CG)
            Are = A[:, :, 0:nk]
            Aim = A[:, :, nk:2 * Hh]
            P.op("dve", lambda h: h.tensor_tensor(out=B[s][:, :, 0:nk], in0=Are, in1=bc(C["twc"][0], M, 0, nk), op=ALU.mult),
                 reads=[self.bbuf[ba], C["twc"][1]], writes=[bB[s]])
            P.op("dve", lambda h: h.tensor_tensor(out=B[s][:, :, nk:W2], in0=Are, in1=bc(C["ntws"][0], M, 0, nk), op=ALU.mult),
                 reads=[self.bbuf[ba], C["ntws"][1]], writes=[bB[s]])
            P.op("dve", lambda h: h.tensor_tensor(out=t1[s][:, :, 1:Hh], in0=Aim, in1=bc(C["tws"][0], M, 1, Hh), op=ALU.mult),
                 reads=[self.bbuf[ba], C["tws"][1]], writes=[bt1[s]])
            P.op("dve", lambda h: h.tensor_tensor(out=t1[s][:, :, nk + 1:nk + Hh], in0=Aim, in1=bc(C["twc"][0], M, 1, Hh), op=ALU.mult),
                 reads=[self.bbuf[ba], C["twc"][1]], writes=[bt1[s]])
            P.op("pool", lambda h: h.tensor_tensor(out=B[s][:, :, 1:Hh], in0=B[s][:, :, 1:Hh], in1=t1[s][:, :, 1:Hh], op=ALU.add), reads=[bB[s], bt1[s]], writes=[bB[s]])
            P.op("pool", lambda h: h.tensor_tensor(out=B[s][:, :, nk + 1:nk + Hh], in0=B[s][:, :, nk + 1:nk + Hh], in1=t1[s][:, :, nk + 1:nk + Hh], op=ALU.add), reads=[bB[s], bt1[s]], writes=[bB[s]])

        def stage_b(gi):
            s = gi % NS
            bre, bim = 2, 3
            Bre = B[s][:, :, 0:nk]
            Bim = B[s][:, :, nk:W2]
            Xre = self.bank[bre][0:M, 0:CG * nk].rearrange("p (c w) -> p c w", c=CG)
            Xim = self.bank[bim][0:M, 0:CG * nk].rearrange("p (c w) -> p c w", c=CG)
            P.op("pe", lambda h: h.matmul(Xre, C["c2"][0][:], Bre, start=True, stop=False), reads=[bB[s], C["c2"][1]], writes=[self.bbuf[bre]])
            P.op("pe", lambda h: h.matmul(Xre, C["s2"][0][:], Bim, start=False, stop=True), reads=[bB[s], C["s2"][1]], writes=[self.bbuf[bre]])
            P.op("pe", lambda h: h.matmul(Xim, C["c2"][0][:], Bim, start=True, stop=False), reads=[bB[s], C["c2"][1]], writes=[self.bbuf[bim]])
            P.op("pe", lambda h: h.matmul(Xim, C["ns2"][0][:], Bre, start=False, stop=True), reads=[bB[s], C["ns2"][1]], writes=[self.bbuf[bim]])
            if mode == "filter":
                P.op("act", lambda h: h.copy(out=ko[s][:, :, 0:nk], in_=Xre), reads=[self.bbuf[bre]], writes=[bko[s]])
                P.op("dve", lambda h: h.tensor_copy(out=ko[s][:, :, nk:W2], in_=Xim), reads=[self.bbuf[bim]], writes=[bko[s]])
                P.dma(lambda h: h.dma_start(out=kf[gi].rearrange("m (c w) -> m c w", c=CG), in_=ko[s][:]), reads=[bko[s]])
                return
            Kre = kt[s][:, :, 0:nk]
            Kim = kt[s][:, :, nk:W2]
            P.op("dve", lambda h: h.tensor_tensor(out=Y[s][:, :, 0:nk], in0=Xre, in1=Kre, op=ALU.mult), reads=[self.bbuf[bre], bkt[s]], writes=[bY[s]])
            P.op("dve", lambda h: h.tensor_tensor(out=Y[s][:, :, nk:W2], in0=Xre, in1=Kim, op=ALU.mult), reads=[self.bbuf[bre], bkt[s]], writes=[bY[s]])
            P.op("dve", lambda h: h.tensor_tensor(out=t1[s][:, :, 0:nk], in0=Xim, in1=Kim, op=ALU.mult), reads=[self.bbuf[bim], bkt[s], bB[s]], writes=[bt1[s]])
            P.op("dve", lambda h: h.tensor_tensor(out=t1[s][:, :, nk:W2], in0=Xim, in1=Kre, op=ALU.mult), reads=[self.bbuf[bim], bkt[s]], writes=[bt1[s]])
            P.op("pool", lambda h: h.tensor_tensor(out=Y[s][:, :, 0:nk], in0=Y[s][:, :, 0:nk], in1=t1[s][:, :, 0:nk], op=ALU.subtract), reads=[bY[s], bt1[s]], writes=[bY[s]])
            P.op("pool", lambda h: h.tensor_tensor(out=Y[s][:, :, nk:W2], in0=Y[s][:, :, nk:W2], in1=t1[s][:, :, nk:W2], op=ALU.add), reads=[bY[s], bt1[s]], writes=[bY[s]])

        def stage_c(gi):
            s = gi % NS
            for c in range(CG):
                bi = ib[c // cpb]
                o = self.bank[bi][0:nk, (c % cpb) * 2 * M:(c % cpb + 1) * 2 * M]
                P.op("pe", lambda h, c=c, o=o: h.matmul(o, Y[s][:, c, 0:nk], C["if2a"][0][:], start=True, stop=False), reads=[bY[s], C["if2a"][1]], writes=[self.bbuf[bi]])
                P.op("pe", lambda h, c=c, o=o: h.matmul(o, Y[s][:, c, nk:W2], C["if2b"][0][:], start=False, stop=True), reads=[bY[s], C["if2b"][1]], writes=[self.bbuf[bi]])
            for bi_i, bi in enumerate(ib):
                cs_ = slice(bi_i * cpb, (bi_i + 1) * cpb)
                Cc = self.bank[bi][0:nk, 0:cpb * 2 * M].rearrange("p (c w) -> p c w", c=cpb)
                Cre = Cc[:, :, 0:M]
                Cim = Cc[:, :, M:2 * M]

                def bc2(tab):
                    return tab[0:nk, 0:M].unsqueeze(1).to_broadcast([nk, cpb, M])
                P.op("dve", lambda h: h.tensor_tensor(out=Dd[s][:, cs_, 0:M], in0=Cre, in1=bc2(C["twct"][0]), op=ALU.mult), reads=[self.bbuf[bi], C["twct"][1]], writes=[bD[s]])
                P.op("dve", lambda h: h.tensor_tensor(out=Dd[s][:, cs_, M:2 * M], in0=Cre, in1=bc2(C["twst"][0]), op=ALU.mult), reads=[self.bbuf[bi], C["twst"][1]], writes=[bD[s]])
                P.op("dve", lambda h: h.tensor_tensor(out=t2[s][:, cs_, 0:M], in0=Cim, in1=bc2(C["ntwst"][0]), op=ALU.mult), reads=[self.bbuf[bi], C["ntwst"][1]], writes=[bt2[s]])
                P.op("dve", lambda h: h.tensor_tensor(out=t2[s][:, cs_, M:2 * M], in0=Cim, in1=bc2(C["twct"][0]), op=ALU.mult), reads=[self.bbuf[bi], C["twct"][1]], writes=[bt2[s]])
            P.op("pool", lambda h: h.tensor_tensor(out=Dd[s][:], in0=Dd[s][:], in1=t2[s][:], op=ALU.add), reads=[bD[s], bt2[s]], writes=[bD[s]])

        def stage_d(gi):
            s = gi % NS
            c0 = gi * CG
            yb = 6 + gi % 2
            yv = self.bank[yb][0:nin, 0:CG * M].rearrange("p (c w) -> p c w", c=CG)
            P.op("pe", lambda h: h.matmul(yv, C["gc"][0][:], Dd[s][:, :, 0:M], start=True, stop=False), reads=[bD[s], C["gc"][1]], writes=[self.bbuf[yb]])
            P.op("pe", lambda h: h.matmul(yv, C["gs"][0][:], Dd[s][:, :, M:2 * M], start=False, stop=True), reads=[bD[s], C["gs"][1]], writes=[self.bbuf[yb]])
            P.op("act", lambda h: h.copy(out=yo[s][:], in_=yv), reads=[self.bbuf[yb]], writes=[byo[s]])
            P.dma(lambda h: h.dma_start(out=yc[c0:c0 + CG, :].rearrange("c (a b) -> a c b", b=M), in_=yo[s][:]), reads=[byo[s]])

        nst = 4 if mode == "data" else 2
        P.store_q = "act"
        for t in range(G + nst - 1):
            if mode == "data":
                if 0 <= t - 3 < G:
                    stage_d(t - 3)
                if 0 <= t - 2 < G:
                    stage_c(t - 2)
            if 0 <= t - 1 < G:
                stage_b(t - 1)
            if t < G:
                stage_a(t)
        P.store_q = "pool"
        rel = bxt + bxt2
        if mode == "data":
            rel += bkt + byo
        else:
            rel += bko
        return rel

    def layer_hyena(self, i, g):
        P = self.P
        j = i // 2
        L, M, N, H = g.L, g.M, g.N, g.H
        hbin = self.dram["hb0_" + g.name].rearrange("(k p) t -> p k t", p=128)
        x0d = self.dram["x0_" + g.name].rearrange("(k p) t -> p k t", p=128)
        vgd = self.dram["vg_" + g.name].rearrange("(k p) t -> p k t", p=128)
        ycd = self.dram["yc_" + g.name].rearrange("(k p) t -> p k t", p=128)
        kund = self.dram["kun_" + g.name].rearrange("(k p) t -> p k t", p=128)
        rnd = self.dram["rn_" + g.name]
        zt = self.dram["zt_" + g.name]
        tb = self.dram["tb_" + g.name]
        with contextlib.ExitStack() as sl:
            coef = self.sb(sl, "h_coef", [128, 8, 2, 32])
            bcoef = P.buf()
            with contextlib.ExitStack() as st:
                w1 = self.sb(st, "hf_w1", [33, 64])
                w2 = self.sb(st, "hf_w2", [64, 64])
                w3 = self.sb(st, "hf_w3", [64, 2 * D])
                bw = P.buf()
                bw_2 = P.buf()
                bw_3 = P.buf()
                P.dma(lambda h: h.dma_start(out=w1[:], in_=self.dram["hy_filt_w1"][j]), writes=[bw])
                P.dma(lambda h: h.dma_start(out=w2[:], in_=self.dram["hy_filt_w2"][j]), writes=[bw_2])
                P.dma(lambda h: h.dma_start(out=w3[:], in_=self.dram["hy_filt_w3"][j]), writes=[bw_3])
                fbs = self.sb(st, "hf_fbs", [64, 2])
                bfbs = P.buf()
                for q, nm in enumerate(("fb1_%d" % j, "fb2_%d" % j)):
                    P.op("dve", lambda h, q=q, nm=nm: h.tensor_scalar(fbs[:, q:q + 1], self.vcol(nm, 0, 64), self.vcol("ffr_%d" % j, 0, 64), 0.0, ALU.mult, ALU.add),
                         reads=[self.bvecs], writes=[bfbs])
                asum = self.sb(st, "hf_asum", [128, 2, 8, 40])
                basum = P.buf()
                P.op("dve", lambda h: h.memset(asum[:], 0.0), writes=[basum])
                ev = self.sb(st, "hf_ev", [128, 8, 2, 64])
                bev = P.buf()
                zs = [self.sb(st, "hf_z%d" % s, [33, 512]) for s in range(2)]
                bzs = [P.buf() for _ in range(2)]
                ts = [self.sb(st, "hf_t%d" % s, [128, 512]) for s in range(2)]
                bts = [P.buf() for _ in range(2)]
                a1 = [self.sb(st, "hf_a1%d" % s, [64, 512]) for s in range(2)]
                ba1 = [P.buf() for _ in range(2)]
                wr = [self.sb(st, "hf_wr%d" % s, [64, 512]) for s in range(2)]
                bwr = [P.buf() for _ in range(2)]
                a2 = [self.sb(st, "hf_a2%d" % s, [64, 512]) for s in range(2)]
                ba2 = [P.buf() for _ in range(2)]
                win = [self.sb(st, "hf_win%d" % s, [128, 8, 512]) for s in range(2)]
                bwin = [P.buf() for _ in range(2)]
                ko = [self.sb(st, "hf_ko%d" % s, [128, 8, 512]) for s in range(2)]
                bko = [P.buf() for _ in range(2)]
                ftmp = [self.sb(st, "hf_ftmp%d" % s, [128, 512]) for s in range(2)]
                bftmp = [P.buf() for _ in range(2)]
                jobs = [(c0, 512, 0 if c0 < H else 1) for c0 in range(0, N, 512)] + [(N, 64, 2)]
                for ji, (c0, T, kind) in enumerate(jobs):
                    s = ji % 2
                    P.dma(lambda h, s=s, c0=c0, T=T: h.dma_start(out=zs[s][:, 0:T], in_=zt[:, c0:c0 + T]), writes=[bzs[s]])
                    P.dma(lambda h, s=s, c0=c0, T=T: h.dma_start(out=ts[s][:, 0:T], in_=tb[:, c0:c0 + T]), writes=[bts[s]])
                    bq = ji % 2
                    P.op("pe", lambda h, s=s, T=T, bq=bq: h.matmul(self.bank[bq][0:64, 0:T], w1[:], zs[s][:, 0:T], start=True, stop=True), reads=[bw, bzs[s]], writes=[self.bbuf[bq]])
                    P.op("dve", lambda h, s=s, T=T, bq=bq: h.tensor_scalar(a1[s][:, 0:T], self.bank[bq][0:64, 0:T], self.vcol("ffr_%d" % j, 0, 64), fbs[:, 0:1], ALU.mult, ALU.add),
                         reads=[self.bbuf[bq], bfbs, self.bvecs], writes=[ba1[s]])
                    for _w in range(2):
                        P.op("dve", lambda h, s=s, T=T: h.tensor_scalar(wr[s][:, 0:T], a1[s][:, 0:T], float(PI), float(-2 * PI), ALU.is_gt, ALU.mult), reads=[ba1[s]], writes=[bwr[s]])
                        P.op("dve", lambda h, s=s, T=T: h.tensor_tensor(out=a1[s][:, 0:T], in0=a1[s][:, 0:T], in1=wr[s][:, 0:T], op=ALU.add), reads=[ba1[s], bwr[s]], writes=[ba1[s]])
                        P.op("dve", lambda h, s=s, T=T: h.tensor_scalar(wr[s][:, 0:T], a1[s][:, 0:T], float(-PI), float(2 * PI), ALU.is_lt, ALU.mult), reads=[ba1[s]], writes=[bwr[s]])
                        P.op("dve", lambda h, s=s, T=T: h.tensor_tensor(out=a1[s][:, 0:T], in0=a1[s][:, 0:T], in1=wr[s][:, 0:T], op=ALU.add), reads=[ba1[s], bwr[s]], writes=[ba1[s]])
                    P.op("act", lambda h, s=s, T=T: h.activation(out=a1[s][:, 0:T], in_=a1[s][:, 0:T], func=AF.Sin), reads=[ba1[s]], writes=[ba1[s]])
                    P.op("pe", lambda h, s=s, T=T, bq=bq: h.matmul(self.bank[bq][0:64, 0:T], w2[:], a1[s][:, 0:T], start=True, stop=True), reads=[bw_2, ba1[s]], writes=[self.bbuf[bq]])
                    P.op("dve", lambda h, s=s, T=T, bq=bq: h.tensor_scalar(a2[s][:, 0:T], self.bank[bq][0:64, 0:T], self.vcol("ffr_%d" % j, 0, 64), fbs[:, 1:2], ALU.mult, ALU.add),
                         reads=[self.bbuf[bq], bfbs, self.bvecs], writes=[ba2[s]])
                    for _w in range(2):
                        P.op("dve", lambda h, s=s, T=T: h.tensor_scalar(wr[s][:, 0:T], a2[s][:, 0:T], float(PI), float(-2 * PI), ALU.is_gt, ALU.mult), reads=[ba2[s]], writes=[bwr[s]])
                        P.op("dve", lambda h, s=s, T=T: h.tensor_tensor(out=a2[s][:, 0:T], in0=a2[s][:, 0:T], in1=wr[s][:, 0:T], op=ALU.add), reads=[ba2[s], bwr[s]], writes=[ba2[s]])
                        P.op("dve", lambda h, s=s, T=T: h.tensor_scalar(wr[s][:, 0:T], a2[s][:, 0:T], float(-PI), float(2 * PI), ALU.is_lt, ALU.mult), reads=[ba2[s]], writes=[bwr[s]])
                        P.op("dve", lambda h, s=s, T=T: h.tensor_tensor(out=a2[s][:, 0:T], in0=a2[s][:, 0:T], in1=wr[s][:, 0:T], op=ALU.add), reads=[ba2[s], bwr[s]], writes=[ba2[s]])
                    P.op("act", lambda h, s=s, T=T: h.activation(out=a2[s][:, 0:T], in_=a2[s][:, 0:T], func=AF.Sin), reads=[ba2[s]], writes=[ba2[s]])
                    for k in range(8):
                        P.op("act", lambda h, s=s, k=k, T=T: h.activation(out=win[s][:, k, 0:T], in_=ts[s][:, 0:T], func=AF.Exp, scale=self.vcol("ndelta", k)), reads=[bts[s], self.bvecs], writes=[bwin[s]])
                    fbl = [0] if kind == 0 else ([1] if kind == 1 else [0, 1])
                    for fb in fbl:
                        for k in range(8):
                            bk = 2 + (k % 4)
                            ch = fb * 8 + k
                            P.op("pe", lambda h, s=s, ch=ch, bk=bk, T=T: h.matmul(self.bank[bk][:, 0:T], w3[:, ch * 128:(ch + 1) * 128], a2[s][:, 0:T], start=True, stop=True),
                                 reads=[bw_3, ba2[s]], writes=[self.bbuf[bk]])
                            if kind == 2:
                                dst = ev[:, k, fb, 0:T]
                                bdst = bev
                            else:
                                dst = ko[s][:, k, 0:T]
                                bdst = bko[s]
                            tq = (k + fb) % 2
                            P.op("dve", lambda h, ch=ch, bk=bk, T=T, tq=tq: h.tensor_scalar(ftmp[tq][:, 0:T], self.bank[bk][:, 0:T], self.vcol("fb3_%d" % j, ch), None, ALU.add),
                                 reads=[self.bbuf[bk], self.bvecs], writes=[bftmp[tq]])
                            P.op("dve", lambda h, s=s, k=k, T=T, dst=dst, tq=tq: h.scalar_tensor_tensor(out=dst, in0=win[s][:, k, 0:T], scalar=0.05, in1=ftmp[tq][:, 0:T], op0=ALU.add, op1=ALU.mult),
                                 reads=[bftmp[tq], bwin[s]], writes=[bdst])
                            if kind != 2:
                                P.op("dve", lambda h, s=s, k=k, fb=fb, T=T, dst=dst, ji=ji: h.tensor_reduce(out=asum[:, fb, k, ji:ji + 1], in_=dst, axis=AX.X, op=ALU.add, apply_absolute_value=True),
                                     reads=[bdst], writes=[basum])
                    if kind != 2:
                        P.dma(lambda h, s=s, c0=c0, T=T: h.dma_start(out=kund[:, :, c0:c0 + T], in_=ko[s][:, :, 0:T]), reads=[bko[s]], queue="sp")
                nrm = self.sb(st, "hf_nrm", [128, 2, 8])
                bnrm = P.buf()
                ex = self.sb(st, "hf_ex", [128, 2, 8])
                bex = P.buf()
                P.op("dve", lambda h: h.tensor_reduce(out=nrm[:], in_=asum[:], axis=AX.X, op=ALU.add), reads=[basum], writes=[bnrm])
                exb = self.sb(st, "hf_exb", [128, 8])
                bexb = P.buf()
                P.op("dve", lambda h: h.tensor_reduce(out=ex[:, 0, :], in_=ev[:, :, 0, 16:32], axis=AX.X, op=ALU.add, apply_absolute_value=True), reads=[bev], writes=[bex])
                P.op("dve", lambda h: h.tensor_reduce(out=ex[:, 1, :], in_=ev[:, :, 1, 0:32], axis=AX.X, op=ALU.add, apply_absolute_value=True), reads=[bev], writes=[bex])
                P.op("dve", lambda h: h.tensor_reduce(out=exb[:], in_=ev[:, :, 1, 1:17], axis=AX.X, op=ALU.add, apply_absolute_value=True), reads=[bev], writes=[bexb])
                P.op("dve", lambda h: h.tensor_tensor(out=ex[:, 1, :], in0=ex[:, 1, :], in1=exb[:], op=ALU.subtract), reads=[bex, bexb], writes=[bex])
                P.op("dve", lambda h: h.tensor_tensor(out=nrm[:], in0=nrm[:], in1=ex[:], op=ALU.add), reads=[bnrm, bex], writes=[bnrm])
                P.op("dve", lambda h: h.tensor_scalar(nrm[:], nrm[:], 1e-6, None, ALU.add), reads=[bnrm], writes=[bnrm])
                P.op("dve", lambda h: h.reciprocal(out=nrm[:], in_=nrm[:]), reads=[bnrm], writes=[bnrm])
                for k in range(8):
                    for fb in range(2):
                        P.op("dve", lambda h, k=k, fb=fb: h.tensor_scalar(ev[:, k, fb, :], ev[:, k, fb, :], nrm[:, fb, k:k + 1], None, ALU.mult), reads=[bev, bnrm], writes=[bev])
                    P.op("dve", lambda h, k=k: h.tensor_tensor(out=coef[:, k, 0, :], in0=ev[:, k, 0, 0:32], in1=ev[:, k, 1, 32:64], op=ALU.subtract), reads=[bev], writes=[bcoef])
                    P.op("dve", lambda h, k=k: h.tensor_tensor(out=coef[:, k, 1, :], in0=ev[:, k, 1, 0:32], in1=ev[:, k, 0, 32:64], op=ALU.subtract), reads=[bev], writes=[bcoef])
                P.dma(lambda h: h.dma_start(out=rnd.rearrange("f (k p) -> p f k", p=128), in_=nrm[:], allow_slow_non_contiguous=True), reads=[bnrm])
                P.barrier()
                P.release([bw, bw_2, bw_3] + bzs + bts + bko + [bnrm])
            with contextlib.ExitStack() as st:
                C = {}
                for k in ("f1", "c2", "s2", "ns2", "if2a", "if2b", "twc", "tws", "ntws", "twct", "twst", "ntwst", "gc", "gs"):
                    nm = "%s_%s" % (k, g.name)
                    shp = list(self.dram[nm].shape)
                    C[k] = self.load_const(st, nm, shp, FDT if k in ("f1", "c2", "s2", "ns2", "if2a", "if2b", "gc", "gs") else F32)
                crel = [v[1] for v in C.values()] + getattr(self, "const_rel", [])
                self.const_rel = []
                with contextlib.ExitStack() as s1:
                    sc = self.sb(s1, "hf_sc", [128, D])
                    bsc = P.buf()
                    bsc2 = P.buf()
                    P.dma(lambda h: h.dma_start(out=sc[0:M // 2, :], in_=rnd[0:1, :].to_broadcast([M // 2, D])), writes=[bsc])
                    P.dma(lambda h: h.dma_start(out=sc[M // 2:M, :], in_=rnd[1:2, :].to_broadcast([M // 2, D])), writes=[bsc2])
                    bscj = P.buf()
                    P.op("pool", lambda h: h.tensor_copy(out=sc[0:1, 0:1], in_=sc[0:1, 0:1]), reads=[bsc, bsc2], writes=[bscj])
                    rel = self.fft_pass(g, s1, C, self.dram["kun_" + g.name], M, "filter", {"sc": sc, "bsc": bscj})
                    P.barrier()
                    P.release(rel + [bsc, bsc2])
                if self.dbg.get("stop_h") == "filter":
                    P.release(crel)
                    return
                with contextlib.ExitStack() as s1:
                    wd = self.dram["b_hy_w_in"]
                    w = self.sb(s1, "h_w", [128, 8, 3 * D], BF16)
                    bw = P.buf()
                    P.dma(lambda h: h.dma_start(out=w[:], in_=wd[j].rearrange("(k p) d -> p k d", p=128)), writes=[bw])
                    hb = [self.sb(s1, "h_hb%d" % s, [128, 8, 512], BF16) for s in range(2)]
                    bhb = [P.buf() for _ in range(2)]
                    cv = [self.sb(s1, "h_cv%d" % s, [128, 512]) for s in range(3)]
                    bcv = [P.buf() for _ in range(3)]
                    x0o = [self.sb(s1, "h_x0o%d" % s, [128, 8, 512]) for s in range(2)]
                    bx0o = [P.buf() for _ in range(2)]
                    vgo = [self.sb(s1, "h_vgo%d" % s, [128, 8, 512]) for s in range(2)]
                    bvgo = [P.buf() for _ in range(2)]
                    tiles = tiles_even(L, 510)
                    bc_ = 0
                    for ti, (t0, T) in enumerate(tiles):
                        s = ti % 2
                        P.dma(lambda h, s=s, t0=t0, T=T: h.dma_start(out=hb[s][:, :, 0:T + 2], in_=hbin[:, :, t0:t0 + T + 2]), writes=[bhb[s]])
                        for k in range(8):
                            for part in (1, 2, 0):
                                n = part * 8 + k
                                bk = bc_ % 4
                                bc_ += 1
                                for kk in range(8):
                                    P.op("pe", lambda h, s=s, kk=kk, n=n, bk=bk, T=T: h.matmul(self.bank[bk][:, 0:T + 2], w[:, kk, n * 128:(n + 1) * 128], hb[s][:, kk, 0:T + 2], start=(kk == 0), stop=(kk == 7)),
                                         reads=[bw, bhb[s]], writes=[self.bbuf[bk]])
                                if part == 1:
                                    dst, bd_ = cv[0][:, 0:T], bcv[0]
                                elif part == 2:
                                    dst, bd_ = cv[1][:, 0:T], bcv[1]
                                else:
                                    dst, bd_ = x0o[s][:, k, 0:T], bx0o[s]
                                e = "dve"
                                P.op("act", lambda h, n=n, bk=bk, T=T, dst=dst: h.activation(out=dst, in_=self.bank[bk][:, 1:T + 1], func=AF.Identity, scale=self.vcol("hcw%d_1" % j, n), bias=self.vcol("hcb%d" % j, n)),
                                     reads=[self.bbuf[bk], self.bvecs], writes=[bd_])
                                P.op(e, lambda h, n=n, bk=bk, T=T, dst=dst: h.scalar_tensor_tensor(out=dst, in0=self.bank[bk][:, 0:T], scalar=self.vcol("hcw%d_0" % j, n), in1=dst, op0=ALU.mult, op1=ALU.add),
                                     reads=[self.bbuf[bk], bd_], writes=[bd_])
                                P.op(e, lambda h, n=n, bk=bk, T=T, dst=dst: h.scalar_tensor_tensor(out=dst, in0=self.bank[bk][:, 2:T + 2], scalar=self.vcol("hcw%d_2" % j, n), in1=dst, op0=ALU.mult, op1=ALU.add),
                                     reads=[self.bbuf[bk], bd_], writes=[bd_])
                                if part == 2:
                                    P.op("pool", lambda h, s=s, k=k, T=T: h.tensor_tensor(out=vgo[s][:, k, 0:T], in0=cv[0][:, 0:T], in1=cv[1][:, 0:T], op=ALU.mult), reads=[bcv[0], bcv[1]], writes=[bvgo[s]])
                        P.dma(lambda h, s=s, t0=t0, T=T: h.dma_start(out=x0d[:, :, t0:t0 + T], in_=x0o[s][:, :, 0:T]), reads=[bx0o[s]])
                        P.dma(lambda h, s=s, t0=t0, T=T: h.dma_start(out=vgd[:, :, t0:t0 + T], in_=vgo[s][:, :, 0:T]), reads=[bvgo[s]])
                    P.barrier()
                    P.release([bw] + bhb + bx0o + bvgo)
                if self.dbg.get("stop_h") == "inproj":
                    P.release(crel)
                    return
                with contextlib.ExitStack() as s1:
                    rel = self.fft_pass(g, s1, C, self.dram["vg_" + g.name], g.nin, "data", None)
                    P.barrier()
                    P.release(rel)
                P.release(crel)
            if self.dbg.get("stop_h") == "fft":
                return
            def prologue(what, arg):
                if what == "alloc":
                    st = arg
                    E = {}
                    E["ycs"] = [self.sb(st, "hp_yc%d" % s, [128, 8, 512]) for s in range(2)]
                    E["bycs"] = [P.buf() for _ in range(2)]
                    E["vgs"] = [self.sb(st, "hp_vg%d" % s, [128, 8, 512]) for s in range(2)]
                    E["bvgs"] = [P.buf() for _ in range(2)]
                    E["x0s"] = [self.sb(st, "hp_x0%d" % s, [128, 8, 512]) for s in range(2)]
                    E["bx0s"] = [P.buf() for _ in range(2)]
                    E["vh"] = self.sb(st, "hp_vh", [128, 8, 16])
                    E["vt"] = self.sb(st, "hp_vt", [128, 8, 16])
                    E["bvh"] = P.buf()
                    E["bvt"] = P.buf()
                    P.dma(lambda h: h.dma_start(out=E["vh"][:], in_=vgd[:, :, 0:16]), writes=[E["bvh"]])
                    P.dma(lambda h: h.dma_start(out=E["vt"][:], in_=vgd[:, :, L - 16:L]), writes=[E["bvt"]])
                    return E
                if what == "bufs":
                    E = arg
                    return E["bycs"] + E["bvgs"] + E["bx0s"] + [E["bvh"], E["bvt"]]
                ti, t0, T, xin, bxin, E = arg
                s = ti % 2
                yt, byt = E["ycs"][s], E["bycs"][s]
                P.dma(lambda h: h.dma_start(out=yt[:, :, 0:T], in_=ycd[:, :, t0:t0 + T]), writes=[byt])
                P.dma(lambda h: h.dma_start(out=E["vgs"][s][:, :, 0:T], in_=vgd[:, :, t0:t0 + T]), writes=[E["bvgs"][s]])
                P.dma(lambda h: h.dma_start(out=E["x0s"][s][:, :, 0:T], in_=x0d[:, :, t0:t0 + T]), writes=[E["bx0s"][s]])
                for k in range(8):
                    e = "dve"
                    if t0 == 0:
                        for m in range(H + 1, H + 16):
                            ee = m - H + 16
                            nt = L - m
                            so = m - (L - 16)
                            P.op(e, lambda h, k=k, ee=ee, nt=nt, so=so: h.scalar_tensor_tensor(out=yt[:, k, 0:nt], in0=E["vt"][:, k, so:so + nt], scalar=coef[:, k, 1, ee:ee + 1], in1=yt[:, k, 0:nt], op0=ALU.mult, op1=ALU.add),
                                 reads=[E["bvt"], bcoef, byt], writes=[byt])
                    if t0 + T == L and T == 16:
                        for d in range(H, H + 16):
                            ee = d - H + 16
                            nt = L - d
                            to = d - t0
                            P.op(e, lambda h, k=k, ee=ee, nt=nt, to=to: h.scalar_tensor_tensor(out=yt[:, k, to:to + nt], in0=E["vh"][:, k, 0:nt], scalar=coef[:, k, 0, ee:ee + 1], in1=yt[:, k, to:to + nt], op0=ALU.mult, op1=ALU.add),
                                 reads=[E["bvh"], bcoef, byt], writes=[byt])
                    P.op(e, lambda h, k=k: h.scalar_tensor_tensor(out=yt[:, k, 0:T], in0=E["vgs"][s][:, k, 0:T], scalar=self.vcol("skip%d" % j, k), in1=yt[:, k, 0:T], op0=ALU.mult, op1=ALU.add),
                         reads=[E["bvgs"][s], byt, self.bvecs], writes=[byt])
                    P.op("pool", lambda h, k=k: h.tensor_tensor(out=xin[:, k, 0:T], in0=yt[:, k, 0:T], in1=E["x0s"][s][:, k, 0:T], op=ALU.mult),
                         reads=[byt, E["bx0s"][s]], writes=[bxin])
            self.phase_proj(i, g, "b_hy_w_out", j, prologue=prologue)

    def phase_output(self, g):
        P = self.P
        hf = self.dram["hf0_" + g.name].rearrange("(k p) t -> p k t", p=128)
        y = self.dram["y_" + g.name]
        with contextlib.ExitStack() as st:
            xi = [self.sb(st, "o_x%d" % s, [128, 8, 512]) for s in range(2)]
            bxi = [P.buf() for _ in range(2)]
            yo = [self.sb(st, "o_y%d" % s, [128, 4, D]) for s in range(2)]
            byo = [P.buf() for _ in range(2)]
            for ti, t0 in enumerate(range(0, g.n, 512)):
                s = ti % 2
                P.dma(lambda h, s=s, t0=t0: h.dma_start(out=xi[s][:], in_=hf[:, :, NMETA + t0:NMETA + t0 + 512]), writes=[bxi[s]])
                for jj in range(4):
                    for k in range(8):
                        bk = (jj % 2) * 2 + k // 4
                        P.op("pe", lambda h, s=s, jj=jj, k=k, bk=bk: h.transpose(self.bank[bk][:, (k % 4) * 128:(k % 4 + 1) * 128], xi[s][:, k, jj * 128:(jj + 1) * 128], self.ident[:]),
                             reads=[bxi[s], self.bident], writes=[self.bbuf[bk]])
                        if k % 4 == 3:
                            e = "dve" if k == 3 else "act"
                            if e == "dve":
                                P.op("dve", lambda h, s=s, jj=jj, k=k, bk=bk: h.tensor_copy(out=yo[s][:, jj, (k // 4) * 512:(k // 4 + 1) * 512], in_=self.bank[bk][:]), reads=[self.bbuf[bk]], writes=[byo[s]])
                            else:
                                P.op("act", lambda h, s=s, jj=jj, k=k, bk=bk: h.copy(out=yo[s][:, jj, (k // 4) * 512:(k // 4 + 1) * 512], in_=self.bank[bk][:]), reads=[self.bbuf[bk]], writes=[byo[s]])
                P.dma(lambda h, s=s, t0=t0: h.dma_start(out=y[t0:t0 + 512, :].rearrange("(j p) d -> p j d", p=128), in_=yo[s][:]), reads=[byo[s]])
            P.barrier()
            P.release(bxi + byo)


GROUPS = None


def make_consts(groups):
    c = {}
    for g in groups:
        for k, v in fft_consts(g).items():
            c["%s_%s" % (k, g.name)] = v
        zt, tb = filter_grids(g)
        c["zt_" + g.name] = zt
        c["tb_" + g.name] = tb
        cs, sn = rope_tables(g)
        c["cos_" + g.name] = cs
        c["sin_" + g.name] = sn
    c["ident"] = np.eye(128, dtype=np.float32)
    return c


WNAMES = ("meta_tokens", "hy_w_in", "hy_w_out", "hy_filt_w1", "hy_filt_w2", "hy_filt_w3", "at_w_qkv", "at_w_out", "ffn_w_in", "ffn_w_out")


def kernel(**inputs):
    inp = {k: np.asarray(v) for k, v in inputs.items()}
    groups = [Group("p", inp["x_prompt"].shape[1]), Group("s", inp["x_sample"].shape[1])]
    vm, vecs = build_vecs(inp)
    consts = make_consts(groups)
    b = Builder(groups, vm)
    nc = b.build()
    ncore = 8
    in_maps = []
    for c in range(ncore):
        m = dict(consts)
        m["vecs"] = vecs
        for w in WNAMES:
            m[w] = np.ascontiguousarray(inp[w], dtype=np.float32)
        m["x_p"] = np.ascontiguousarray(inp["x_prompt"][c], dtype=np.float32)
        m["x_s"] = np.ascontiguousarray(inp["x_sample"][c], dtype=np.float32)
        in_maps.append(m)
    res = run_bass_kernel_spmd(nc, in_maps, core_ids=list(range(ncore)))
    yp = np.stack([np.asarray(r["y_p"]) for r in res.results], axis=0).astype(np.float32)
    ys = np.stack([np.asarray(r["y_s"]) for r in res.results], axis=0).astype(np.float32)
    return (yp, ys)
```

```python
import contextlib
import math
import os

import numpy as np
import concourse.bass as bass
import concourse.mybir as mybir
from concourse.bass_utils import run_bass_kernel_spmd

F32 = mybir.dt.float32
BF16 = mybir.dt.bfloat16
AF = mybir.ActivationFunctionType
ALU = mybir.AluOpType
AX = mybir.AxisListType

D = 1024
DFF = 2816
NFF = DFF // 128
DEPTH = 4
NMETA = 16
ALPHA = (2 * DEPTH) ** 0.25
LN_EPS = 1e-5
RMS_EPS = 1e-6
HD = 128
NH = 8
NKV = 2
PI = math.pi
FDT = BF16
CG = 4


class Stream:
    __slots__ = ("name", "sem", "ops", "step", "is_pe")

    def __init__(self, name, sem, step, is_pe=False):
        self.name, self.sem, self.ops, self.step, self.is_pe = name, sem, [], step, is_pe


class Op:
    __slots__ = ("cs", "fn", "deps", "needed", "count", "raw")

    def __init__(self, cs, fn, deps):
        self.cs, self.fn, self.needed, self.count = cs, fn, False, 0
        self.deps, self.raw = deps


class Buf:
    __slots__ = ("w", "r", "ld", "st")

    def __init__(self):
        self.w, self.r, self.ld, self.st = None, [], None, None


class _Rec:
    def __init__(self):
        self.call = None

    def __getattr__(self, name):
        def f(*a, **k):
            assert self.call is None, "op closure must make exactly one engine call"
            self.call = (name, a, k)
            return self
        return f


def _bind(fn):
    rec = _Rec()
    fn(rec)
    name, a, k = rec.call
    return lambda h: getattr(h, name)(*a, **k)


class Prog:
    ENGS = ("pe", "dve", "act", "pool", "sp")

    def __init__(self, nc, es, same_sync=True):
        self.nc, self.es, self.same_sync = nc, es, same_sync
        self.cs = {}
        self.issue = {e: [] for e in self.ENGS}
        for e in self.ENGS:
            self.cs[e] = Stream(e, es.enter_context(nc.semaphore("sem_" + e)), 1, e == "pe")
        self.dma_streams = []
        self.free_streams = []
        self.rr = 0
        self.store_q = "pool"

    def buf(self):
        return Buf()

    def release(self, bufs):
        for b in bufs:
            for s in (b.ld, b.st):
                if s is not None:
                    self.free_streams.append(s)
            b.ld = b.st = None

    def _dstream(self):
        if self.free_streams:
            return self.free_streams.pop()
        s = Stream("d%d" % len(self.dma_streams),
                   self.es.enter_context(self.nc.semaphore("dsem%d" % len(self.dma_streams))), 16)
        self.dma_streams.append(s)
        return s

    @staticmethod
    def _deps(reads, writes):
        deps = []
        raw = set()
        for b in reads:
            if b.w is not None:
                deps.append(b.w)
                raw.add(id(b.w))
        for b in writes:
            if b.w is not None:
                deps.append(b.w)
            deps.extend(b.r)
        return deps, raw

    @staticmethod
    def _commit(op, reads, writes):
        for b in reads:
            b.r.append(op)
        for b in writes:
            b.w = op
            b.r = []

    def op(self, eng, fn, reads=(), writes=()):
        cs = self.cs[eng]
        o = Op(cs, _bind(fn), self._deps(reads, writes))
        cs.ops.append(o)
        self.issue[eng].append(o)
        self._commit(o, reads, writes)
        return o

    def dma(self, fn, reads=(), writes=(), queue=None):
        if queue is None:
            queue = "sp" if writes else self.store_q
        if writes:
            track = writes[0]
            if track.ld is None:
                track.ld = self._dstream()
            cs = track.ld
        else:
            track = reads[0]
            if track.st is None:
                track.st = self._dstream()
            cs = track.st
        o = Op(cs, _bind(fn), self._deps(reads, writes))
        o.needed = True
        cs.ops.append(o)
        self.issue[queue].append(o)
        self._commit(o, reads, writes)
        return o

    def barrier(self):
        lasts = []
        for s in list(self.cs.values()) + self.dma_streams:
            if s.ops:
                lasts.append(s.ops[-1])
        for e in self.ENGS:
            o = Op(self.cs[e], None, (list(lasts), set(id(x) for x in lasts)))
            self.cs[e].ops.append(o)
            self.issue[e].append(o)

    def finalize_counts(self):
        for e in self.ENGS:
            own = self.cs[e]
            for o in self.issue[e]:
                for d in o.deps:
                    if d.cs is own and (own.is_pe or not self.same_sync or id(d) not in o.raw):
                        continue
                    d.needed = True
        for s in list(self.cs.values()) + self.dma_streams:
            c = 0
            for o in s.ops:
                if o.needed and o.fn is not None:
                    c += s.step
                o.count = c

    def emit(self, eng, h):
        own = self.cs[eng]
        seen = {}
        for o in self.issue[eng]:
            need = {}
            for d in o.deps:
                if d.cs is own and (own.is_pe or not self.same_sync or id(d) not in o.raw):
                    continue
                if d.count == 0 or seen.get(d.cs.name, 0) >= d.count:
                    continue
                if need.get(d.cs.name, (None, 0))[1] < d.count:
                    need[d.cs.name] = (d.cs, d.count)
            for nm, (s, c) in need.items():
                h.wait_ge(s.sem, c)
                seen[nm] = c
            if o.fn is None:
                continue
            ins = o.fn(h)
            if o.needed:
                ins.then_inc(o.cs.sem, o.cs.step)

    def run_block(self):
        self.barrier()
        self.finalize_counts()
        with self.nc.Block() as block:
            @block.tensor
            def _(h):
                self.emit("pe", h)

            @block.vector
            def _(h):
                self.emit("dve", h)

            @block.scalar
            def _(h):
                self.emit("act", h)

            @block.gpsimd
            def _(h):
                self.emit("pool", h)

            @block.sync
            def _(h):
                self.emit("sp", h)


def cols(v):
    v = np.asarray(v, np.float32)
    return np.ascontiguousarray(v.reshape(-1, 128).T)


class Group:
    def __init__(self, name, n_tok):
        self.name = name
        self.n = n_tok
        self.L = n_tok + NMETA
        self.M = 128 if n_tok > 4096 else 64
        self.N = self.M * self.M
        self.H = self.N // 2
        self.Hh = self.M // 2
        self.nk = self.Hh + 1
        self.nin = self.L // self.M + 1
        self.tail = self.L - (self.nin - 1) * self.M
        self.Lp = self.nin * self.M
        self.CG = 4 if self.M == 128 else 8
        assert self.L - self.H == 16 and self.tail == 16


def fft_consts(g):
    M, N, Hh, nk, nin = g.M, g.N, g.Hh, g.nk, g.nin
    c = {}
    n1 = np.arange(M)[:, None].astype(np.float64)
    k1 = np.arange(nk)[None, :].astype(np.float64)
    th = 2 * np.pi * n1 * k1 / M
    f1 = np.zeros((M, 2 * Hh))
    f1[:, :nk] = np.cos(th)
    f1[:, nk:] = -np.sin(th[:, 1:Hh])
    c["f1"] = f1
    n2 = np.arange(M)[:, None].astype(np.float64)
    k2 = np.arange(M)[None, :].astype(np.float64)
    th2 = 2 * np.pi * n2 * k2 / M
    c["c2"] = np.cos(th2)
    c["s2"] = np.sin(th2)
    c["ns2"] = -np.sin(th2)
    c["if2a"] = np.concatenate([np.cos(th2), np.sin(th2)], axis=1)
    c["if2b"] = np.concatenate([-np.sin(th2), np.cos(th2)], axis=1)
    tht = 2 * np.pi * n2 * k1 / N
    c["twc"] = np.cos(tht)
    c["tws"] = np.sin(tht)
    c["ntws"] = -np.sin(tht)
    c["twct"] = np.cos(tht).T.copy()
    c["twst"] = np.sin(tht).T.copy()
    c["ntwst"] = -np.sin(tht).T.copy()
    wk = np.full((nk, 1), 2.0)
    wk[0] = 1.0
    wk[Hh] = 1.0
    k1c = np.arange(nk)[:, None].astype(np.float64)
    n1r = np.arange(nin)[None, :].astype(np.float64)
    thg = 2 * np.pi * k1c * n1r / M
    c["gc"] = wk * np.cos(thg) / N
    c["gs"] = -wk * np.sin(thg) / N
    return {k: np.ascontiguousarray(v, dtype=np.float32) for k, v in c.items()}


def filter_grids(g):
    N, H, L = g.N, g.H, g.L
    j = np.arange(N)
    pos_c = np.where(j < H, j, N - j).astype(np.float64)
    e_pos = np.concatenate([[0], np.arange(H - 15, H + 16)]).astype(np.float64)
    e2 = np.concatenate([[0], e_pos[1:][::-1]])
    pos_e = np.concatenate([e_pos, e2])
    pos = np.concatenate([pos_c, pos_e])
    t = pos / (L - 1)
    w = (2.0 * np.pi / L) * pos
    bands = np.linspace(1e-4, 15, 16)
    z = np.concatenate([t[:, None], np.cos(w[:, None] * bands), -np.sin(w[:, None] * bands)], axis=1)
    zt = np.ascontiguousarray(z.T, dtype=np.float32)
    tb = np.ascontiguousarray(np.broadcast_to(t[None, :].astype(np.float32), (128, N + 64)))
    return zt, tb


def rope_tables(g):
    n = g.n
    rows = n // 64
    row = np.concatenate([np.full((NMETA,), -1.0), np.repeat(np.arange(rows), 64)]).astype(np.float32)
    col = np.concatenate([np.arange(NMETA), np.tile(np.arange(64), rows)]).astype(np.float32)
    inv = (np.float32(10000.0) ** (-np.arange(0, 64, 2, dtype=np.float32) / np.float32(64))).astype(np.float32)
    ang = np.concatenate([row[:, None] * inv, col[:, None] * inv], axis=1)
    cs = np.cos(ang).T
    sn = np.sin(ang).T
    cos_t = np.concatenate([cs, cs], axis=0)
    sin_t = np.concatenate([-sn, sn], axis=0)
    return np.ascontiguousarray(cos_t, np.float32), np.ascontiguousarray(sin_t, np.float32)


def tiles_even(L, tmax):
    n = -(-L // tmax)
    T = -(-L // n)
    out = []
    t0 = 0
    while t0 < L:
        out.append((t0, min(T, L - t0)))
        t0 += T
    return out


class VecMap:
    def __init__(self):
        self.off = {}
        self.n = 0
        self.parts = []

    def add(self, name, arr):
        a = np.asarray(arr, np.float32)
        if a.shape[0] != 128:
            a = np.concatenate([a, np.zeros((128 - a.shape[0], a.shape[1]), np.float32)], axis=0)
        self.off[name] = self.n
        self.n += a.shape[1]
        self.parts.append(a)

    def build(self):
        return np.ascontiguousarray(np.concatenate(self.parts, axis=1))


def build_vecs(inp):
    vm = VecMap()
    for i in range(DEPTH):
        vm.add("ln1g%d" % i, cols(inp["ln1_g"][i]))
        vm.add("ln1b%d" % i, cols(inp["ln1_b"][i]))
        vm.add("ln2g%d" % i, cols(inp["ln2_g"][i]))
        vm.add("ln2b%d" % i, cols(inp["ln2_b"][i]))
        for tap in range(3):
            vm.add("fcw%d_%d" % (i, tap), cols(inp["ffn_conv_w"][i, tap]))
        vm.add("fcb%d" % i, cols(inp["ffn_conv_b"][i]))
    for j in range(2):
        for tap in range(3):
            vm.add("hcw%d_%d" % (j, tap), cols(inp["hy_conv_w"][j, tap]))
        vm.add("hcb%d" % j, cols(inp["hy_conv_b"][j]))
        vm.add("skip%d" % j, cols(inp["hy_skip"][j]))
        vm.add("fb3_%d" % j, cols(inp["hy_filt_b3"][j]))
        vm.add("fb1_%d" % j, np.asarray(inp["hy_filt_b1"][j], np.float32)[:, None])
        vm.add("fb2_%d" % j, np.asarray(inp["hy_filt_b2"][j], np.float32)[:, None])
        vm.add("ffr_%d" % j, np.asarray(inp["hy_filt_freq"][j], np.float32)[:, None])
        qg = np.asarray(inp["at_q_gain"][j], np.float32)
        kg = np.asarray(inp["at_k_gain"][j], np.float32)
        vm.add("qg%d" % j, qg[:, None])
        vm.add("kg%d" % j, kg[:, None])
    max_decay = math.log(1.0 / 1e-2) / 0.3
    min_decay = math.log(1.0 / 1e-2) / 1.5
    deltas = np.linspace(min_decay, max_decay, D, dtype=np.float32)
    vm.add("ndelta", cols(-deltas))
    return vm, vm.build()


class Builder:
    def __init__(self, groups, vm, dbg=None):
        self.groups = groups
        self.vm = vm
        self.dbg = dbg or {}
        self.nc = bass.Bass("TRN2", target_bir_lowering=False)
        self.es = contextlib.ExitStack()
        self.P = None
        self.dram = {}

    def din(self, name, shape, dt=F32):
        t = self.nc.dram_tensor(name, list(shape), dt, kind="ExternalInput").ap()
        self.dram[name] = t
        return t

    def dout(self, name, shape, dt=F32):
        t = self.nc.dram_tensor(name, list(shape), dt, kind="ExternalOutput").ap()
        self.dram[name] = t
        return t

    def dscr(self, name, shape, dt=F32):
        kind = "ExternalOutput" if name in self.dbg.get("outs", ()) else "Internal"
        t = self.nc.dram_tensor(name, list(shape), dt, kind=kind).ap()
        self.dram[name] = t
        return t

    def sb(self, st, name, shape, dt=F32):
        self.uid = getattr(self, "uid", 0) + 1
        return st.enter_context(self.nc.sbuf_tensor("sb%d_%s" % (self.uid, name), list(shape), dt))

    def vcol(self, name, k=0, p=128):
        o = self.vm.off[name] + k
        return self.vecs[0:p, o:o + 1]

    def build(self):
        nc, es = self.nc, self.es
        with es:
            self.P = P = Prog(nc, es, same_sync=bool(self.dbg.get("same_sync", True)))
            self.declare_io()
            self.bank = [es.enter_context(nc.psum_tensor("bank%d" % i, [128, 512], F32)) for i in range(8)]
            self.bbuf = [P.buf() for _ in range(8)]
            self.vecs = self.sb(es, "vecs", [128, self.vm.n])
            self.bvecs = P.buf()
            P.dma(lambda h: h.dma_start(out=self.vecs[:], in_=self.dram["vecs"][:, :]), writes=[self.bvecs])
            self.ident = self.sb(es, "ident", [128, 128])
            self.bident = P.buf()
            P.dma(lambda h: h.dma_start(out=self.ident[:], in_=self.dram["ident"][:, :]), writes=[self.bident])
            self.onesD = self.sb(es, "onesD", [128, 128], BF16)
            self.onesH = self.sb(es, "onesH", [128, 128], BF16)
            self.ones1 = self.sb(es, "ones1", [128, 128], BF16)
            self.bones = P.buf()
            P.op("dve", lambda h: h.memset(self.onesD[:], 1.0 / D), writes=[self.bones])
            P.op("dve", lambda h: h.memset(self.onesH[:], 1.0 / HD), writes=[self.bones])
            P.op("dve", lambda h: h.memset(self.ones1[:], 1.0), writes=[self.bones])
            self.zero = self.sb(es, "zero", [128, 8], BF16)
            self.bzero = P.buf()
            P.op("dve", lambda h: h.memset(self.zero[:], 0.0), writes=[self.bzero])
            stop = self.dbg.get("stop", "end")
            self.phase_prep_weights()
            for g in self.groups:
                if stop != "setup":
                    self.phase_input(g)
            P.barrier()
            done = (stop in ("input", "setup"))
            for i in range(DEPTH):
                if done:
                    break
                for g in self.groups:
                    if i % 2 == 0:
                        self.layer_hyena(i, g)
                    else:
                        self.layer_attn(i, g)
                    if stop == "mix%d" % i:
                        continue
                    self.phase_ffn(i, g)
                if stop in ("mix%d" % i, "ffn%d" % i):
                    done = True
            if not done:
                for g in self.groups:
                    self.phase_output(g)
            P.run_block()
        return nc

    def declare_io(self):
        for g in self.groups:
            self.din("x_" + g.name, [g.n, D])
            self.dout("y_" + g.name, [g.n, D])
            self.din("cos_" + g.name, [128, g.L])
            self.din("sin_" + g.name, [128, g.L])
            self.din("zt_" + g.name, [33, g.N + 64])
            self.din("tb_" + g.name, [128, g.N + 64])
            for k in ("f1", "c2", "s2", "ns2", "if2a", "if2b", "twc", "tws", "ntws", "twct", "twst", "ntwst", "gc", "gs"):
                shp = {"f1": [g.M, 2 * g.Hh], "c2": [g.M, g.M], "s2": [g.M, g.M], "ns2": [g.M, g.M],
                       "if2a": [g.M, 2 * g.M], "if2b": [g.M, 2 * g.M], "twc": [g.M, g.nk], "tws": [g.M, g.nk],
                       "ntws": [g.M, g.nk], "twct": [g.nk, g.M], "twst": [g.nk, g.M], "ntwst": [g.nk, g.M],
                       "gc": [g.nk, g.nin], "gs": [g.nk, g.nin]}[k]
                self.din("%s_%s" % (k, g.name), shp)
            for pp in range(2):
                self.dscr("hf%d_%s" % (pp, g.name), [D, g.L])
                self.dscr("hb%d_%s" % (pp, g.name), [D, g.L + 2], BF16)
            self.dscr("x0_" + g.name, [D, g.L])
            self.dscr("vg_" + g.name, [D, g.L])
            self.dscr("yc_" + g.name, [D, g.Lp])
            self.dscr("kun_" + g.name, [D, g.N])
            self.dscr("kf_" + g.name, [D // g.CG, g.M, g.CG * 2 * g.nk])
            self.dscr("rn_" + g.name, [2, D])
            self.dscr("qT_" + g.name, [D, g.L], BF16)
            self.dscr("xo_" + g.name, [D, g.L], BF16)
        self.din("vecs", [128, self.vm.n])
        self.din("ident", [128, 128])
        self.din("meta_tokens", [NMETA, D])
        self.din("hy_w_in", [2, D, 3 * D])
        self.din("hy_w_out", [2, D, D])
        self.din("hy_filt_w1", [2, 33, 64])
        self.din("hy_filt_w2", [2, 64, 64])
        self.din("hy_filt_w3", [2, 64, 2 * D])
        self.din("at_w_qkv", [2, D, 1536])
        self.din("at_w_out", [2, D, D])
        self.din("ffn_w_in", [DEPTH, D, 2 * DFF])
        self.din("ffn_w_out", [DEPTH, DFF, D])
        self.dscr("b_hy_w_in", [2, D, 3 * D], BF16)
        self.dscr("b_hy_w_out", [2, D, D], BF16)
        self.dscr("b_at_w_qkv", [2, D, 1536], BF16)
        self.dscr("b_at_w_out", [2, D, D], BF16)
        self.dscr("b_ffn_w1", [DEPTH, NFF, 128, 2, 8, 128], BF16)
        self.dscr("b_ffn_w_out", [DEPTH, DFF, D], BF16)

    def phase_prep_weights(self):
        P, nc = self.P, self.nc
        if self.dbg.get("skip_prep"):
            return
        with contextlib.ExitStack() as st:
            NS = 4
            CW = 3072
            src = [self.sb(st, "pw_s%d" % i, [128, CW]) for i in range(NS)]
            dst = [self.sb(st, "pw_d%d" % i, [128, CW], BF16) for i in range(NS)]
            bs = [P.buf() for _ in range(NS)]
            bd = [P.buf() for _ in range(NS)]
            cnt = [0]
            engs = ("dve", "act")

            def job(src_ap, ncols, dst_ap_fn):
                i = cnt[0] % NS
                e = engs[cnt[0] % 2]
                cnt[0] += 1
                P.dma(lambda h: h.dma_start(out=src[i][:, 0:ncols], in_=src_ap), writes=[bs[i]])
                if e == "act":
                    P.op("act", lambda h: h.copy(out=dst[i][:, 0:ncols], in_=src[i][:, 0:ncols]), reads=[bs[i]], writes=[bd[i]])
                else:
                    P.op(e, lambda h: h.tensor_copy(out=dst[i][:, 0:ncols], in_=src[i][:, 0:ncols]), reads=[bs[i]], writes=[bd[i]])
                o, s = dst_ap_fn(dst[i])
                P.dma(lambda h: h.dma_start(out=o, in_=s), reads=[bd[i]])

            def plain(name, lead, K, Ncol):
                s = self.dram[name]
                d = self.dram["b_" + name]
                for l in range(lead):
                    for k in range(K // 128):
                        for c0 in range(0, Ncol, CW):
                            w = min(CW, Ncol - c0)
                            job(s[l, k * 128:(k + 1) * 128, c0:c0 + w], w,
                                (lambda l=l, k=k, c0=c0, w=w: (lambda t: (d[l, k * 128:(k + 1) * 128, c0:c0 + w], t[:, 0:w])))())
            plain("hy_w_in", 2, D, 3 * D)
            plain("hy_w_out", 2, D, D)
            plain("at_w_qkv", 2, D, 1536)
            plain("at_w_out", 2, D, D)
            plain("ffn_w_out", DEPTH, DFF, D)
            s = self.dram["ffn_w_in"]
            d = self.dram["b_ffn_w1"]
            for l in range(DEPTH):
                for k in range(8):
                    for half in range(2):
                        def mk(l=l, k=k, half=half):
                            def f(t):
                                o = d[l, :, :, half, k, :].rearrange("n p c -> p n c")
                                return o, t[:, 0:DFF].rearrange("p (n c) -> p n c", c=128)
                            return f
                        job(s[l, k * 128:(k + 1) * 128, half * DFF:(half + 1) * DFF], DFF, mk())
            P.barrier()
            P.release(bs + bd)

    def phase_input(self, g):
        P = self.P
        hf = self.dram["hf0_" + g.name]
        hb = self.dram["hb0_" + g.name]
        x = self.dram["x_" + g.name]
        with contextlib.ExitStack() as st:
            xt = [self.sb(st, "in_x%d" % i, [128, 4, D]) for i in range(2)]
            bx = [P.buf() for _ in range(2)]
            of = [self.sb(st, "in_of%d" % i, [128, 8, 512]) for i in range(2)]
            ob = [self.sb(st, "in_ob%d" % i, [128, 8, 512], BF16) for i in range(2)]
            bof = [P.buf() for _ in range(2)]
            bob = [P.buf() for _ in range(2)]
            for pp in range(0 if not self.dbg.get("no_halo") else 2, 2):
                hbp = self.dram["hb%d_%s" % (pp, g.name)].rearrange("(k p) t -> p k t", p=128)
                for c in (0, g.L + 1):
                    P.dma(lambda h, hbp=hbp, c=c: h.dma_start(out=hbp[:, :, c:c + 1], in_=self.zero[:, 0:8].rearrange("p (k o) -> p k o", o=1), allow_slow_non_contiguous=True),
                          reads=[self.bzero])
            jobs = [("meta", 0, NMETA)] + [("x", t0, 512) for t0 in range(0, g.n, 512)]
            if self.dbg.get("in_nometa"):
                jobs = jobs[1:]
            for ji, (kind, t0, T) in enumerate(jobs):
                s = ji % 2
                nch = (T + 127) // 128
                if kind == "meta":
                    P.dma(lambda h, s=s: h.dma_start(out=xt[s][0:NMETA, 0, :], in_=self.dram["meta_tokens"][:, :]), writes=[bx[s]])
                    dcol = 0
                else:
                    P.dma(lambda h, s=s, t0=t0: h.dma_start(out=xt[s][:, :, :], in_=x[t0:t0 + 512, :].rearrange("(j p) d -> p j d", p=128)), writes=[bx[s]])
                    dcol = NMETA + t0
                for k in range(8):
                    bk = k % 2
                    for j in range(nch):
                        tp = min(128, T - j * 128)
                        P.op("pe", lambda h, s=s, k=k, j=j, tp=tp, bk=bk: h.transpose(self.bank[bk][:, j * 128:j * 128 + tp], xt[s][0:tp, j, k * 128:(k + 1) * 128], self.ident[0:tp, 0:tp]),
                             reads=[bx[s], self.bident], writes=[self.bbuf[bk]])
                    P.op("dve", lambda h, s=s, k=k, bk=bk, T=T: h.tensor_copy(out=of[s][:, k, 0:T], in_=self.bank[bk][:, 0:T]), reads=[self.bbuf[bk]], writes=[bof[s]])
                    P.op("act", lambda h, s=s, k=k, T=T: h.copy(out=ob[s][:, k, 0:T], in_=of[s][:, k, 0:T]), reads=[bof[s]], writes=[bob[s]])
                if not self.dbg.get("in_nohf"):
                    P.dma(lambda h, s=s, T=T, dcol=dcol: h.dma_start(out=hf.rearrange("(k p) t -> p k t", p=128)[:, :, dcol:dcol + T], in_=of[s][:, :, 0:T]), reads=[bof[s]])
                if not self.dbg.get("in_nohb"):
                    P.dma(lambda h, s=s, T=T, dcol=dcol: h.dma_start(out=hb.rearrange("(k p) t -> p k t", p=128)[:, :, 1 + dcol:1 + dcol + T], in_=ob[s][:, :, 0:T]), reads=[bob[s]])
            P.barrier()
            P.release(bx + bof + bob)

    def res_ln(self, st_bufs, i, which, g, t0, T, out_pp, acc_fn, defer=False):
        P = self.P
        S = st_bufs
        hres, bhres = S["hres2"][S["hs"]], S["bhres2"][S["hs"]]
        r, br = S["r"], S["br"]
        rb, brb = S["rb"], S["brb"]
        r2, br2 = S["r2"], S["br2"]
        ob, bob = S["ob"], S["bob"]
        mean, bmean = S["mean"], S["bmean"]
        rstd, brstd = S["rstd"], S["brstd"]
        gname = "ln%dg%d" % (which, i)
        bname = "ln%db%d" % (which, i)
        for k in range(8):
            bk = S["obanks"][k % len(S["obanks"])]
            acc_fn(k, bk)
            P.op("dve", lambda h, k=k, bk=bk: h.scalar_tensor_tensor(out=r[:, k, 0:T], in0=hres[:, k, 0:T], scalar=float(ALPHA), in1=self.bank[bk][:, 0:T], op0=ALU.mult, op1=ALU.add),
                 reads=[bhres, self.bbuf[bk]], writes=[br[k]])
            P.op("act", lambda h, k=k: h.copy(out=rb[:, k, 0:T], in_=r[:, k, 0:T]), reads=[br[k]], writes=[brb[k]])
            P.op("act", lambda h, k=k: h.activation(out=r2[:, k, 0:T], in_=r[:, k, 0:T], func=AF.Square), reads=[br[k]], writes=[br2[k]])
        bm, be = S["sbanks"]
        hf = self.dram["hf%d_%s" % (out_pp, g.name)].rearrange("(k p) t -> p k t", p=128)
        hb = self.dram["hb%d_%s" % (out_pp, g.name)].rearrange("(k p) t -> p k t", p=128)

        def part_b():
            for k in range(8):
                P.op("pe", lambda h, k=k: h.matmul(self.bank[bm][:, 0:T], self.onesD[:], rb[:, k, 0:T], start=(k == 0), stop=(k == 7)),
                     reads=[brb[k], self.bones], writes=[self.bbuf[bm]])
            for k in range(8):
                P.op("pe", lambda h, k=k: h.matmul(self.bank[be][:, 0:T], self.onesD[:], r2[:, k, 0:T], start=(k == 0), stop=(k == 7)),
                     reads=[br2[k], self.bones], writes=[self.bbuf[be]])
            P.op("act", lambda h: h.copy(out=mean[:, 0:T], in_=self.bank[bm][:, 0:T]), reads=[self.bbuf[bm]], writes=[bmean])
            P.op("dve", lambda h: h.tensor_tensor(out=rstd[:, 0:T], in0=mean[:, 0:T], in1=mean[:, 0:T], op=ALU.mult), reads=[bmean], writes=[brstd])
            P.op("dve", lambda h: h.scalar_tensor_tensor(out=rstd[:, 0:T], in0=self.bank[be][:, 0:T], scalar=float(LN_EPS), in1=rstd[:, 0:T], op0=ALU.add, op1=ALU.subtract), reads=[self.bbuf[be], brstd], writes=[brstd])
            P.op("act", lambda h: h.activation(out=rstd[:, 0:T], in_=rstd[:, 0:T], func=AF.Sqrt), reads=[brstd], writes=[brstd])
            P.op("dve", lambda h: h.reciprocal(out=rstd[:, 0:T], in_=rstd[:, 0:T]), reads=[brstd], writes=[brstd])

        def part_c(k):
            P.op("pool", lambda h: h.tensor_tensor(out=r[:, k, 0:T], in0=r[:, k, 0:T], in1=mean[:, 0:T], op=ALU.subtract), reads=[br[k], bmean], writes=[br[k]])
            P.op("dve", lambda h: h.tensor_tensor(out=r[:, k, 0:T], in0=r[:, k, 0:T], in1=rstd[:, 0:T], op=ALU.mult), reads=[br[k], brstd], writes=[br[k]])
            P.op("act", lambda h: h.activation(out=ob[:, k, 0:T], in_=r[:, k, 0:T], func=AF.Identity, scale=self.vcol(gname, k), bias=self.vcol(bname, k)),
                 reads=[br[k], self.bvecs], writes=[bob])
            P.op("act", lambda h: h.activation(out=r[:, k, 0:T], in_=r[:, k, 0:T], func=AF.Identity, scale=self.vcol(gname, k), bias=self.vcol(bname, k)),
                 reads=[br[k], self.bvecs], writes=[br[k]])

        def part_d():
            P.dma(lambda h: h.dma_start(out=hf[:, :, t0:t0 + T], in_=r[:, :, 0:T]), reads=[S["brall"]] + br)
            P.dma(lambda h: h.dma_start(out=hb[:, :, 1 + t0:1 + t0 + T], in_=ob[:, :, 0:T]), reads=[bob])

        pieces = [part_b] + [(lambda k=k: part_c(k)) for k in range(8)] + [part_d]
        if defer:
            return pieces
        for p_ in pieces:
            p_()
        return []

    def alloc_res_ln(self, st, pfx, obanks, sbanks, nh=1):
        P = self.P
        S = {}
        S["hres2"] = [self.sb(st, pfx + "hres%d" % q, [128, 8, 512]) for q in range(nh)]
        S["bhres2"] = [P.buf() for _ in range(nh)]
        S["hs"] = 0
        S["r"] = self.sb(st, pfx + "r", [128, 8, 512])
        S["br"] = [P.buf() for _ in range(8)]
        S["brall"] = P.buf()
        S["rb"] = self.sb(st, pfx + "rb", [128, 8, 512], BF16)
        S["brb"] = [P.buf() for _ in range(8)]
        S["r2"] = self.sb(st, pfx + "r2", [128, 8, 512], BF16)
        S["br2"] = [P.buf() for _ in range(8)]
        S["ob"] = self.sb(st, pfx + "ob", [128, 8, 512], BF16)
        S["bob"] = P.buf()
        S["mean"] = self.sb(st, pfx + "mean", [128, 512])
        S["bmean"] = P.buf()
        S["rstd"] = self.sb(st, pfx + "rstd", [128, 512])
        S["brstd"] = P.buf()
        S["obanks"] = obanks
        S["sbanks"] = sbanks
        return S

    def res_ln_bufs(self, S):
        return S["bhres2"] + [S["brall"], S["bob"], S["bmean"], S["brstd"]] + S["br"] + S["brb"] + S["br2"]

    def phase_ffn(self, i, g):
        P = self.P
        hbin = self.dram["hb1_" + g.name].rearrange("(k p) t -> p k t", p=128)
        hfin = self.dram["hf1_" + g.name].rearrange("(k p) t -> p k t", p=128)
        w1d = self.dram["b_ffn_w1"]
        w2d = self.dram["b_ffn_w_out"]
        with contextlib.ExitStack() as st:
            w2 = self.sb(st, "f_w2", [128, NFF, D], BF16)
            bw2 = P.buf()
            P.dma(lambda h: h.dma_start(out=w2[:, 0:11, :], in_=w2d[i, 0:11 * 128, :].rearrange("(n p) d -> p n d", p=128)), writes=[bw2])
            bw2b = P.buf()
            P.dma(lambda h: h.dma_start(out=w2[:, 11:22, :], in_=w2d[i, 11 * 128:22 * 128, :].rearrange("(n p) d -> p n d", p=128)), writes=[bw2b])
            NW = 8
            w1 = [self.sb(st, "f_w1_%d" % s, [128, 2, 8, 128], BF16) for s in range(NW)]
            bw1 = [P.buf() for _ in range(NW)]
            hb = [self.sb(st, "f_hb%d" % s, [128, 8, 512], BF16) for s in range(2)]
            bhb = [P.buf() for _ in range(2)]
            act = self.sb(st, "f_act", [128, NFF, 512], BF16)
            bact = [P.buf() for _ in range(NFF)]
            tmp = [self.sb(st, "f_tmp%d" % s, [128, 512]) for s in range(2)]
            btmp = [P.buf() for _ in range(2)]
            gl = [self.sb(st, "f_gl%d" % s, [128, 512]) for s in range(2)]
            bgl = [P.buf() for _ in range(2)]
            asb = [self.sb(st, "f_asb%d" % s, [128, 512]) for s in range(2)]
            basb = [P.buf() for _ in range(2)]
            S = self.alloc_res_ln(st, "f_", obanks=[4, 5], sbanks=[4, 5], nh=2)
            abanks = [2, 3, 6, 7]
            tiles = tiles_even(g.L, 510)
            wi = 0
            pend_ln = []
            for ti, (t0, T) in enumerate(tiles):
                s = ti % 2
                P.dma(lambda h, s=s, t0=t0, T=T: h.dma_start(out=hb[s][:, :, 0:T + 2], in_=hbin[:, :, t0:t0 + T + 2]), writes=[bhb[s]])
                P.dma(lambda h, t0=t0, T=T, s=s: h.dma_start(out=S["hres2"][s][:, :, 0:T], in_=hfin[:, :, t0:t0 + T]), writes=[S["bhres2"][s]])
                for n in range(NFF):
                    if n >= 1 and pend_ln:
                        pend_ln.pop(0)()
                    ws = wi % NW
                    wi += 1
                    P.dma(lambda h, ws=ws, n=n: h.dma_start(out=w1[ws][:], in_=w1d[i, n]), writes=[bw1[ws]])
                    bg = n % 2
                    ba = abanks[n % 4]
                    for k in range(8):
                        P.op("pe", lambda h, ws=ws, k=k, s=s, T=T, bg=bg: h.matmul(self.bank[bg][:, 0:T + 2], w1[ws][:, 0, k, :], hb[s][:, k, 0:T + 2], start=(k == 0), stop=(k == 7)),
                             reads=[bw1[ws], bhb[s]], writes=[self.bbuf[bg]])
                    for k in range(8):
                        P.op("pe", lambda h, ws=ws, k=k, s=s, T=T, ba=ba: h.matmul(self.bank[ba][:, 0:T], w1[ws][:, 1, k, :], hb[s][:, k, 1:T + 1], start=(k == 0), stop=(k == 7)),
                             reads=[bw1[ws], bhb[s]], writes=[self.bbuf[ba]])
                    q = n % 2
                    P.op("dve", lambda h, n=n, q=q, bg=bg, T=T: h.tensor_scalar(tmp[q][:, 0:T], self.bank[bg][:, 1:T + 1], self.vcol("fcw%d_1" % i, n), self.vcol("fcb%d" % i, n), ALU.mult, ALU.add),
                         reads=[self.bbuf[bg], self.bvecs], writes=[btmp[q]])
                    P.op("dve", lambda h, n=n, q=q, bg=bg, T=T: h.scalar_tensor_tensor(out=tmp[q][:, 0:T], in0=self.bank[bg][:, 0:T], scalar=self.vcol("fcw%d_0" % i, n), in1=tmp[q][:, 0:T], op0=ALU.mult, op1=ALU.add),
                         reads=[self.bbuf[bg], btmp[q]], writes=[btmp[q]])
                    P.op("dve", lambda h, n=n, q=q, bg=bg, T=T: h.scalar_tensor_tensor(out=tmp[q][:, 0:T], in0=self.bank[bg][:, 2:T + 2], scalar=self.vcol("fcw%d_2" % i, n), in1=tmp[q][:, 0:T], op0=ALU.mult, op1=ALU.add),
                         reads=[self.bbuf[bg], btmp[q]], writes=[btmp[q]])
                    P.op("act", lambda h, q=q, T=T: h.activation(out=gl[q][:, 0:T], in_=tmp[q][:, 0:T], func=AF.Gelu), reads=[btmp[q]], writes=[bgl[q]])
                    P.op("act", lambda h, q=q, ba=ba, T=T: h.copy(out=asb[q][:, 0:T], in_=self.bank[ba][:, 0:T]), reads=[self.bbuf[ba]], writes=[basb[q]])
                    P.op("pool", lambda h, n=n, q=q, T=T: h.tensor_tensor(out=act[:, n, 0:T], in0=asb[q][:, 0:T], in1=gl[q][:, 0:T], op=ALU.mult),
                         reads=[basb[q], bgl[q]], writes=[bact[n]])

                def acc(k, bk, T=T):
                    for n in range(NFF):
                        P.op("pe", lambda h, n=n, k=k, bk=bk: h.matmul(self.bank[bk][:, 0:T], w2[:, n, k * 128:(k + 1) * 128], act[:, n, 0:T], start=(n == 0), stop=(n == NFF - 1)),
                             reads=[bact[n], bw2, bw2b], writes=[self.bbuf[bk]])
                S["hs"] = s
                pend_ln = self.res_ln(S, i, 2, g, t0, T, 0, acc, defer=False)
            while pend_ln:
                pend_ln.pop(0)()
            P.barrier()
            P.release([bw2, bw2b] + bw1 + bhb + self.res_ln_bufs(S))

    def phase_proj(self, i, g, wname, j, prologue=None):
        P = self.P
        wd = self.dram[wname]
        xo = self.dram["xo_" + g.name].rearrange("(k p) t -> p k t", p=128)
        hfin = self.dram["hf0_" + g.name].rearrange("(k p) t -> p k t", p=128)
        with contextlib.ExitStack() as st:
            w = self.sb(st, "p_w", [128, 8, D], BF16)
            bw = P.buf()
            P.dma(lambda h: h.dma_start(out=w[:], in_=wd[j].rearrange("(k p) d -> p k d", p=128)), writes=[bw])
            xin = [self.sb(st, "p_x%d" % s, [128, 8, 512], BF16) for s in range(2)]
            bxin = [P.buf() for _ in range(2)]
            S = self.alloc_res_ln(st, "p_", obanks=[0, 1, 2, 3], sbanks=[6, 7])
            extra = prologue("alloc", st) if prologue else None
            tiles = tiles_even(g.L, 512) if not prologue else [(t0, min(512, g.L - t0)) for t0 in range(0, g.L, 512)]
            for ti, (t0, T) in enumerate(tiles):
                s = ti % 2
                if prologue:
                    prologue("tile", (ti, t0, T, xin[s], bxin[s], extra))
                else:
                    P.dma(lambda h, s=s, t0=t0, T=T: h.dma_start(out=xin[s][:, :, 0:T], in_=xo[:, :, t0:t0 + T]), writes=[bxin[s]])
                P.dma(lambda h, t0=t0, T=T: h.dma_start(out=S["hres2"][0][:, :, 0:T], in_=hfin[:, :, t0:t0 + T]), writes=[S["bhres2"][0]])
                S["hs"] = 0

                def acc(k, bk, T=T, s=s):
                    for kk in range(8):
                        P.op("pe", lambda h, kk=kk, k=k, bk=bk: h.matmul(self.bank[bk][:, 0:T], w[:, kk, k * 128:(k + 1) * 128], xin[s][:, kk, 0:T], start=(kk == 0), stop=(kk == 7)),
                             reads=[bxin[s], bw], writes=[self.bbuf[bk]])
                self.res_ln(S, i, 1, g, t0, T, 1, acc)
            P.barrier()
            rel = [bw] + bxin + self.res_ln_bufs(S)
            if prologue:
                rel += prologue("bufs", extra)
            P.release(rel)

    def layer_attn(self, i, g):
        P = self.P
        j = i // 2
        L = g.L
        hbin = self.dram["hb0_" + g.name].rearrange("(k p) t -> p k t", p=128)
        wq = self.dram["b_at_w_qkv"]
        qT = self.dram["qT_" + g.name].rearrange("(k p) t -> p k t", p=128)
        xo = self.dram["xo_" + g.name].rearrange("(k p) t -> p k t", p=128)
        cosd = self.dram["cos_" + g.name]
        sind = self.dram["sin_" + g.name]
        tiles = [(t0, min(512, L - t0)) for t0 in range(0, L, 512)]
        nkc = (L + 127) // 128
        with contextlib.ExitStack() as st:
            kT = self.sb(st, "a_kT", [128, NKV, L], BF16)
            bkT = [[P.buf() for _ in tiles] for _ in range(NKV)]
            vv = self.sb(st, "a_v", [128, nkc, NKV * HD], BF16)
            bvv = [P.buf() for _ in range(nkc)]
            with contextlib.ExitStack() as s1:
                w = self.sb(s1, "a_w", [128, 8, 1536], BF16)
                bw = P.buf()
                P.dma(lambda h: h.dma_start(out=w[:], in_=wq[j].rearrange("(k p) d -> p k d", p=128)), writes=[bw])
                hb = [self.sb(s1, "a_hb%d" % s, [128, 8, 512], BF16) for s in range(2)]
                bhb = [P.buf() for _ in range(2)]
                cs = [self.sb(s1, "a_cos%d" % s, [128, 512]) for s in range(2)]
                sn = [self.sb(s1, "a_sin%d" % s, [128, 512]) for s in range(2)]
                bcs = [P.buf() for _ in range(2)]
                bsn = [P.buf() for _ in range(2)]
                sq = [self.sb(s1, "a_sq%d" % s, [128, 512], BF16) for s in range(2)]
                bsq = [P.buf() for _ in range(2)]
                rs = [self.sb(s1, "a_rs%d" % s, [128, 512]) for s in range(2)]
                brs = [P.buf() for _ in range(2)]
                aa = [self.sb(s1, "a_aa%d" % s, [128, 512]) for s in range(2)]
                baa = [P.buf() for _ in range(2)]
                ar = [self.sb(s1, "a_ar%d" % s, [128, 512]) for s in range(2)]
                bar = [P.buf() for _ in range(2)]
                qo = [self.sb(s1, "a_qo%d" % s, [128, 8, 512], BF16) for s in range(2)]
                bqo = [P.buf() for _ in range(2)]
                vst = [self.sb(s1, "a_vst%d" % s, [128, 256]) for s in range(2)]
                hc = 0
                for ti, (t0, T) in enumerate(tiles):
                    s = ti % 2
                    P.dma(lambda h, s=s, t0=t0, T=T: h.dma_start(out=hb[s][:, :, 0:T], in_=hbin[:, :, 1 + t0:1 + t0 + T]), writes=[bhb[s]])
                    P.dma(lambda h, s=s, t0=t0, T=T: h.dma_start(out=cs[s][:, 0:T], in_=cosd[:, t0:t0 + T]), writes=[bcs[s]])
                    P.dma(lambda h, s=s, t0=t0, T=T: h.dma_start(out=sn[s][:, 0:T], in_=sind[:, t0:t0 + T]), writes=[bsn[s]])
                    for hh in range(NH + NKV):
                        q = hc % 2
                        hc += 1
                        bq = (hc % 2) * 2
                        bs_ = bq + 1
                        gn = ("qg%d" % j) if hh < NH else ("kg%d" % j)
                        for k in range(8):
                            P.op("pe", lambda h, k=k, hh=hh, s=s, T=T, bq=bq: h.matmul(self.bank[bq][:, 0:T], w[:, k, hh * 128:(hh + 1) * 128], hb[s][:, k, 0:T], start=(k == 0), stop=(k == 7)),
                                 reads=[bw, bhb[s]], writes=[self.bbuf[bq]])
                        P.op("dve", lambda h, q=q, bq=bq, T=T: h.tensor_copy(out=aa[q][:, 0:T], in_=self.bank[bq][:, 0:T]), reads=[self.bbuf[bq]], writes=[baa[q]])
                        P.op("act", lambda h, q=q, T=T: h.activation(out=sq[q][:, 0:T], in_=aa[q][:, 0:T], func=AF.Square), reads=[baa[q]], writes=[bsq[q]])
                        P.op("pe", lambda h, q=q, bs_=bs_, T=T: h.matmul(self.bank[bs_][:, 0:T], self.onesH[:], sq[q][:, 0:T], start=True, stop=True),
                             reads=[bsq[q], self.bones], writes=[self.bbuf[bs_]])
                        P.op("dve", lambda h, q=q, bs_=bs_, T=T: h.tensor_scalar(rs[q][:, 0:T], self.bank[bs_][:, 0:T], float(RMS_EPS), None, ALU.add),
                             reads=[self.bbuf[bs_]], writes=[brs[q]])
                        P.op("act", lambda h, q=q, T=T: h.activation(out=rs[q][:, 0:T], in_=rs[q][:, 0:T], func=AF.Sqrt), reads=[brs[q]], writes=[brs[q]])
                        P.op("dve", lambda h, q=q, T=T: h.reciprocal(out=rs[q][:, 0:T], in_=rs[q][:, 0:T]), reads=[brs[q]], writes=[brs[q]])
                        P.op("dve", lambda h, q=q, bq=bq, T=T, gn=gn: h.scalar_tensor_tensor(out=aa[q][:, 0:T], in0=aa[q][:, 0:T], scalar=self.vcol(gn), in1=rs[q][:, 0:T], op0=ALU.mult, op1=ALU.mult),
                             reads=[baa[q], bsq[q], brs[q], self.bvecs], writes=[baa[q]])
                        P.op("pool", lambda h, q=q, T=T: h.tensor_copy(out=ar[q][0:64, 0:T], in_=aa[q][64:128, 0:T]), reads=[baa[q]], writes=[bar[q]])
                        P.op("pool", lambda h, q=q, T=T: h.tensor_copy(out=ar[q][64:128, 0:T], in_=aa[q][0:64, 0:T]), reads=[baa[q]], writes=[bar[q]])
                        P.op("pool", lambda h, q=q, s=s, T=T: h.tensor_tensor(out=ar[q][:, 0:T], in0=ar[q][:, 0:T], in1=sn[s][:, 0:T], op=ALU.mult), reads=[bar[q], bsn[s]], writes=[bar[q]])
                        P.op("dve", lambda h, q=q, s=s, T=T: h.tensor_tensor(out=aa[q][:, 0:T], in0=aa[q][:, 0:T], in1=cs[s][:, 0:T], op=ALU.mult), reads=[baa[q], bcs[s]], writes=[baa[q]])
                        if hh < NH:
                            P.op("dve", lambda h, q=q, s=s, hh=hh, T=T: h.tensor_tensor(out=qo[s][:, hh, 0:T], in0=aa[q][:, 0:T], in1=ar[q][:, 0:T], op=ALU.add),
                                 reads=[baa[q], bar[q]], writes=[bqo[s]])
                        else:
                            P.op("dve", lambda h, q=q, hh=hh, t0=t0, T=T: h.tensor_tensor(out=kT[:, hh - NH, t0:t0 + T], in0=aa[q][:, 0:T], in1=ar[q][:, 0:T], op=ALU.add),
                                 reads=[baa[q], bar[q]], writes=[bkT[hh - NH][ti]])
                    P.dma(lambda h, s=s, t0=t0, T=T: h.dma_start(out=qT[:, :, t0:t0 + T], in_=qo[s][:, :, 0:T]), reads=[bqo[s]])
                    for c in range((T + 127) // 128):
                        tp = min(128, T - c * 128)
                        kc = t0 // 128 + c
                        bv = 4 + (kc % 2)
                        for k in range(8):
                            P.op("pe", lambda h, k=k, s=s, c=c, tp=tp, bv=bv: h.matmul(self.bank[bv][0:tp, 0:256], hb[s][:, k, c * 128:c * 128 + tp], w[:, k, 1280:1536], start=(k == 0), stop=(k == 7)),
                                 reads=[bw, bhb[s]], writes=[self.bbuf[bv]])
                        P.op("act", lambda h, kc=kc, tp=tp, bv=bv: h.copy(out=vv[0:tp, kc, :], in_=self.bank[bv][0:tp, 0:256]), reads=[self.bbuf[bv]], writes=[bvv[kc]])
                P.barrier()
                P.release([bw] + bhb + bcs + bsn + bqo)
            with contextlib.ExitStack() as s2:
                NQ = 3
                qt = [self.sb(s2, "a_qt%d" % s, [128, 512], BF16) for s in range(NQ)]
                bqt = [P.buf() for _ in range(NQ)]
                NP = 4
                pt = [self.sb(s2, "a_pt%d" % s, [128, 512], BF16) for s in range(NP)]
                bpt = [P.buf() for _ in range(NP)]
                rd = [self.sb(s2, "a_rd%d" % s, [128, 512]) for s in range(2)]
                brd = [P.buf() for _ in range(2)]
                ot = [self.sb(s2, "a_ot%d" % s, [128, 512], BF16) for s in range(2)]
                bot = [P.buf() for _ in range(2)]
                scale = float(HD ** -0.5)
                it = 0
                pc = 0
                allk = [b for row in bkT for b in row]
                for ti, (t0, T) in enumerate(tiles):
                    for hh in range(NH):
                        kv = hh // 4
                        qs = it % NQ
                        o2 = it % 2
                        it += 1
                        bo = 4 + o2 * 2
                        bd = bo + 1
                        P.dma(lambda h, qs=qs, hh=hh, t0=t0, T=T: h.dma_start(out=qt[qs][:, 0:T], in_=qT[:, hh, t0:t0 + T]), writes=[bqt[qs]])
                        LOOK = 2
                        slots = {}
                        for kq in range(nkc + LOOK):
                            if kq < nkc:
                                kc = kq
                                kp = min(128, L - kc * 128)
                                sbk = pc % 4
                                ps = pc % NP
                                pc += 1
                                slots[kc] = ps
                                P.op("pe", lambda h, kc=kc, kp=kp, kv=kv, qs=qs, sbk=sbk, T=T: h.matmul(self.bank[sbk][0:kp, 0:T], kT[:, kv, kc * 128:kc * 128 + kp], qt[qs][:, 0:T], start=True, stop=True),
                                     reads=[bqt[qs], bkT[kv][kc // 4]], writes=[self.bbuf[sbk]])
                                P.op("act", lambda h, kp=kp, sbk=sbk, ps=ps, T=T: h.activation(out=pt[ps][0:kp, 0:T], in_=self.bank[sbk][0:kp, 0:T], func=AF.Exp, scale=scale),
                                     reads=[self.bbuf[sbk]], writes=[bpt[ps]])
                            if kq >= LOOK:
                                kc = kq - LOOK
                                kp = min(128, L - kc * 128)
                                ps = slots[kc]
                                P.op("pe", lambda h, kc=kc, kp=kp, kv=kv, ps=ps, bo=bo, T=T: h.matmul(self.bank[bo][:, 0:T], vv[0:kp, kc, kv * HD:(kv + 1) * HD], pt[ps][0:kp, 0:T], start=(kc == 0), stop=(kc == nkc - 1)),
                                     reads=[bpt[ps], bvv[kc]], writes=[self.bbuf[bo]])
                                P.op("pe", lambda h, kc=kc, kp=kp, ps=ps, bd=bd, T=T: h.matmul(self.bank[bd][:, 0:T], self.ones1[0:kp, :], pt[ps][0:kp, 0:T], start=(kc == 0), stop=(kc == nkc - 1)),
                                     reads=[bpt[ps], self.bones], writes=[self.bbuf[bd]])
                        P.op("dve", lambda h, o2=o2, bd=bd, T=T: h.reciprocal(out=rd[o2][:, 0:T], in_=self.bank[bd][:, 0:T]), reads=[self.bbuf[bd]], writes=[brd[o2]])
                        P.op("dve", lambda h, o2=o2, bo=bo, T=T: h.tensor_tensor(out=ot[o2][:, 0:T], in0=self.bank[bo][:, 0:T], in1=rd[o2][:, 0:T], op=ALU.mult),
                             reads=[self.bbuf[bo], brd[o2]], writes=[bot[o2]])
                        P.dma(lambda h, o2=o2, hh=hh, t0=t0, T=T: h.dma_start(out=xo[:, hh, t0:t0 + T], in_=ot[o2][:, 0:T]), reads=[bot[o2]])
                P.barrier()
                P.release(bqt + bot)
        self.phase_proj(i, g, "b_at_w_out", j)

    def load_const(self, st, name, shape, dt=F32):
        P = self.P
        t = self.sb(st, "c_" + name, shape)
        b = P.buf()
        P.dma(lambda h: h.dma_start(out=t[:], in_=self.dram[name][:, :]), writes=[b])
        if dt == F32:
            return t, b
        self.const_rel = getattr(self, "const_rel", []) + [b]
        t2 = self.sb(st, "cb_" + name, shape, dt)
        b2 = P.buf()
        P.op("dve", lambda h: h.tensor_copy(out=t2[:], in_=t[:]), reads=[b], writes=[b2])
        return t2, b2

    def fft_pass(self, g, st, C, src, nrows, mode, extra):
        P = self.P
        M, Hh, nk, nin = g.M, g.Hh, g.nk, g.nin
        CG = g.CG
        kf = self.dram["kf_" + g.name]
        yc = self.dram["yc_" + g.name]
        W2 = 2 * nk
        NS = 4
        G = D // CG
        xt = [self.sb(st, "ff_x%d" % s, [M, CG, M], FDT) for s in range(NS)]
        bxc = [P.buf() for _ in range(NS)]
        x32 = [self.sb(st, "ff_x32_%d" % s, [M, CG, M]) for s in range(NS)]
        bxt = [P.buf() for _ in range(NS)]
        bxt2 = [P.buf() for _ in range(NS)]
        B = [self.sb(st, "ff_B%d" % s, [M, CG, W2], FDT) for s in range(NS)]
        bB = [P.buf() for _ in range(NS)]
        t1 = [self.sb(st, "ff_t1%d" % s, [M, CG, W2], FDT) for s in range(NS)]
        bt1 = [P.buf() for _ in range(NS)]
        if mode == "data":
            for s in range(NS):
                P.op("pool", lambda h, s=s: h.memset(x32[s][:], 0.0), writes=[bxt[s], bxt2[s]])
            kt = [self.sb(st, "ff_k%d" % s, [M, CG, W2]) for s in range(NS)]
            bkt = [P.buf() for _ in range(NS)]
            Y = [self.sb(st, "ff_Y%d" % s, [M, CG, W2], FDT) for s in range(NS)]
            bY = [P.buf() for _ in range(NS)]
            Dd = [self.sb(st, "ff_D%d" % s, [nk, CG, 2 * M], FDT) for s in range(NS)]
            bD = [P.buf() for _ in range(NS)]
            t2 = [self.sb(st, "ff_t2%d" % s, [nk, CG, 2 * M], FDT) for s in range(NS)]
            bt2 = [P.buf() for _ in range(NS)]
            yo = [self.sb(st, "ff_yo%d" % s, [nin, CG, M]) for s in range(NS)]
            byo = [P.buf() for _ in range(NS)]
        else:
            sc = extra["sc"]
            bsc = extra["bsc"]
            ko = [self.sb(st, "ff_ko%d" % s, [M, CG, W2]) for s in range(NS)]
            bko = [P.buf() for _ in range(NS)]

        def bc(tab, p, lo, hi):
            a = tab[0:p, lo:hi]
            return a.unsqueeze(1).to_broadcast([p, CG, hi - lo])

        cpb = 512 // (2 * M)
        ib = [4, 5][:CG // cpb]
        assert CG // cpb <= 2 and CG * 2 * Hh <= 512 and CG * M <= 512

        def stage_a(gi):
            s = gi % NS
            c0 = gi * CG
            ba = gi % 2
            if mode == "data":
                P.dma(lambda h: h.dma_start(out=x32[s][0:nin - 1, :, :], in_=src[c0:c0 + CG, 0:(nin - 1) * M].rearrange("c (a b) -> a c b", b=M)), writes=[bxt[s]])
                P.dma(lambda h: h.dma_start(out=x32[s][nin - 1:nin, :, 0:g.tail], in_=src[c0:c0 + CG, (nin - 1) * M:g.L].unsqueeze(0)), writes=[bxt2[s]])
                P.dma(lambda h: h.dma_start(out=kt[s][:], in_=kf[gi].rearrange("m (c w) -> m c w", c=CG)), writes=[bkt[s]])
                P.op("act", lambda h: h.copy(out=xt[s][0:nrows, :, :], in_=x32[s][0:nrows, :, :]), reads=[bxt[s], bxt2[s]], writes=[bxc[s]])
                rdeps = [bxc[s]]
            else:
                P.dma(lambda h: h.dma_start(out=x32[s][:, :, :], in_=src[c0:c0 + CG, :].rearrange("c (a b) -> a c b", b=M)), writes=[bxt[s]])
                P.op("pool", lambda h: h.tensor_tensor(out=xt[s][:], in0=x32[s][:], in1=sc[0:M, c0:c0 + CG].unsqueeze(2).to_broadcast([M, CG, M]), op=ALU.mult),
                     reads=[bxt[s], bsc], writes=[bxc[s]])
                rdeps = [bxc[s]]
            for c in range(CG):
                P.op("pe", lambda h, c=c: h.matmul(self.bank[ba][0:M, c * 2 * Hh:(c + 1) * 2 * Hh], xt[s][0:nrows, c, :], C["f1"][0][0:nrows, :], start=True, stop=True),
                     reads=rdeps + [C["f1"][1]], writes=[self.bbuf[ba]])
            A = self.bank[ba][0:M, 0:CG * 2 * Hh].rearrange("p (c w) -> p c w", c=CG)
            Are = A[:, :, 0:nk]
            Aim = A[:, :, nk:2 * Hh]
            P.op("dve", lambda h: h.tensor_tensor(out=B[s][:, :, 0:nk], in0=Are, in1=bc(C["twc"][0], M, 0, nk), op=ALU.mult),
                 reads=[self.bbuf[ba], C["twc"][1]], writes=[bB[s]])
            P.op("dve", lambda h: h.tensor_tensor(out=B[s][:, :, nk:W2], in0=Are, in1=bc(C["ntws"][0], M, 0, nk), op=ALU.mult),
                 reads=[self.bbuf[ba], C["ntws"][1]], writes=[bB[s]])
            P.op("dve", lambda h: h.tensor_tensor(out=t1[s][:, :, 1:Hh], in0=Aim, in1=bc(C["tws"][0], M, 1, Hh), op=ALU.mult),
                 reads=[self.bbuf[ba], C["tws"][1]], writes=[bt1[s]])
            P.op("dve", lambda h: h.tensor_tensor(out=t1[s][:, :, nk + 1:nk + Hh], in0=Aim, in1=bc(C["twc"][0], M, 1, Hh), op=ALU.mult),
                 reads=[self.bbuf[ba], C["twc"][1]], writes=[bt1[s]])
            P.op("pool", lambda h: h.tensor_tensor(out=B[s][:, :, 1:Hh], in0=B[s][:, :, 1:Hh], in1=t1[s][:, :, 1:Hh], op=ALU.add), reads=[bB[s], bt1[s]], writes=[bB[s]])
            P.op("pool", lambda h: h.tensor_tensor(out=B[s][:, :, nk + 1:nk + Hh], in0=B[s][:, :, nk + 1:nk + Hh], in1=t1[s][:, :, nk + 1:nk + Hh], op=ALU.add), reads=[bB[s], bt1[s]], writes=[bB[s]])

        def stage_b(gi):
            s = gi % NS
            bre, bim = 2, 3
            Bre = B[s][:, :, 0:nk]
            Bim = B[s][:, :, nk:W2]
            Xre = self.bank[bre][0:M, 0:CG * nk].rearrange("p (c w) -> p c w", c=CG)
            Xim = self.bank[bim][0:M, 0:CG * nk].rearrange("p (c w) -> p c w", c=CG)
            P.op("pe", lambda h: h.matmul(Xre, C["c2"][0][:], Bre, start=True, stop=False), reads=[bB[s], C["c2"][1]], writes=[self.bbuf[bre]])
            P.op("pe", lambda h: h.matmul(Xre, C["s2"][0][:], Bim, start=False, stop=True), reads=[bB[s], C["s2"][1]], writes=[self.bbuf[bre]])
            P.op("pe", lambda h: h.matmul(Xim, C["c2"][0][:], Bim, start=True, stop=False), reads=[bB[s], C["c2"][1]], writes=[self.bbuf[bim]])
            P.op("pe", lambda h: h.matmul(Xim, C["ns2"][0][:], Bre, start=False, stop=True), reads=[bB[s], C["ns2"][1]], writes=[self.bbuf[bim]])
            if mode == "filter":
                P.op("act", lambda h: h.copy(out=ko[s][:, :, 0:nk], in_=Xre), reads=[self.bbuf[bre]], writes=[bko[s]])
                P.op("dve", lambda h: h.tensor_copy(out=ko[s][:, :, nk:W2], in_=Xim), reads=[self.bbuf[bim]], writes=[bko[s]])
                P.dma(lambda h: h.dma_start(out=kf[gi].rearrange("m (c w) -> m c w", c=CG), in_=ko[s][:]), reads=[bko[s]])
                return
            Kre = kt[s][:, :, 0:nk]
            Kim = kt[s][:, :, nk:W2]
            P.op("dve", lambda h: h.tensor_tensor(out=Y[s][:, :, 0:nk], in0=Xre, in1=Kre, op=ALU.mult), reads=[self.bbuf[bre], bkt[s]], writes=[bY[s]])
            P.op("dve", lambda h: h.tensor_tensor(out=Y[s][:, :, nk:W2], in0=Xre, in1=Kim, op=ALU.mult), reads=[self.bbuf[bre], bkt[s]], writes=[bY[s]])
            P.op("dve", lambda h: h.tensor_tensor(out=t1[s][:, :, 0:nk], in0=Xim, in1=Kim, op=ALU.mult), reads=[self.bbuf[bim], bkt[s], bB[s]], writes=[bt1[s]])
            P.op("dve", lambda h: h.tensor_tensor(out=t1[s][:, :, nk:W2], in0=Xim, in1=Kre, op=ALU.mult), reads=[self.bbuf[bim], bkt[s]], writes=[bt1[s]])
            P.op("pool", lambda h: h.tensor_tensor(out=Y[s][:, :, 0:nk], in0=Y[s][:, :, 0:nk], in1=t1[s][:, :, 0:nk], op=ALU.subtract), reads=[bY[s], bt1[s]], writes=[bY[s]])
            P.op("pool", lambda h: h.tensor_tensor(out=Y[s][:, :, nk:W2], in0=Y[s][:, :, nk:W2], in1=t1[s][:, :, nk:W2], op=ALU.add), reads=[bY[s], bt1[s]], writes=[bY[s]])

        def stage_c(gi):
            s = gi % NS
            for c in range(CG):
                bi = ib[c // cpb]
                o = self.bank[bi][0:nk, (c % cpb) * 2 * M:(c % cpb + 1) * 2 * M]
                P.op("pe", lambda h, c=c, o=o: h.matmul(o, Y[s][:, c, 0:nk], C["if2a"][0][:], start=True, stop=False), reads=[bY[s], C["if2a"][1]], writes=[self.bbuf[bi]])
                P.op("pe", lambda h, c=c, o=o: h.matmul(o, Y[s][:, c, nk:W2], C["if2b"][0][:], start=False, stop=True), reads=[bY[s], C["if2b"][1]], writes=[self.bbuf[bi]])
            for bi_i, bi in enumerate(ib):
                cs_ = slice(bi_i * cpb, (bi_i + 1) * cpb)
                Cc = self.bank[bi][0:nk, 0:cpb * 2 * M].rearrange("p (c w) -> p c w", c=cpb)
                Cre = Cc[:, :, 0:M]
                Cim = Cc[:, :, M:2 * M]

                def bc2(tab):
                    return tab[0:nk, 0:M].unsqueeze(1).to_broadcast([nk, cpb, M])
                P.op("dve", lambda h: h.tensor_tensor(out=Dd[s][:, cs_, 0:M], in0=Cre, in1=bc2(C["twct"][0]), op=ALU.mult), reads=[self.bbuf[bi], C["twct"][1]], writes=[bD[s]])
                P.op("dve", lambda h: h.tensor_tensor(out=Dd[s][:, cs_, M:2 * M], in0=Cre, in1=bc2(C["twst"][0]), op=ALU.mult), reads=[self.bbuf[bi], C["twst"][1]], writes=[bD[s]])
                P.op("dve", lambda h: h.tensor_tensor(out=t2[s][:, cs_, 0:M], in0=Cim, in1=bc2(C["ntwst"][0]), op=ALU.mult), reads=[self.bbuf[bi], C["ntwst"][1]], writes=[bt2[s]])
                P.op("dve", lambda h: h.tensor_tensor(out=t2[s][:, cs_, M:2 * M], in0=Cim, in1=bc2(C["twct"][0]), op=ALU.mult), reads=[self.bbuf[bi], C["twct"][1]], writes=[bt2[s]])
            P.op("pool", lambda h: h.tensor_tensor(out=Dd[s][:], in0=Dd[s][:], in1=t2[s][:], op=ALU.add), reads=[bD[s], bt2[s]], writes=[bD[s]])

        def stage_d(gi):
            s = gi % NS
            c0 = gi * CG
            yb = 6 + gi % 2
            yv = self.bank[yb][0:nin, 0:CG * M].rearrange("p (c w) -> p c w", c=CG)
            P.op("pe", lambda h: h.matmul(yv, C["gc"][0][:], Dd[s][:, :, 0:M], start=True, stop=False), reads=[bD[s], C["gc"][1]], writes=[self.bbuf[yb]])
            P.op("pe", lambda h: h.matmul(yv, C["gs"][0][:], Dd[s][:, :, M:2 * M], start=False, stop=True), reads=[bD[s], C["gs"][1]], writes=[self.bbuf[yb]])
            P.op("act", lambda h: h.copy(out=yo[s][:], in_=yv), reads=[self.bbuf[yb]], writes=[byo[s]])
            P.dma(lambda h: h.dma_start(out=yc[c0:c0 + CG, :].rearrange("c (a b) -> a c b", b=M), in_=yo[s][:]), reads=[byo[s]])

        nst = 4 if mode == "data" else 2
        P.store_q = "act"
        for t in range(G + nst - 1):
            if mode == "data":
                if 0 <= t - 3 < G:
                    stage_d(t - 3)
                if 0 <= t - 2 < G:
                    stage_c(t - 2)
            if 0 <= t - 1 < G:
                stage_b(t - 1)
            if t < G:
                stage_a(t)
        P.store_q = "pool"
        rel = bxt + bxt2
        if mode == "data":
            rel += bkt + byo
        else:
            rel += bko
        return rel

    def layer_hyena(self, i, g):
        P = self.P
        j = i // 2
        L, M, N, H = g.L, g.M, g.N, g.H
        hbin = self.dram["hb0_" + g.name].rearrange("(k p) t -> p k t", p=128)
        x0d = self.dram["x0_" + g.name].rearrange("(k p) t -> p k t", p=128)
        vgd = self.dram["vg_" + g.name].rearrange("(k p) t -> p k t", p=128)
        ycd = self.dram["yc_" + g.name].rearrange("(k p) t -> p k t", p=128)
        kund = self.dram["kun_" + g.name].rearrange("(k p) t -> p k t", p=128)
        rnd = self.dram["rn_" + g.name]
        zt = self.dram["zt_" + g.name]
        tb = self.dram["tb_" + g.name]
        with contextlib.ExitStack() as sl:
            coef = self.sb(sl, "h_coef", [128, 8, 2, 32])
            bcoef = P.buf()
            with contextlib.ExitStack() as st:
                w1 = self.sb(st, "hf_w1", [33, 64])
                w2 = self.sb(st, "hf_w2", [64, 64])
                w3 = self.sb(st, "hf_w3", [64, 2 * D])
                bw = P.buf()
                bw_2 = P.buf()
                bw_3 = P.buf()
                P.dma(lambda h: h.dma_start(out=w1[:], in_=self.dram["hy_filt_w1"][j]), writes=[bw])
                P.dma(lambda h: h.dma_start(out=w2[:], in_=self.dram["hy_filt_w2"][j]), writes=[bw_2])
                P.dma(lambda h: h.dma_start(out=w3[:], in_=self.dram["hy_filt_w3"][j]), writes=[bw_3])
                fbs = self.sb(st, "hf_fbs", [64, 2])
                bfbs = P.buf()
                for q, nm in enumerate(("fb1_%d" % j, "fb2_%d" % j)):
                    P.op("dve", lambda h, q=q, nm=nm: h.tensor_scalar(fbs[:, q:q + 1], self.vcol(nm, 0, 64), self.vcol("ffr_%d" % j, 0, 64), 0.0, ALU.mult, ALU.add),
                         reads=[self.bvecs], writes=[bfbs])
                asum = self.sb(st, "hf_asum", [128, 2, 8, 40])
                basum = P.buf()
                P.op("dve", lambda h: h.memset(asum[:], 0.0), writes=[basum])
                ev = self.sb(st, "hf_ev", [128, 8, 2, 64])
                bev = P.buf()
                zs = [self.sb(st, "hf_z%d" % s, [33, 512]) for s in range(2)]
                bzs = [P.buf() for _ in range(2)]
                ts = [self.sb(st, "hf_t%d" % s, [128, 512]) for s in range(2)]
                bts = [P.buf() for _ in range(2)]
                a1 = [self.sb(st, "hf_a1%d" % s, [64, 512]) for s in range(2)]
                ba1 = [P.buf() for _ in range(2)]
                wr = [self.sb(st, "hf_wr%d" % s, [64, 512]) for s in range(2)]
                bwr = [P.buf() for _ in range(2)]
                a2 = [self.sb(st, "hf_a2%d" % s, [64, 512]) for s in range(2)]
                ba2 = [P.buf() for _ in range(2)]
                win = [self.sb(st, "hf_win%d" % s, [128, 8, 512]) for s in range(2)]
                bwin = [P.buf() for _ in range(2)]
                ko = [self.sb(st, "hf_ko%d" % s, [128, 8, 512]) for s in range(2)]
                bko = [P.buf() for _ in range(2)]
                ftmp = [self.sb(st, "hf_ftmp%d" % s, [128, 512]) for s in range(2)]
                bftmp = [P.buf() for _ in range(2)]
                jobs = [(c0, 512, 0 if c0 < H else 1) for c0 in range(0, N, 512)] + [(N, 64, 2)]
                for ji, (c0, T, kind) in enumerate(jobs):
                    s = ji % 2
                    P.dma(lambda h, s=s, c0=c0, T=T: h.dma_start(out=zs[s][:, 0:T], in_=zt[:, c0:c0 + T]), writes=[bzs[s]])
                    P.dma(lambda h, s=s, c0=c0, T=T: h.dma_start(out=ts[s][:, 0:T], in_=tb[:, c0:c0 + T]), writes=[bts[s]])
                    bq = ji % 2
                    P.op("pe", lambda h, s=s, T=T, bq=bq: h.matmul(self.bank[bq][0:64, 0:T], w1[:], zs[s][:, 0:T], start=True, stop=True), reads=[bw, bzs[s]], writes=[self.bbuf[bq]])
                    P.op("dve", lambda h, s=s, T=T, bq=bq: h.tensor_scalar(a1[s][:, 0:T], self.bank[bq][0:64, 0:T], self.vcol("ffr_%d" % j, 0, 64), fbs[:, 0:1], ALU.mult, ALU.add),
                         reads=[self.bbuf[bq], bfbs, self.bvecs], writes=[ba1[s]])
                    for _w in range(2):
                        P.op("dve", lambda h, s=s, T=T: h.tensor_scalar(wr[s][:, 0:T], a1[s][:, 0:T], float(PI), float(-2 * PI), ALU.is_gt, ALU.mult), reads=[ba1[s]], writes=[bwr[s]])
                        P.op("dve", lambda h, s=s, T=T: h.tensor_tensor(out=a1[s][:, 0:T], in0=a1[s][:, 0:T], in1=wr[s][:, 0:T], op=ALU.add), reads=[ba1[s], bwr[s]], writes=[ba1[s]])
                        P.op("dve", lambda h, s=s, T=T: h.tensor_scalar(wr[s][:, 0:T], a1[s][:, 0:T], float(-PI), float(2 * PI), ALU.is_lt, ALU.mult), reads=[ba1[s]], writes=[bwr[s]])
                        P.op("dve", lambda h, s=s, T=T: h.tensor_tensor(out=a1[s][:, 0:T], in0=a1[s][:, 0:T], in1=wr[s][:, 0:T], op=ALU.add), reads=[ba1[s], bwr[s]], writes=[ba1[s]])
                    P.op("act", lambda h, s=s, T=T: h.activation(out=a1[s][:, 0:T], in_=a1[s][:, 0:T], func=AF.Sin), reads=[ba1[s]], writes=[ba1[s]])
                    P.op("pe", lambda h, s=s, T=T, bq=bq: h.matmul(self.bank[bq][0:64, 0:T], w2[:], a1[s][:, 0:T], start=True, stop=True), reads=[bw_2, ba1[s]], writes=[self.bbuf[bq]])
                    P.op("dve", lambda h, s=s, T=T, bq=bq: h.tensor_scalar(a2[s][:, 0:T], self.bank[bq][0:64, 0:T], self.vcol("ffr_%d" % j, 0, 64), fbs[:, 1:2], ALU.mult, ALU.add),
                         reads=[self.bbuf[bq], bfbs, self.bvecs], writes=[ba2[s]])
                    for _w in range(2):
                        P.op("dve", lambda h, s=s, T=T: h.tensor_scalar(wr[s][:, 0:T], a2[s][:, 0:T], float(PI), float(-2 * PI), ALU.is_gt, ALU.mult), reads=[ba2[s]], writes=[bwr[s]])
                        P.op("dve", lambda h, s=s, T=T: h.tensor_tensor(out=a2[s][:, 0:T], in0=a2[s][:, 0:T], in1=wr[s][:, 0:T], op=ALU.add), reads=[ba2[s], bwr[s]], writes=[ba2[s]])
                        P.op("dve", lambda h, s=s, T=T: h.tensor_scalar(wr[s][:, 0:T], a2[s][:, 0:T], float(-PI), float(2 * PI), ALU.is_lt, ALU.mult), reads=[ba2[s]], writes=[bwr[s]])
                        P.op("dve", lambda h, s=s, T=T: h.tensor_tensor(out=a2[s][:, 0:T], in0=a2[s][:, 0:T], in1=wr[s][:, 0:T], op=ALU.add), reads=[ba2[s], bwr[s]], writes=[ba2[s]])
                    P.op("act", lambda h, s=s, T=T: h.activation(out=a2[s][:, 0:T], in_=a2[s][:, 0:T], func=AF.Sin), reads=[ba2[s]], writes=[ba2[s]])
                    for k in range(8):
                        P.op("act", lambda h, s=s, k=k, T=T: h.activation(out=win[s][:, k, 0:T], in_=ts[s][:, 0:T], func=AF.Exp, scale=self.vcol("ndelta", k)), reads=[bts[s], self.bvecs], writes=[bwin[s]])
                    fbl = [0] if kind == 0 else ([1] if kind == 1 else [0, 1])
                    for fb in fbl:
                        for k in range(8):
                            bk = 2 + (k % 4)
                            ch = fb * 8 + k
                            P.op("pe", lambda h, s=s, ch=ch, bk=bk, T=T: h.matmul(self.bank[bk][:, 0:T], w3[:, ch * 128:(ch + 1) * 128], a2[s][:, 0:T], start=True, stop=True),
                                 reads=[bw_3, ba2[s]], writes=[self.bbuf[bk]])
                            if kind == 2:
                                dst = ev[:, k, fb, 0:T]
                                bdst = bev
                            else:
                                dst = ko[s][:, k, 0:T]
                                bdst = bko[s]
                            tq = (k + fb) % 2
                            P.op("dve", lambda h, ch=ch, bk=bk, T=T, tq=tq: h.tensor_scalar(ftmp[tq][:, 0:T], self.bank[bk][:, 0:T], self.vcol("fb3_%d" % j, ch), None, ALU.add),
                                 reads=[self.bbuf[bk], self.bvecs], writes=[bftmp[tq]])
                            P.op("dve", lambda h, s=s, k=k, T=T, dst=dst, tq=tq: h.scalar_tensor_tensor(out=dst, in0=win[s][:, k, 0:T], scalar=0.05, in1=ftmp[tq][:, 0:T], op0=ALU.add, op1=ALU.mult),
                                 reads=[bftmp[tq], bwin[s]], writes=[bdst])
                            if kind != 2:
                                P.op("dve", lambda h, s=s, k=k, fb=fb, T=T, dst=dst, ji=ji: h.tensor_reduce(out=asum[:, fb, k, ji:ji + 1], in_=dst, axis=AX.X, op=ALU.add, apply_absolute_value=True),
                                     reads=[bdst], writes=[basum])
                    if kind != 2:
                        P.dma(lambda h, s=s, c0=c0, T=T: h.dma_start(out=kund[:, :, c0:c0 + T], in_=ko[s][:, :, 0:T]), reads=[bko[s]], queue="sp")
                nrm = self.sb(st, "hf_nrm", [128, 2, 8])
                bnrm = P.buf()
                ex = self.sb(st, "hf_ex", [128, 2, 8])
                bex = P.buf()
                P.op("dve", lambda h: h.tensor_reduce(out=nrm[:], in_=asum[:], axis=AX.X, op=ALU.add), reads=[basum], writes=[bnrm])
                exb = self.sb(st, "hf_exb", [128, 8])
                bexb = P.buf()
                P.op("dve", lambda h: h.tensor_reduce(out=ex[:, 0, :], in_=ev[:, :, 0, 16:32], axis=AX.X, op=ALU.add, apply_absolute_value=True), reads=[bev], writes=[bex])
                P.op("dve", lambda h: h.tensor_reduce(out=ex[:, 1, :], in_=ev[:, :, 1, 0:32], axis=AX.X, op=ALU.add, apply_absolute_value=True), reads=[bev], writes=[bex])
                P.op("dve", lambda h: h.tensor_reduce(out=exb[:], in_=ev[:, :, 1, 1:17], axis=AX.X, op=ALU.add, apply_absolute_value=True), reads=[bev], writes=[bexb])
                P.op("dve", lambda h: h.tensor_tensor(out=ex[:, 1, :], in0=ex[:, 1, :], in1=exb[:], op=ALU.subtract), reads=[bex, bexb], writes=[bex])
                P.op("dve", lambda h: h.tensor_tensor(out=nrm[:], in0=nrm[:], in1=ex[:], op=ALU.add), reads=[bnrm, bex], writes=[bnrm])
                P.op("dve", lambda h: h.tensor_scalar(nrm[:], nrm[:], 1e-6, None, ALU.add), reads=[bnrm], writes=[bnrm])
                P.op("dve", lambda h: h.reciprocal(out=nrm[:], in_=nrm[:]), reads=[bnrm], writes=[bnrm])
                for k in range(8):
                    for fb in range(2):
                        P.op("dve", lambda h, k=k, fb=fb: h.tensor_scalar(ev[:, k, fb, :], ev[:, k, fb, :], nrm[:, fb, k:k + 1], None, ALU.mult), reads=[bev, bnrm], writes=[bev])
                    P.op("dve", lambda h, k=k: h.tensor_tensor(out=coef[:, k, 0, :], in0=ev[:, k, 0, 0:32], in1=ev[:, k, 1, 32:64], op=ALU.subtract), reads=[bev], writes=[bcoef])
                    P.op("dve", lambda h, k=k: h.tensor_tensor(out=coef[:, k, 1, :], in0=ev[:, k, 1, 0:32], in1=ev[:, k, 0, 32:64], op=ALU.subtract), reads=[bev], writes=[bcoef])
                P.dma(lambda h: h.dma_start(out=rnd.rearrange("f (k p) -> p f k", p=128), in_=nrm[:], allow_slow_non_contiguous=True), reads=[bnrm])
                P.barrier()
                P.release([bw, bw_2, bw_3] + bzs + bts + bko + [bnrm])
            with contextlib.ExitStack() as st:
                C = {}
                for k in ("f1", "c2", "s2", "ns2", "if2a", "if2b", "twc", "tws", "ntws", "twct", "twst", "ntwst", "gc", "gs"):
                    nm = "%s_%s" % (k, g.name)
                    shp = list(self.dram[nm].shape)
                    C[k] = self.load_const(st, nm, shp, FDT if k in ("f1", "c2", "s2", "ns2", "if2a", "if2b", "gc", "gs") else F32)
                crel = [v[1] for v in C.values()] + getattr(self, "const_rel", [])
                self.const_rel = []
                with contextlib.ExitStack() as s1:
                    sc = self.sb(s1, "hf_sc", [128, D])
                    bsc = P.buf()
                    bsc2 = P.buf()
                    P.dma(lambda h: h.dma_start(out=sc[0:M // 2, :], in_=rnd[0:1, :].to_broadcast([M // 2, D])), writes=[bsc])
                    P.dma(lambda h: h.dma_start(out=sc[M // 2:M, :], in_=rnd[1:2, :].to_broadcast([M // 2, D])), writes=[bsc2])
                    bscj = P.buf()
                    P.op("pool", lambda h: h.tensor_copy(out=sc[0:1, 0:1], in_=sc[0:1, 0:1]), reads=[bsc, bsc2], writes=[bscj])
                    rel = self.fft_pass(g, s1, C, self.dram["kun_" + g.name], M, "filter", {"sc": sc, "bsc": bscj})
                    P.barrier()
                    P.release(rel + [bsc, bsc2])
                if self.dbg.get("stop_h") == "filter":
                    P.release(crel)
                    return
                with contextlib.ExitStack() as s1:
                    wd = self.dram["b_hy_w_in"]
                    w = self.sb(s1, "h_w", [128, 8, 3 * D], BF16)
                    bw = P.buf()
                    P.dma(lambda h: h.dma_start(out=w[:], in_=wd[j].rearrange("(k p) d -> p k d", p=128)), writes=[bw])
                    hb = [self.sb(s1, "h_hb%d" % s, [128, 8, 512], BF16) for s in range(2)]
                    bhb = [P.buf() for _ in range(2)]
                    cv = [self.sb(s1, "h_cv%d" % s, [128, 512]) for s in range(3)]
                    bcv = [P.buf() for _ in range(3)]
                    x0o = [self.sb(s1, "h_x0o%d" % s, [128, 8, 512]) for s in range(2)]
                    bx0o = [P.buf() for _ in range(2)]
                    vgo = [self.sb(s1, "h_vgo%d" % s, [128, 8, 512]) for s in range(2)]
                    bvgo = [P.buf() for _ in range(2)]
                    tiles = tiles_even(L, 510)
                    bc_ = 0
                    for ti, (t0, T) in enumerate(tiles):
                        s = ti % 2
                        P.dma(lambda h, s=s, t0=t0, T=T: h.dma_start(out=hb[s][:, :, 0:T + 2], in_=hbin[:, :, t0:t0 + T + 2]), writes=[bhb[s]])
                        for k in range(8):
                            for part in (1, 2, 0):
                                n = part * 8 + k
                                bk = bc_ % 4
                                bc_ += 1
                                for kk in range(8):
                                    P.op("pe", lambda h, s=s, kk=kk, n=n, bk=bk, T=T: h.matmul(self.bank[bk][:, 0:T + 2], w[:, kk, n * 128:(n + 1) * 128], hb[s][:, kk, 0:T + 2], start=(kk == 0), stop=(kk == 7)),
                                         reads=[bw, bhb[s]], writes=[self.bbuf[bk]])
                                if part == 1:
                                    dst, bd_ = cv[0][:, 0:T], bcv[0]
                                elif part == 2:
                                    dst, bd_ = cv[1][:, 0:T], bcv[1]
                                else:
                                    dst, bd_ = x0o[s][:, k, 0:T], bx0o[s]
                                e = "dve"
                                P.op("act", lambda h, n=n, bk=bk, T=T, dst=dst: h.activation(out=dst, in_=self.bank[bk][:, 1:T + 1], func=AF.Identity, scale=self.vcol("hcw%d_1" % j, n), bias=self.vcol("hcb%d" % j, n)),
                                     reads=[self.bbuf[bk], self.bvecs], writes=[bd_])
                                P.op(e, lambda h, n=n, bk=bk, T=T, dst=dst: h.scalar_tensor_tensor(out=dst, in0=self.bank[bk][:, 0:T], scalar=self.vcol("hcw%d_0" % j, n), in1=dst, op0=ALU.mult, op1=ALU.add),
                                     reads=[self.bbuf[bk], bd_], writes=[bd_])
                                P.op(e, lambda h, n=n, bk=bk, T=T, dst=dst: h.scalar_tensor_tensor(out=dst, in0=self.bank[bk][:, 2:T + 2], scalar=self.vcol("hcw%d_2" % j, n), in1=dst, op0=ALU.mult, op1=ALU.add),
                                     reads=[self.bbuf[bk], bd_], writes=[bd_])
                                if part == 2:
                                    P.op("pool", lambda h, s=s, k=k, T=T: h.tensor_tensor(out=vgo[s][:, k, 0:T], in0=cv[0][:, 0:T], in1=cv[1][:, 0:T], op=ALU.mult), reads=[bcv[0], bcv[1]], writes=[bvgo[s]])
                        P.dma(lambda h, s=s, t0=t0, T=T: h.dma_start(out=x0d[:, :, t0:t0 + T], in_=x0o[s][:, :, 0:T]), reads=[bx0o[s]])
                        P.dma(lambda h, s=s, t0=t0, T=T: h.dma_start(out=vgd[:, :, t0:t0 + T], in_=vgo[s][:, :, 0:T]), reads=[bvgo[s]])
                    P.barrier()
                    P.release([bw] + bhb + bx0o + bvgo)
                if self.dbg.get("stop_h") == "inproj":
                    P.release(crel)
                    return
                with contextlib.ExitStack() as s1:
                    rel = self.fft_pass(g, s1, C, self.dram["vg_" + g.name], g.nin, "data", None)
                    P.barrier()
                    P.release(rel)
                P.release(crel)
            if self.dbg.get("stop_h") == "fft":
                return
            def prologue(what, arg):
                if what == "alloc":
                    st = arg
                    E = {}
                    E["ycs"] = [self.sb(st, "hp_yc%d" % s, [128, 8, 512]) for s in range(2)]
                    E["bycs"] = [P.buf() for _ in range(2)]
                    E["vgs"] = [self.sb(st, "hp_vg%d" % s, [128, 8, 512]) for s in range(2)]
                    E["bvgs"] = [P.buf() for _ in range(2)]
                    E["x0s"] = [self.sb(st, "hp_x0%d" % s, [128, 8, 512]) for s in range(2)]
                    E["bx0s"] = [P.buf() for _ in range(2)]
                    E["vh"] = self.sb(st, "hp_vh", [128, 8, 16])
                    E["vt"] = self.sb(st, "hp_vt", [128, 8, 16])
                    E["bvh"] = P.buf()
                    E["bvt"] = P.buf()
                    P.dma(lambda h: h.dma_start(out=E["vh"][:], in_=vgd[:, :, 0:16]), writes=[E["bvh"]])
                    P.dma(lambda h: h.dma_start(out=E["vt"][:], in_=vgd[:, :, L - 16:L]), writes=[E["bvt"]])
                    return E
                if what == "bufs":
                    E = arg
                    return E["bycs"] + E["bvgs"] + E["bx0s"] + [E["bvh"], E["bvt"]]
                ti, t0, T, xin, bxin, E = arg
                s = ti % 2
                yt, byt = E["ycs"][s], E["bycs"][s]
                P.dma(lambda h: h.dma_start(out=yt[:, :, 0:T], in_=ycd[:, :, t0:t0 + T]), writes=[byt])
                P.dma(lambda h: h.dma_start(out=E["vgs"][s][:, :, 0:T], in_=vgd[:, :, t0:t0 + T]), writes=[E["bvgs"][s]])
                P.dma(lambda h: h.dma_start(out=E["x0s"][s][:, :, 0:T], in_=x0d[:, :, t0:t0 + T]), writes=[E["bx0s"][s]])
                for k in range(8):
                    e = "dve"
                    if t0 == 0:
                        for m in range(H + 1, H + 16):
                            ee = m - H + 16
                            nt = L - m
                            so = m - (L - 16)
                            P.op(e, lambda h, k=k, ee=ee, nt=nt, so=so: h.scalar_tensor_tensor(out=yt[:, k, 0:nt], in0=E["vt"][:, k, so:so + nt], scalar=coef[:, k, 1, ee:ee + 1], in1=yt[:, k, 0:nt], op0=ALU.mult, op1=ALU.add),
                                 reads=[E["bvt"], bcoef, byt], writes=[byt])
                    if t0 + T == L and T == 16:
                        for d in range(H, H + 16):
                            ee = d - H + 16
                            nt = L - d
                            to = d - t0
                            P.op(e, lambda h, k=k, ee=ee, nt=nt, to=to: h.scalar_tensor_tensor(out=yt[:, k, to:to + nt], in0=E["vh"][:, k, 0:nt], scalar=coef[:, k, 0, ee:ee + 1], in1=yt[:, k, to:to + nt], op0=ALU.mult, op1=ALU.add),
                                 reads=[E["bvh"], bcoef, byt], writes=[byt])
                    P.op(e, lambda h, k=k: h.scalar_tensor_tensor(out=yt[:, k, 0:T], in0=E["vgs"][s][:, k, 0:T], scalar=self.vcol("skip%d" % j, k), in1=yt[:, k, 0:T], op0=ALU.mult, op1=ALU.add),
                         reads=[E["bvgs"][s], byt, self.bvecs], writes=[byt])
                    P.op("pool", lambda h, k=k: h.tensor_tensor(out=xin[:, k, 0:T], in0=yt[:, k, 0:T], in1=E["x0s"][s][:, k, 0:T], op=ALU.mult),
                         reads=[byt, E["bx0s"][s]], writes=[bxin])
            self.phase_proj(i, g, "b_hy_w_out", j, prologue=prologue)

    def phase_output(self, g):
        P = self.P
        hf = self.dram["hf0_" + g.name].rearrange("(k p) t -> p k t", p=128)
        y = self.dram["y_" + g.name]
        with contextlib.ExitStack() as st:
            xi = [self.sb(st, "o_x%d" % s, [128, 8, 512]) for s in range(2)]
            bxi = [P.buf() for _ in range(2)]
            yo = [self.sb(st, "o_y%d" % s, [128, 4, D]) for s in range(2)]
            byo = [P.buf() for _ in range(2)]
            for ti, t0 in enumerate(range(0, g.n, 512)):
                s = ti % 2
                P.dma(lambda h, s=s, t0=t0: h.dma_start(out=xi[s][:], in_=hf[:, :, NMETA + t0:NMETA + t0 + 512]), writes=[bxi[s]])
                for jj in range(4):
                    for k in range(8):
                        bk = (jj % 2) * 2 + k // 4
                        P.op("pe", lambda h, s=s, jj=jj, k=k, bk=bk: h.transpose(self.bank[bk][:, (k % 4) * 128:(k % 4 + 1) * 128], xi[s][:, k, jj * 128:(jj + 1) * 128], self.ident[:]),
                             reads=[bxi[s], self.bident], writes=[self.bbuf[bk]])
                        if k % 4 == 3:
                            e = "dve" if k == 3 else "act"
                            if e == "dve":
                                P.op("dve", lambda h, s=s, jj=jj, k=k, bk=bk: h.tensor_copy(out=yo[s][:, jj, (k // 4) * 512:(k // 4 + 1) * 512], in_=self.bank[bk][:]), reads=[self.bbuf[bk]], writes=[byo[s]])
                            else:
                                P.op("act", lambda h, s=s, jj=jj, k=k, bk=bk: h.copy(out=yo[s][:, jj, (k // 4) * 512:(k // 4 + 1) * 512], in_=self.bank[bk][:]), reads=[self.bbuf[bk]], writes=[byo[s]])
                P.dma(lambda h, s=s, t0=t0: h.dma_start(out=y[t0:t0 + 512, :].rearrange("(j p) d -> p j d", p=128), in_=yo[s][:]), reads=[byo[s]])
            P.barrier()
            P.release(bxi + byo)


GROUPS = None


def make_consts(groups):
    c = {}
    for g in groups:
        for k, v in fft_consts(g).items():
            c["%s_%s" % (k, g.name)] = v
        zt, tb = filter_grids(g)
        c["zt_" + g.name] = zt
        c["tb_" + g.name] = tb
        cs, sn = rope_tables(g)
        c["cos_" + g.name] = cs
        c["sin_" + g.name] = sn
    c["ident"] = np.eye(128, dtype=np.float32)
    return c


WNAMES = ("meta_tokens", "hy_w_in", "hy_w_out", "hy_filt_w1", "hy_filt_w2", "hy_filt_w3", "at_w_qkv", "at_w_out", "ffn_w_in", "ffn_w_out")


def kernel(**inputs):
    inp = {k: np.asarray(v) for k, v in inputs.items()}
    groups = [Group("p", inp["x_prompt"].shape[1]), Group("s", inp["x_sample"].shape[1])]
    vm, vecs = build_vecs(inp)
    consts = make_consts(groups)
    b = Builder(groups, vm)
    nc = b.build()
    ncore = 8
    in_maps = []
    for c in range(ncore):
        m = dict(consts)
        m["vecs"] = vecs
        for w in WNAMES:
            m[w] = np.ascontiguousarray(inp[w], dtype=np.float32)
        m["x_p"] = np.ascontiguousarray(inp["x_prompt"][c], dtype=np.float32)
        m["x_s"] = np.ascontiguousarray(inp["x_sample"][c], dtype=np.float32)
        in_maps.append(m)
    res = run_bass_kernel_spmd(nc, in_maps, core_ids=list(range(ncore)))
    yp = np.stack([np.asarray(r["y_p"]) for r in res.results], axis=0).astype(np.float32)
    ys = np.stack([np.asarray(r["y_s"]) for r in res.results], axis=0).astype(np.float32)
    return (yp, ys)
```

```python
import contextlib
import math
import os

import numpy as np
import concourse.bass as bass
import concourse.mybir as mybir
from concourse.bass_utils import run_bass_kernel_spmd

F32 = mybir.dt.float32
BF16 = mybir.dt.bfloat16
AF = mybir.ActivationFunctionType
ALU = mybir.AluOpType
AX = mybir.AxisListType

D = 1024
DFF = 2816
NFF = DFF // 128
DEPTH = 4
NMETA = 16
ALPHA = (2 * DEPTH) ** 0.25
LN_EPS = 1e-5
RMS_EPS = 1e-6
HD = 128
NH = 8
NKV = 2
PI = math.pi
FDT = BF16
CG = 4


class Stream:
    __slots__ = ("name", "sem", "ops", "step", "is_pe")

    def __init__(self, name, sem, step, is_pe=False):
        self.name, self.sem, self.ops, self.step, self.is_pe = name, sem, [], step, is_pe


class Op:
    __slots__ = ("cs", "fn", "deps", "needed", "count", "raw")

    def __init__(self, cs, fn, deps):
        self.cs, self.fn, self.needed, self.count = cs, fn, False, 0
        self.deps, self.raw = deps


class Buf:
    __slots__ = ("w", "r", "ld", "st")

    def __init__(self):
        self.w, self.r, self.ld, self.st = None, [], None, None


class _Rec:
    def __init__(self):
        self.call = None

    def __getattr__(self, name):
        def f(*a, **k):
            assert self.call is None, "op closure must make exactly one engine call"
            self.call = (name, a, k)
            return self
        return f


def _bind(fn):
    rec = _Rec()
    fn(rec)
    name, a, k = rec.call
    return lambda h: getattr(h, name)(*a, **k)


class Prog:
    ENGS = ("pe", "dve", "act", "pool", "sp")

    def __init__(self, nc, es, same_sync=True):
        self.nc, self.es, self.same_sync = nc, es, same_sync
        self.cs = {}
        self.issue = {e: [] for e in self.ENGS}
        for e in self.ENGS:
            self.cs[e] = Stream(e, es.enter_context(nc.semaphore("sem_" + e)), 1, e == "pe")
        self.dma_streams = []
        self.free_streams = []
        self.rr = 0
        self.store_q = "pool"

    def buf(self):
        return Buf()

    def release(self, bufs):
        for b in bufs:
            for s in (b.ld, b.st):
                if s is not None:
                    self.free_streams.append(s)
            b.ld = b.st = None

    def _dstream(self):
        if self.free_streams:
            return self.free_streams.pop()
        s = Stream("d%d" % len(self.dma_streams),
                   self.es.enter_context(self.nc.semaphore("dsem%d" % len(self.dma_streams))), 16)
        self.dma_streams.append(s)
        return s

    @staticmethod
    def _deps(reads, writes):
        deps = []
        raw = set()
        for b in reads:
            if b.w is not None:
                deps.append(b.w)
                raw.add(id(b.w))
        for b in writes:
            if b.w is not None:
                deps.append(b.w)
            deps.extend(b.r)
        return deps, raw

    @staticmethod
    def _commit(op, reads, writes):
        for b in reads:
            b.r.append(op)
        for b in writes:
            b.w = op
            b.r = []

    def op(self, eng, fn, reads=(), writes=()):
        cs = self.cs[eng]
        o = Op(cs, _bind(fn), self._deps(reads, writes))
        cs.ops.append(o)
        self.issue[eng].append(o)
        self._commit(o, reads, writes)
        return o

    def dma(self, fn, reads=(), writes=(), queue=None):
        if queue is None:
            queue = "sp" if writes else self.store_q
        if writes:
            track = writes[0]
            if track.ld is None:
                track.ld = self._dstream()
            cs = track.ld
        else:
            track = reads[0]
            if track.st is None:
                track.st = self._dstream()
            cs = track.st
        o = Op(cs, _bind(fn), self._deps(reads, writes))
        o.needed = True
        cs.ops.append(o)
        self.issue[queue].append(o)
        self._commit(o, reads, writes)
        return o

    def barrier(self):
        lasts = []
        for s in list(self.cs.values()) + self.dma_streams:
            if s.ops:
                lasts.append(s.ops[-1])
        for e in self.ENGS:
            o = Op(self.cs[e], None, (list(lasts), set(id(x) for x in lasts)))
            self.cs[e].ops.append(o)
            self.issue[e].append(o)

    def finalize_counts(self):
        for e in self.ENGS:
            own = self.cs[e]
            for o in self.issue[e]:
                for d in o.deps:
                    if d.cs is own and (own.is_pe or not self.same_sync or id(d) not in o.raw):
                        continue
                    d.needed = True
        for s in list(self.cs.values()) + self.dma_streams:
            c = 0
            for o in s.ops:
                if o.needed and o.fn is not None:
                    c += s.step
                o.count = c

    def emit(self, eng, h):
        own = self.cs[eng]
        seen = {}
        for o in self.issue[eng]:
            need = {}
            for d in o.deps:
                if d.cs is own and (own.is_pe or not self.same_sync or id(d) not in o.raw):
                    continue
                if d.count == 0 or seen.get(d.cs.name, 0) >= d.count:
                    continue
                if need.get(d.cs.name, (None, 0))[1] < d.count:
                    need[d.cs.name] = (d.cs, d.count)
            for nm, (s, c) in need.items():
                h.wait_ge(s.sem, c)
                seen[nm] = c
            if o.fn is None:
                continue
            ins = o.fn(h)
            if o.needed:
                ins.then_inc(o.cs.sem, o.cs.step)

    def run_block(self):
        self.barrier()
        self.finalize_counts()
        with self.nc.Block() as block:
            @block.tensor
            def _(h):
                self.emit("pe", h)

            @block.vector
            def _(h):
                self.emit("dve", h)

            @block.scalar
            def _(h):
                self.emit("act", h)

            @block.gpsimd
            def _(h):
                self.emit("pool", h)

            @block.sync
            def _(h):
                self.emit("sp", h)


def cols(v):
    v = np.asarray(v, np.float32)
    return np.ascontiguousarray(v.reshape(-1, 128).T)


class Group:
    def __init__(self, name, n_tok):
        self.name = name
        self.n = n_tok
        self.L = n_tok + NMETA
        self.M = 128 if n_tok > 4096 else 64
        self.N = self.M * self.M
        self.H = self.N // 2
        self.Hh = self.M // 2
        self.nk = self.Hh + 1
        self.nin = self.L // self.M + 1
        self.tail = self.L - (self.nin - 1) * self.M
        self.Lp = self.nin * self.M
        self.CG = 4 if self.M == 128 else 8
        assert self.L - self.H == 16 and self.tail == 16


def fft_consts(g):
    M, N, Hh, nk, nin = g.M, g.N, g.Hh, g.nk, g.nin
    c = {}
    n1 = np.arange(M)[:, None].astype(np.float64)
    k1 = np.arange(nk)[None, :].astype(np.float64)
    th = 2 * np.pi * n1 * k1 / M
    f1 = np.zeros((M, 2 * Hh))
    f1[:, :nk] = np.cos(th)
    f1[:, nk:] = -np.sin(th[:, 1:Hh])
    c["f1"] = f1
    n2 = np.arange(M)[:, None].astype(np.float64)
    k2 = np.arange(M)[None, :].astype(np.float64)
    th2 = 2 * np.pi * n2 * k2 / M
    c["c2"] = np.cos(th2)
    c["s2"] = np.sin(th2)
    c["ns2"] = -np.sin(th2)
    c["if2a"] = np.concatenate([np.cos(th2), np.sin(th2)], axis=1)
    c["if2b"] = np.concatenate([-np.sin(th2), np.cos(th2)], axis=1)
    tht = 2 * np.pi * n2 * k1 / N
    c["twc"] = np.cos(tht)
    c["tws"] = np.sin(tht)
    c["ntws"] = -np.sin(tht)
    c["twct"] = np.cos(tht).T.copy()
    c["twst"] = np.sin(tht).T.copy()
    c["ntwst"] = -np.sin(tht).T.copy()
    wk = np.full((nk, 1), 2.0)
    wk[0] = 1.0
    wk[Hh] = 1.0
    k1c = np.arange(nk)[:, None].astype(np.float64)
    n1r = np.arange(nin)[None, :].astype(np.float64)
    thg = 2 * np.pi * k1c * n1r / M
    c["gc"] = wk * np.cos(thg) / N
    c["gs"] = -wk * np.sin(thg) / N
    return {k: np.ascontiguousarray(v, dtype=np.float32) for k, v in c.items()}


def filter_grids(g):
    N, H, L = g.N, g.H, g.L
    j = np.arange(N)
    pos_c = np.where(j < H, j, N - j).astype(np.float64)
    e_pos = np.concatenate([[0], np.arange(H - 15, H + 16)]).astype(np.float64)
    e2 = np.concatenate([[0], e_pos[1:][::-1]])
    pos_e = np.concatenate([e_pos, e2])
    pos = np.concatenate([pos_c, pos_e])
    t = pos / (L - 1)
    w = (2.0 * np.pi / L) * pos
    bands = np.linspace(1e-4, 15, 16)
    z = np.concatenate([t[:, None], np.cos(w[:, None] * bands), -np.sin(w[:, None] * bands)], axis=1)
    zt = np.ascontiguousarray(z.T, dtype=np.float32)
    tb = np.ascontiguousarray(np.broadcast_to(t[None, :].astype(np.float32), (128, N + 64)))
    return zt, tb


def rope_tables(g):
    n = g.n
    rows = n // 64
    row = np.concatenate([np.full((NMETA,), -1.0), np.repeat(np.arange(rows), 64)]).astype(np.float32)
    col = np.concatenate([np.arange(NMETA), np.tile(np.arange(64), rows)]).astype(np.float32)
    inv = (np.float32(10000.0) ** (-np.arange(0, 64, 2, dtype=np.float32) / np.float32(64))).astype(np.float32)
    ang = np.concatenate([row[:, None] * inv, col[:, None] * inv], axis=1)
    cs = np.cos(ang).T
    sn = np.sin(ang).T
    cos_t = np.concatenate([cs, cs], axis=0)
    sin_t = np.concatenate([-sn, sn], axis=0)
    return np.ascontiguousarray(cos_t, np.float32), np.ascontiguousarray(sin_t, np.float32)


def tiles_even(L, tmax):
    n = -(-L // tmax)
    T = -(-L // n)
    out = []
    t0 = 0
    while t0 < L:
        out.append((t0, min(T, L - t0)))
        t0 += T
    return out


class VecMap:
    def __init__(self):
        self.off = {}
        self.n = 0
        self.parts = []

    def add(self, name, arr):
        a = np.asarray(arr, np.float32)
        if a.shape[0] != 128:
            a = np.concatenate([a, np.zeros((128 - a.shape[0], a.shape[1]), np.float32)], axis=0)
        self.off[name] = self.n
        self.n += a.shape[1]
        self.parts.append(a)

    def build(self):
        return np.ascontiguousarray(np.concatenate(self.parts, axis=1))


def build_vecs(inp):
    vm = VecMap()
    for i in range(DEPTH):
        vm.add("ln1g%d" % i, cols(inp["ln1_g"][i]))
        vm.add("ln1b%d" % i, cols(inp["ln1_b"][i]))
        vm.add("ln2g%d" % i, cols(inp["ln2_g"][i]))
        vm.add("ln2b%d" % i, cols(inp["ln2_b"][i]))
        for tap in range(3):
            vm.add("fcw%d_%d" % (i, tap), cols(inp["ffn_conv_w"][i, tap]))
        vm.add("fcb%d" % i, cols(inp["ffn_conv_b"][i]))
    for j in range(2):
        for tap in range(3):
            vm.add("hcw%d_%d" % (j, tap), cols(inp["hy_conv_w"][j, tap]))
        vm.add("hcb%d" % j, cols(inp["hy_conv_b"][j]))
        vm.add("skip%d" % j, cols(inp["hy_skip"][j]))
        vm.add("fb3_%d" % j, cols(inp["hy_filt_b3"][j]))
        vm.add("fb1_%d" % j, np.asarray(inp["hy_filt_b1"][j], np.float32)[:, None])
        vm.add("fb2_%d" % j, np.asarray(inp["hy_filt_b2"][j], np.float32)[:, None])
        vm.add("ffr_%d" % j, np.asarray(inp["hy_filt_freq"][j], np.float32)[:, None])
        qg = np.asarray(inp["at_q_gain"][j], np.float32)
        kg = np.asarray(inp["at_k_gain"][j], np.float32)
        vm.add("qg%d" % j, qg[:, None])
        vm.add("kg%d" % j, kg[:, None])
    max_decay = math.log(1.0 / 1e-2) / 0.3
    min_decay = math.log(1.0 / 1e-2) / 1.5
    deltas = np.linspace(min_decay, max_decay, D, dtype=np.float32)
    vm.add("ndelta", cols(-deltas))
    return vm, vm.build()


class Builder:
    def __init__(self, groups, vm, dbg=None):
        self.groups = groups
        self.vm = vm
        self.dbg = dbg or {}
        self.nc = bass.Bass("TRN2", target_bir_lowering=False)
        self.es = contextlib.ExitStack()
        self.P = None
        self.dram = {}

    def din(self, name, shape, dt=F32):
        t = self.nc.dram_tensor(name, list(shape), dt, kind="ExternalInput").ap()
        self.dram[name] = t
        return t

    def dout(self, name, shape, dt=F32):
        t = self.nc.dram_tensor(name, list(shape), dt, kind="ExternalOutput").ap()
        self.dram[name] = t
        return t

    def dscr(self, name, shape, dt=F32):
        kind = "ExternalOutput" if name in self.dbg.get("outs", ()) else "Internal"
        t = self.nc.dram_tensor(name, list(shape), dt, kind=kind).ap()
        self.dram[name] = t
        return t

    def sb(self, st, name, shape, dt=F32):
        self.uid = getattr(self, "uid", 0) + 1
        return st.enter_context(self.nc.sbuf_tensor("sb%d_%s" % (self.uid, name), list(shape), dt))

    def vcol(self, name, k=0, p=128):
        o = self.vm.off[name] + k
        return self.vecs[0:p, o:o + 1]

    def build(self):
        nc, es = self.nc, self.es
        with es:
            self.P = P = Prog(nc, es, same_sync=bool(self.dbg.get("same_sync", True)))
            self.declare_io()
            self.bank = [es.enter_context(nc.psum_tensor("bank%d" % i, [128, 512], F32)) for i in range(8)]
            self.bbuf = [P.buf() for _ in range(8)]
            self.vecs = self.sb(es, "vecs", [128, self.vm.n])
            self.bvecs = P.buf()
            P.dma(lambda h: h.dma_start(out=self.vecs[:], in_=self.dram["vecs"][:, :]), writes=[self.bvecs])
            self.ident = self.sb(es, "ident", [128, 128])
            self.bident = P.buf()
            P.dma(lambda h: h.dma_start(out=self.ident[:], in_=self.dram["ident"][:, :]), writes=[self.bident])
            self.onesD = self.sb(es, "onesD", [128, 128], BF16)
            self.onesH = self.sb(es, "onesH", [128, 128], BF16)
            self.ones1 = self.sb(es, "ones1", [128, 128], BF16)
            self.bones = P.buf()
            P.op("dve", lambda h: h.memset(self.onesD[:], 1.0 / D), writes=[self.bones])
            P.op("dve", lambda h: h.memset(self.onesH[:], 1.0 / HD), writes=[self.bones])
            P.op("dve", lambda h: h.memset(self.ones1[:], 1.0), writes=[self.bones])
            self.zero = self.sb(es, "zero", [128, 8], BF16)
            self.bzero = P.buf()
            P.op("dve", lambda h: h.memset(self.zero[:], 0.0), writes=[self.bzero])
            stop = self.dbg.get("stop", "end")
            self.phase_prep_weights()
            for g in self.groups:
                if stop != "setup":
                    self.phase_input(g)
            P.barrier()
            done = (stop in ("input", "setup"))
            for i in range(DEPTH):
                if done:
                    break
                for g in self.groups:
                    if i % 2 == 0:
                        self.layer_hyena(i, g)
                    else:
                        self.layer_attn(i, g)
                    if stop == "mix%d" % i:
                        continue
                    self.phase_ffn(i, g)
                if stop in ("mix%d" % i, "ffn%d" % i):
                    done = True
            if not done:
                for g in self.groups:
                    self.phase_output(g)
            P.run_block()
        return nc

    def declare_io(self):
        for g in self.groups:
            self.din("x_" + g.name, [g.n, D])
            self.dout("y_" + g.name, [g.n, D])
            self.din("cos_" + g.name, [128, g.L])
            self.din("sin_" + g.name, [128, g.L])
            self.din("zt_" + g.name, [33, g.N + 64])
            self.din("tb_" + g.name, [128, g.N + 64])
            for k in ("f1", "c2", "s2", "ns2", "if2a", "if2b", "twc", "tws", "ntws", "twct", "twst", "ntwst", "gc", "gs"):
                shp = {"f1": [g.M, 2 * g.Hh], "c2": [g.M, g.M], "s2": [g.M, g.M], "ns2": [g.M, g.M],
                       "if2a": [g.M, 2 * g.M], "if2b": [g.M, 2 * g.M], "twc": [g.M, g.nk], "tws": [g.M, g.nk],
                       "ntws": [g.M, g.nk], "twct": [g.nk, g.M], "twst": [g.nk, g.M], "ntwst": [g.nk, g.M],
                       "gc": [g.nk, g.nin], "gs": [g.nk, g.nin]}[k]
                self.din("%s_%s" % (k, g.name), shp)
            for pp in range(2):
                self.dscr("hf%d_%s" % (pp, g.name), [D, g.L])
                self.dscr("hb%d_%s" % (pp, g.name), [D, g.L + 2], BF16)
            self.dscr("x0_" + g.name, [D, g.L])
            self.dscr("vg_" + g.name, [D, g.L])
            self.dscr("yc_" + g.name, [D, g.Lp])
            self.dscr("kun_" + g.name, [D, g.N])
            self.dscr("kf_" + g.name, [D // g.CG, g.M, g.CG * 2 * g.nk])
            self.dscr("rn_" + g.name, [2, D])
            self.dscr("qT_" + g.name, [D, g.L], BF16)
            self.dscr("xo_" + g.name, [D, g.L], BF16)
        self.din("vecs", [128, self.vm.n])
        self.din("ident", [128, 128])
        self.din("meta_tokens", [NMETA, D])
        self.din("hy_w_in", [2, D, 3 * D])
        self.din("hy_w_out", [2, D, D])
        self.din("hy_filt_w1", [2, 33, 64])
        self.din("hy_filt_w2", [2, 64, 64])
        self.din("hy_filt_w3", [2, 64, 2 * D])
        self.din("at_w_qkv", [2, D, 1536])
        self.din("at_w_out", [2, D, D])
        self.din("ffn_w_in", [DEPTH, D, 2 * DFF])
        self.din("ffn_w_out", [DEPTH, DFF, D])
        self.dscr("b_hy_w_in", [2, D, 3 * D], BF16)
        self.dscr("b_hy_w_out", [2, D, D], BF16)
        self.dscr("b_at_w_qkv", [2, D, 1536], BF16)
        self.dscr("b_at_w_out", [2, D, D], BF16)
        self.dscr("b_ffn_w1", [DEPTH, NFF, 128, 2, 8, 128], BF16)
        self.dscr("b_ffn_w_out", [DEPTH, DFF, D], BF16)

    def phase_prep_weights(self):
        P, nc = self.P, self.nc
        if self.dbg.get("skip_prep"):
            return
        with contextlib.ExitStack() as st:
            NS = 4
            CW = 3072
            src = [self.sb(st, "pw_s%d" % i, [128, CW]) for i in range(NS)]
            dst = [self.sb(st, "pw_d%d" % i, [128, CW], BF16) for i in range(NS)]
            bs = [P.buf() for _ in range(NS)]
            bd = [P.buf() for _ in range(NS)]
            cnt = [0]
            engs = ("dve", "act")

            def job(src_ap, ncols, dst_ap_fn):
                i = cnt[0] % NS
                e = engs[cnt[0] % 2]
                cnt[0] += 1
                P.dma(lambda h: h.dma_start(out=src[i][:, 0:ncols], in_=src_ap), writes=[bs[i]])
                if e == "act":
                    P.op("act", lambda h: h.copy(out=dst[i][:, 0:ncols], in_=src[i][:, 0:ncols]), reads=[bs[i]], writes=[bd[i]])
                else:
                    P.op(e, lambda h: h.tensor_copy(out=dst[i][:, 0:ncols], in_=src[i][:, 0:ncols]), reads=[bs[i]], writes=[bd[i]])
                o, s = dst_ap_fn(dst[i])
                P.dma(lambda h: h.dma_start(out=o, in_=s), reads=[bd[i]])

            def plain(name, lead, K, Ncol):
                s = self.dram[name]
                d = self.dram["b_" + name]
                for l in range(lead):
                    for k in range(K // 128):
                        for c0 in range(0, Ncol, CW):
                            w = min(CW, Ncol - c0)
                            job(s[l, k * 128:(k + 1) * 128, c0:c0 + w], w,
                                (lambda l=l, k=k, c0=c0, w=w: (lambda t: (d[l, k * 128:(k + 1) * 128, c0:c0 + w], t[:, 0:w])))())
            plain("hy_w_in", 2, D, 3 * D)
            plain("hy_w_out", 2, D, D)
            plain("at_w_qkv", 2, D, 1536)
            plain("at_w_out", 2, D, D)
            plain("ffn_w_out", DEPTH, DFF, D)
            s = self.dram["ffn_w_in"]
            d = self.dram["b_ffn_w1"]
            for l in range(DEPTH):
                for k in range(8):
                    for half in range(2):
                        def mk(l=l, k=k, half=half):
                            def f(t):
                                o = d[l, :, :, half, k, :].rearrange("n p c -> p n c")
                                return o, t[:, 0:DFF].rearrange("p (n c) -> p n c", c=128)
                            return f
                        job(s[l, k * 128:(k + 1) * 128, half * DFF:(half + 1) * DFF], DFF, mk())
            P.barrier()
            P.release(bs + bd)

    def phase_input(self, g):
        P = self.P
        hf = self.dram["hf0_" + g.name]
        hb = self.dram["hb0_" + g.name]
        x = self.dram["x_" + g.name]
        with contextlib.ExitStack() as st:
            xt = [self.sb(st, "in_x%d" % i, [128, 4, D]) for i in range(2)]
            bx = [P.buf() for _ in range(2)]
            of = [self.sb(st, "in_of%d" % i, [128, 8, 512]) for i in range(2)]
            ob = [self.sb(st, "in_ob%d" % i, [128, 8, 512], BF16) for i in range(2)]
            bof = [P.buf() for _ in range(2)]
            bob = [P.buf() for _ in range(2)]
            for pp in range(0 if not self.dbg.get("no_halo") else 2, 2):
                hbp = self.dram["hb%d_%s" % (pp, g.name)].rearrange("(k p) t -> p k t", p=128)
                for c in (0, g.L + 1):
                    P.dma(lambda h, hbp=hbp, c=c: h.dma_start(out=hbp[:, :, c:c + 1], in_=self.zero[:, 0:8].rearrange("p (k o) -> p k o", o=1), allow_slow_non_contiguous=True),
                          reads=[self.bzero])
            jobs = [("meta", 0, NMETA)] + [("x", t0, 512) for t0 in range(0, g.n, 512)]
            if self.dbg.get("in_nometa"):
                jobs = jobs[1:]
            for ji, (kind, t0, T) in enumerate(jobs):
                s = ji % 2
                nch = (T + 127) // 128
                if kind == "meta":
                    P.dma(lambda h, s=s: h.dma_start(out=xt[s][0:NMETA, 0, :], in_=self.dram["meta_tokens"][:, :]), writes=[bx[s]])
                    dcol = 0
                else:
                    P.dma(lambda h, s=s, t0=t0: h.dma_start(out=xt[s][:, :, :], in_=x[t0:t0 + 512, :].rearrange("(j p) d -> p j d", p=128)), writes=[bx[s]])
                    dcol = NMETA + t0
                for k in range(8):
                    bk = k % 2
                    for j in range(nch):
                        tp = min(128, T - j * 128)
                        P.op("pe", lambda h, s=s, k=k, j=j, tp=tp, bk=bk: h.transpose(self.bank[bk][:, j * 128:j * 128 + tp], xt[s][0:tp, j, k * 128:(k + 1) * 128], self.ident[0:tp, 0:tp]),
                             reads=[bx[s], self.bident], writes=[self.bbuf[bk]])
                    P.op("dve", lambda h, s=s, k=k, bk=bk, T=T: h.tensor_copy(out=of[s][:, k, 0:T], in_=self.bank[bk][:, 0:T]), reads=[self.bbuf[bk]], writes=[bof[s]])
                    P.op("act", lambda h, s=s, k=k, T=T: h.copy(out=ob[s][:, k, 0:T], in_=of[s][:, k, 0:T]), reads=[bof[s]], writes=[bob[s]])
                if not self.dbg.get("in_nohf"):
                    P.dma(lambda h, s=s, T=T, dcol=dcol: h.dma_start(out=hf.rearrange("(k p) t -> p k t", p=128)[:, :, dcol:dcol + T], in_=of[s][:, :, 0:T]), reads=[bof[s]])
                if not self.dbg.get("in_nohb"):
                    P.dma(lambda h, s=s, T=T, dcol=dcol: h.dma_start(out=hb.rearrange("(k p) t -> p k t", p=128)[:, :, 1 + dcol:1 + dcol + T], in_=ob[s][:, :, 0:T]), reads=[bob[s]])
            P.barrier()
            P.release(bx + bof + bob)

    def res_ln(self, st_bufs, i, which, g, t0, T, out_pp, acc_fn, defer=False):
        P = self.P
        S = st_bufs
        hres, bhres = S["hres2"][S["hs"]], S["bhres2"][S["hs"]]
        rs_ = S["rs"] % len(S["r_l"])
        r, br = S["r_l"][rs_], S["br_l"][rs_]
        rb, brb = S["rb"], S["brb"]
        r2, br2 = S["r2"], S["br2"]
        ob, bob = S["ob_l"][rs_], S["bob_l"][rs_]
        mean, bmean = S["mean"], S["bmean"]
        rstd, brstd = S["rstd"], S["brstd"]
        gname = "ln%dg%d" % (which, i)
        bname = "ln%db%d" % (which, i)
        for k in range(8):
            bk = S["obanks"][k % len(S["obanks"])]
            acc_fn(k, bk)
            P.op("dve", lambda h, k=k, bk=bk: h.scalar_tensor_tensor(out=r[:, k, 0:T], in0=hres[:, k, 0:T], scalar=float(ALPHA), in1=self.bank[bk][:, 0:T], op0=ALU.mult, op1=ALU.add),
                 reads=[bhres, self.bbuf[bk]], writes=[br[k]])
            P.op("act", lambda h, k=k: h.copy(out=rb[:, k, 0:T], in_=r[:, k, 0:T]), reads=[br[k]], writes=[brb[k]])
            P.op("act", lambda h, k=k: h.activation(out=r2[:, k, 0:T], in_=r[:, k, 0:T], func=AF.Square), reads=[br[k]], writes=[br2[k]])
        bm, be = S["sbanks"]
        hf = self.dram["hf%d_%s" % (out_pp, g.name)].rearrange("(k p) t -> p k t", p=128)
        hb = self.dram["hb%d_%s" % (out_pp, g.name)].rearrange("(k p) t -> p k t", p=128)

        def part_b():
            for k in range(8):
                P.op("pe", lambda h, k=k: h.matmul(self.bank[bm][:, 0:T], self.onesD[:], rb[:, k, 0:T], start=(k == 0), stop=(k == 7)),
                     reads=[brb[k], self.bones], writes=[self.bbuf[bm]])
            for k in range(8):
                P.op("pe", lambda h, k=k: h.matmul(self.bank[be][:, 0:T], self.onesD[:], r2[:, k, 0:T], start=(k == 0), stop=(k == 7)),
                     reads=[br2[k], self.bones], writes=[self.bbuf[be]])
            P.op("act", lambda h: h.copy(out=mean[:, 0:T], in_=self.bank[bm][:, 0:T]), reads=[self.bbuf[bm]], writes=[bmean])
            P.op("dve", lambda h: h.tensor_tensor(out=rstd[:, 0:T], in0=mean[:, 0:T], in1=mean[:, 0:T], op=ALU.mult), reads=[bmean], writes=[brstd])
            P.op("dve", lambda h: h.scalar_tensor_tensor(out=rstd[:, 0:T], in0=self.bank[be][:, 0:T], scalar=float(LN_EPS), in1=rstd[:, 0:T], op0=ALU.add, op1=ALU.subtract), reads=[self.bbuf[be], brstd], writes=[brstd])
            P.op("act", lambda h: h.activation(out=rstd[:, 0:T], in_=rstd[:, 0:T], func=AF.Sqrt), reads=[brstd], writes=[brstd])
            P.op("dve", lambda h: h.reciprocal(out=rstd[:, 0:T], in_=rstd[:, 0:T]), reads=[brstd], writes=[brstd])

        def part_c(k):
            P.op("pool", lambda h: h.tensor_tensor(out=r[:, k, 0:T], in0=r[:, k, 0:T], in1=mean[:, 0:T], op=ALU.subtract), reads=[br[k], bmean], writes=[br[k]])
            P.op("dve", lambda h: h.tensor_tensor(out=r[:, k, 0:T], in0=r[:, k, 0:T], in1=rstd[:, 0:T], op=ALU.mult), reads=[br[k], brstd], writes=[br[k]])
            P.op("act", lambda h: h.activation(out=ob[:, k, 0:T], in_=r[:, k, 0:T], func=AF.Identity, scale=self.vcol(gname, k), bias=self.vcol(bname, k)),
                 reads=[br[k], self.bvecs], writes=[bob])
            P.op("act", lambda h: h.activation(out=r[:, k, 0:T], in_=r[:, k, 0:T], func=AF.Identity, scale=self.vcol(gname, k), bias=self.vcol(bname, k)),
                 reads=[br[k], self.bvecs], writes=[br[k]])

        def part_d():
            P.dma(lambda h: h.dma_start(out=hf[:, :, t0:t0 + T], in_=r[:, :, 0:T]), reads=[S["brall_l"][rs_]] + br)
            P.dma(lambda h: h.dma_start(out=hb[:, :, 1 + t0:1 + t0 + T], in_=ob[:, :, 0:T]), reads=[bob])

        pieces = [part_b] + [(lambda k=k: part_c(k)) for k in range(8)] + [part_d]
        if defer:
            return pieces
        for p_ in pieces:
            p_()
        return []

    def alloc_res_ln(self, st, pfx, obanks, sbanks, nh=1, nr=1):
        P = self.P
        S = {}
        S["hres2"] = [self.sb(st, pfx + "hres%d" % q, [128, 8, 512]) for q in range(nh)]
        S["bhres2"] = [P.buf() for _ in range(nh)]
        S["hs"] = 0
        S["r_l"] = [self.sb(st, pfx + "r%d" % q, [128, 8, 512]) for q in range(nr)]
        S["br_l"] = [[P.buf() for _ in range(8)] for q in range(nr)]
        S["brall_l"] = [P.buf() for q in range(nr)]
        S["rs"] = 0
        S["rb"] = self.sb(st, pfx + "rb", [128, 8, 512], BF16)
        S["brb"] = [P.buf() for _ in range(8)]
        S["r2"] = self.sb(st, pfx + "r2", [128, 8, 512], BF16)
        S["br2"] = [P.buf() for _ in range(8)]
        S["ob_l"] = [self.sb(st, pfx + "ob%d" % q, [128, 8, 512], BF16) for q in range(nr)]
        S["bob_l"] = [P.buf() for q in range(nr)]
        S["mean"] = self.sb(st, pfx + "mean", [128, 512])
        S["bmean"] = P.buf()
        S["rstd"] = self.sb(st, pfx + "rstd", [128, 512])
        S["brstd"] = P.buf()
        S["obanks"] = obanks
        S["sbanks"] = sbanks
        return S

    def res_ln_bufs(self, S):
        return S["bhres2"] + S["brall_l"] + S["bob_l"] + [S["bmean"], S["brstd"]] + [b for l in S["br_l"] for b in l] + S["brb"] + S["br2"]

    def phase_ffn(self, i, g):
        P = self.P
        hbin = self.dram["hb1_" + g.name].rearrange("(k p) t -> p k t", p=128)
        hfin = self.dram["hf1_" + g.name].rearrange("(k p) t -> p k t", p=128)
        w1d = self.dram["b_ffn_w1"]
        w2d = self.dram["b_ffn_w_out"]
        with contextlib.ExitStack() as st:
            w2 = self.sb(st, "f_w2", [128, NFF, D], BF16)
            bw2 = P.buf()
            P.dma(lambda h: h.dma_start(out=w2[:, 0:11, :], in_=w2d[i, 0:11 * 128, :].rearrange("(n p) d -> p n d", p=128)), writes=[bw2])
            bw2b = P.buf()
            P.dma(lambda h: h.dma_start(out=w2[:, 11:22, :], in_=w2d[i, 11 * 128:22 * 128, :].rearrange("(n p) d -> p n d", p=128)), writes=[bw2b])
            NW = 8
            w1 = [self.sb(st, "f_w1_%d" % s, [128, 2, 8, 128], BF16) for s in range(NW)]
            bw1 = [P.buf() for _ in range(NW)]
            hb = [self.sb(st, "f_hb%d" % s, [128, 8, 512], BF16) for s in range(2)]
            bhb = [P.buf() for _ in range(2)]
            act = self.sb(st, "f_act", [128, NFF, 512], BF16)
            bact = [P.buf() for _ in range(NFF)]
            tmp = [self.sb(st, "f_tmp%d" % s, [128, 512]) for s in range(2)]
            btmp = [P.buf() for _ in range(2)]
            gl = [self.sb(st, "f_gl%d" % s, [128, 512]) for s in range(2)]
            bgl = [P.buf() for _ in range(2)]
            asb = [self.sb(st, "f_asb%d" % s, [128, 512]) for s in range(2)]
            basb = [P.buf() for _ in range(2)]
            S = self.alloc_res_ln(st, "f_", obanks=[4, 5], sbanks=[4, 5], nh=2)
            abanks = [2, 3, 6, 7]
            tiles = tiles_even(g.L, 510)
            wi = 0
            pend_ln = []
            for ti, (t0, T) in enumerate(tiles):
                s = ti % 2
                P.dma(lambda h, s=s, t0=t0, T=T: h.dma_start(out=hb[s][:, :, 0:T + 2], in_=hbin[:, :, t0:t0 + T + 2]), writes=[bhb[s]])
                P.dma(lambda h, t0=t0, T=T, s=s: h.dma_start(out=S["hres2"][s][:, :, 0:T], in_=hfin[:, :, t0:t0 + T]), writes=[S["bhres2"][s]])
                for n in range(NFF):
                    if n >= 1 and pend_ln:
                        pend_ln.pop(0)()
                    ws = wi % NW
                    wi += 1
                    P.dma(lambda h, ws=ws, n=n: h.dma_start(out=w1[ws][:], in_=w1d[i, n]), writes=[bw1[ws]])
                    bg = n % 2
                    ba = abanks[n % 4]
                    for k in range(8):
                        P.op("pe", lambda h, ws=ws, k=k, s=s, T=T, bg=bg: h.matmul(self.bank[bg][:, 0:T + 2], w1[ws][:, 0, k, :], hb[s][:, k, 0:T + 2], start=(k == 0), stop=(k == 7)),
                             reads=[bw1[ws], bhb[s]], writes=[self.bbuf[bg]])
                    for k in range(8):
                        P.op("pe", lambda h, ws=ws, k=k, s=s, T=T, ba=ba: h.matmul(self.bank[ba][:, 0:T], w1[ws][:, 1, k, :], hb[s][:, k, 1:T + 1], start=(k == 0), stop=(k == 7)),
                             reads=[bw1[ws], bhb[s]], writes=[self.bbuf[ba]])
                    q = n % 2
                    P.op("dve", lambda h, n=n, q=q, bg=bg, T=T: h.tensor_scalar(tmp[q][:, 0:T], self.bank[bg][:, 1:T + 1], self.vcol("fcw%d_1" % i, n), self.vcol("fcb%d" % i, n), ALU.mult, ALU.add),
                         reads=[self.bbuf[bg], self.bvecs], writes=[btmp[q]])
                    P.op("dve", lambda h, n=n, q=q, bg=bg, T=T: h.scalar_tensor_tensor(out=tmp[q][:, 0:T], in0=self.bank[bg][:, 0:T], scalar=self.vcol("fcw%d_0" % i, n), in1=tmp[q][:, 0:T], op0=ALU.mult, op1=ALU.add),
                         reads=[self.bbuf[bg], btmp[q]], writes=[btmp[q]])
                    P.op("dve", lambda h, n=n, q=q, bg=bg, T=T: h.scalar_tensor_tensor(out=tmp[q][:, 0:T], in0=self.bank[bg][:, 2:T + 2], scalar=self.vcol("fcw%d_2" % i, n), in1=tmp[q][:, 0:T], op0=ALU.mult, op1=ALU.add),
                         reads=[self.bbuf[bg], btmp[q]], writes=[btmp[q]])
                    P.op("act", lambda h, q=q, T=T: h.activation(out=gl[q][:, 0:T], in_=tmp[q][:, 0:T], func=AF.Gelu), reads=[btmp[q]], writes=[bgl[q]])
                    P.op("act", lambda h, q=q, ba=ba, T=T: h.copy(out=asb[q][:, 0:T], in_=self.bank[ba][:, 0:T]), reads=[self.bbuf[ba]], writes=[basb[q]])
                    P.op("pool", lambda h, n=n, q=q, T=T: h.tensor_tensor(out=act[:, n, 0:T], in0=asb[q][:, 0:T], in1=gl[q][:, 0:T], op=ALU.mult),
                         reads=[basb[q], bgl[q]], writes=[bact[n]])

                def acc(k, bk, T=T):
                    for n in range(NFF):
                        P.op("pe", lambda h, n=n, k=k, bk=bk: h.matmul(self.bank[bk][:, 0:T], w2[:, n, k * 128:(k + 1) * 128], act[:, n, 0:T], start=(n == 0), stop=(n == NFF - 1)),
                             reads=[bact[n], bw2, bw2b], writes=[self.bbuf[bk]])
                S["hs"] = s
                pend_ln = self.res_ln(S, i, 2, g, t0, T, 0, acc, defer=False)
            while pend_ln:
                pend_ln.pop(0)()
            P.barrier()
            P.release([bw2, bw2b] + bw1 + bhb + self.res_ln_bufs(S))

    def phase_proj(self, i, g, wname, j, prologue=None):
        P = self.P
        wd = self.dram[wname]
        xo = self.dram["xo_" + g.name].rearrange("(k p) t -> p k t", p=128)
        hfin = self.dram["hf0_" + g.name].rearrange("(k p) t -> p k t", p=128)
        with contextlib.ExitStack() as st:
            w = self.sb(st, "p_w", [128, 8, D], BF16)
            bw = P.buf()
            P.dma(lambda h: h.dma_start(out=w[:], in_=wd[j].rearrange("(k p) d -> p k d", p=128)), writes=[bw])
            xin = [self.sb(st, "p_x%d" % s, [128, 8, 512], BF16) for s in range(2)]
            bxin = [P.buf() for _ in range(2)]
            S = self.alloc_res_ln(st, "p_", obanks=[0, 1, 2, 3], sbanks=[6, 7], nh=(1 if prologue else 2), nr=(1 if prologue else 2))
            extra = prologue("alloc", st) if prologue else None
            tiles = tiles_even(g.L, 512) if not prologue else [(t0, min(512, g.L - t0)) for t0 in range(0, g.L, 512)]
            for ti, (t0, T) in enumerate(tiles):
                s = ti % 2
                if prologue:
                    prologue("tile", (ti, t0, T, xin[s], bxin[s], extra))
                else:
                    P.dma(lambda h, s=s, t0=t0, T=T: h.dma_start(out=xin[s][:, :, 0:T], in_=xo[:, :, t0:t0 + T]), writes=[bxin[s]])
                hs_ = s % len(S["hres2"])
                P.dma(lambda h, t0=t0, T=T, hs_=hs_: h.dma_start(out=S["hres2"][hs_][:, :, 0:T], in_=hfin[:, :, t0:t0 + T]), writes=[S["bhres2"][hs_]])
                S["hs"] = hs_
                S["rs"] = ti

                def acc(k, bk, T=T, s=s):
                    for kk in range(8):
                        P.op("pe", lambda h, kk=kk, k=k, bk=bk: h.matmul(self.bank[bk][:, 0:T], w[:, kk, k * 128:(k + 1) * 128], xin[s][:, kk, 0:T], start=(kk == 0), stop=(kk == 7)),
                             reads=[bxin[s], bw], writes=[self.bbuf[bk]])
                self.res_ln(S, i, 1, g, t0, T, 1, acc)
            P.barrier()
            rel = [bw] + bxin + self.res_ln_bufs(S)
            if prologue:
                rel += prologue("bufs", extra)
            P.release(rel)

    def layer_attn(self, i, g):
        P = self.P
        j = i // 2
        L = g.L
        hbin = self.dram["hb0_" + g.name].rearrange("(k p) t -> p k t", p=128)
        wq = self.dram["b_at_w_qkv"]
        qT = self.dram["qT_" + g.name].rearrange("(k p) t -> p k t", p=128)
        xo = self.dram["xo_" + g.name].rearrange("(k p) t -> p k t", p=128)
        cosd = self.dram["cos_" + g.name]
        sind = self.dram["sin_" + g.name]
        tiles = [(t0, min(512, L - t0)) for t0 in range(0, L, 512)]
        nkc = (L + 127) // 128
        with contextlib.ExitStack() as st:
            kT = self.sb(st, "a_kT", [128, NKV, L], BF16)
            bkT = [[P.buf() for _ in tiles] for _ in range(NKV)]
            vv = self.sb(st, "a_v", [128, nkc, NKV * HD], BF16)
            bvv = [P.buf() for _ in range(nkc)]
            with contextlib.ExitStack() as s1:
                w = self.sb(s1, "a_w", [128, 8, 1536], BF16)
                bw = P.buf()
                P.dma(lambda h: h.dma_start(out=w[:], in_=wq[j].rearrange("(k p) d -> p k d", p=128)), writes=[bw])
                hb = [self.sb(s1, "a_hb%d" % s, [128, 8, 512], BF16) for s in range(2)]
                bhb = [P.buf() for _ in range(2)]
                cs = [self.sb(s1, "a_cos%d" % s, [128, 512]) for s in range(2)]
                sn = [self.sb(s1, "a_sin%d" % s, [128, 512]) for s in range(2)]
                bcs = [P.buf() for _ in range(2)]
                bsn = [P.buf() for _ in range(2)]
                sq = [self.sb(s1, "a_sq%d" % s, [128, 512], BF16) for s in range(2)]
                bsq = [P.buf() for _ in range(2)]
                rs = [self.sb(s1, "a_rs%d" % s, [128, 512]) for s in range(2)]
                brs = [P.buf() for _ in range(2)]
                aa = [self.sb(s1, "a_aa%d" % s, [128, 512]) for s in range(2)]
                baa = [P.buf() for _ in range(2)]
                ar = [self.sb(s1, "a_ar%d" % s, [128, 512]) for s in range(2)]
                bar = [P.buf() for _ in range(2)]
                qo = [self.sb(s1, "a_qo%d" % s, [128, 8, 512], BF16) for s in range(2)]
                bqo = [P.buf() for _ in range(2)]
                vst = [self.sb(s1, "a_vst%d" % s, [128, 256]) for s in range(2)]
                hc = 0
                for ti, (t0, T) in enumerate(tiles):
                    s = ti % 2
                    P.dma(lambda h, s=s, t0=t0, T=T: h.dma_start(out=hb[s][:, :, 0:T], in_=hbin[:, :, 1 + t0:1 + t0 + T]), writes=[bhb[s]])
                    P.dma(lambda h, s=s, t0=t0, T=T: h.dma_start(out=cs[s][:, 0:T], in_=cosd[:, t0:t0 + T]), writes=[bcs[s]])
                    P.dma(lambda h, s=s, t0=t0, T=T: h.dma_start(out=sn[s][:, 0:T], in_=sind[:, t0:t0 + T]), writes=[bsn[s]])
                    for hh in range(NH + NKV):
                        q = hc % 2
                        hc += 1
                        bq = (hc % 2) * 2
                        bs_ = bq + 1
                        gn = ("qg%d" % j) if hh < NH else ("kg%d" % j)
                        for k in range(8):
                            P.op("pe", lambda h, k=k, hh=hh, s=s, T=T, bq=bq: h.matmul(self.bank[bq][:, 0:T], w[:, k, hh * 128:(hh + 1) * 128], hb[s][:, k, 0:T], start=(k == 0), stop=(k == 7)),
                                 reads=[bw, bhb[s]], writes=[self.bbuf[bq]])
                        P.op("dve", lambda h, q=q, bq=bq, T=T: h.tensor_copy(out=aa[q][:, 0:T], in_=self.bank[bq][:, 0:T]), reads=[self.bbuf[bq]], writes=[baa[q]])
                        P.op("act", lambda h, q=q, T=T: h.activation(out=sq[q][:, 0:T], in_=aa[q][:, 0:T], func=AF.Square), reads=[baa[q]], writes=[bsq[q]])
                        P.op("pe", lambda h, q=q, bs_=bs_, T=T: h.matmul(self.bank[bs_][:, 0:T], self.onesH[:], sq[q][:, 0:T], start=True, stop=True),
                             reads=[bsq[q], self.bones], writes=[self.bbuf[bs_]])
                        P.op("dve", lambda h, q=q, bs_=bs_, T=T: h.tensor_scalar(rs[q][:, 0:T], self.bank[bs_][:, 0:T], float(RMS_EPS), None, ALU.add),
                             reads=[self.bbuf[bs_]], writes=[brs[q]])
                        P.op("act", lambda h, q=q, T=T: h.activation(out=rs[q][:, 0:T], in_=rs[q][:, 0:T], func=AF.Sqrt), reads=[brs[q]], writes=[brs[q]])
                        P.op("dve", lambda h, q=q, T=T: h.reciprocal(out=rs[q][:, 0:T], in_=rs[q][:, 0:T]), reads=[brs[q]], writes=[brs[q]])
                        P.op("dve", lambda h, q=q, bq=bq, T=T, gn=gn: h.scalar_tensor_tensor(out=aa[q][:, 0:T], in0=aa[q][:, 0:T], scalar=self.vcol(gn), in1=rs[q][:, 0:T], op0=ALU.mult, op1=ALU.mult),
                             reads=[baa[q], bsq[q], brs[q], self.bvecs], writes=[baa[q]])
                        P.op("pool", lambda h, q=q, T=T: h.tensor_copy(out=ar[q][0:64, 0:T], in_=aa[q][64:128, 0:T]), reads=[baa[q]], writes=[bar[q]])
                        P.op("pool", lambda h, q=q, T=T: h.tensor_copy(out=ar[q][64:128, 0:T], in_=aa[q][0:64, 0:T]), reads=[baa[q]], writes=[bar[q]])
                        P.op("pool", lambda h, q=q, s=s, T=T: h.tensor_tensor(out=ar[q][:, 0:T], in0=ar[q][:, 0:T], in1=sn[s][:, 0:T], op=ALU.mult), reads=[bar[q], bsn[s]], writes=[bar[q]])
                        P.op("dve", lambda h, q=q, s=s, T=T: h.tensor_tensor(out=aa[q][:, 0:T], in0=aa[q][:, 0:T], in1=cs[s][:, 0:T], op=ALU.mult), reads=[baa[q], bcs[s]], writes=[baa[q]])
                        if hh < NH:
                            P.op("dve", lambda h, q=q, s=s, hh=hh, T=T: h.tensor_tensor(out=qo[s][:, hh, 0:T], in0=aa[q][:, 0:T], in1=ar[q][:, 0:T], op=ALU.add),
                                 reads=[baa[q], bar[q]], writes=[bqo[s]])
                        else:
                            P.op("dve", lambda h, q=q, hh=hh, t0=t0, T=T: h.tensor_tensor(out=kT[:, hh - NH, t0:t0 + T], in0=aa[q][:, 0:T], in1=ar[q][:, 0:T], op=ALU.add),
                                 reads=[baa[q], bar[q]], writes=[bkT[hh - NH][ti]])
                    P.dma(lambda h, s=s, t0=t0, T=T: h.dma_start(out=qT[:, :, t0:t0 + T], in_=qo[s][:, :, 0:T]), reads=[bqo[s]])
                    for c in range((T + 127) // 128):
                        tp = min(128, T - c * 128)
                        kc = t0 // 128 + c
                        bv = 4 + (kc % 2)
                        for k in range(8):
                            P.op("pe", lambda h, k=k, s=s, c=c, tp=tp, bv=bv: h.matmul(self.bank[bv][0:tp, 0:256], hb[s][:, k, c * 128:c * 128 + tp], w[:, k, 1280:1536], start=(k == 0), stop=(k == 7)),
                                 reads=[bw, bhb[s]], writes=[self.bbuf[bv]])
                        P.op("act", lambda h, kc=kc, tp=tp, bv=bv: h.copy(out=vv[0:tp, kc, :], in_=self.bank[bv][0:tp, 0:256]), reads=[self.bbuf[bv]], writes=[bvv[kc]])
                P.barrier()
                P.release([bw] + bhb + bcs + bsn + bqo)
            with contextlib.ExitStack() as s2:
                NQ = 3
                qt = [self.sb(s2, "a_qt%d" % s, [128, 512], BF16) for s in range(NQ)]
                bqt = [P.buf() for _ in range(NQ)]
                NP = 4
                pt = [self.sb(s2, "a_pt%d" % s, [128, 512], BF16) for s in range(NP)]
                bpt = [P.buf() for _ in range(NP)]
                rd = [self.sb(s2, "a_rd%d" % s, [128, 512]) for s in range(2)]
                brd = [P.buf() for _ in range(2)]
                ot = [self.sb(s2, "a_ot%d" % s, [128, 512], BF16) for s in range(2)]
                bot = [P.buf() for _ in range(2)]
                scale = float(HD ** -0.5)
                it = 0
                pc = 0
                allk = [b for row in bkT for b in row]
                for ti, (t0, T) in enumerate(tiles):
                    for hh in range(NH):
                        kv = hh // 4
                        qs = it % NQ
                        o2 = it % 2
                        it += 1
                        bo = 4 + o2 * 2
                        bd = bo + 1
                        P.dma(lambda h, qs=qs, hh=hh, t0=t0, T=T: h.dma_start(out=qt[qs][:, 0:T], in_=qT[:, hh, t0:t0 + T]), writes=[bqt[qs]])
                        LOOK = 2
                        slots = {}
                        for kq in range(nkc + LOOK):
                            if kq < nkc:
                                kc = kq
                                kp = min(128, L - kc * 128)
                                sbk = pc % 4
                                ps = pc % NP
                                pc += 1
                                slots[kc] = ps
                                P.op("pe", lambda h, kc=kc, kp=kp, kv=kv, qs=qs, sbk=sbk, T=T: h.matmul(self.bank[sbk][0:kp, 0:T], kT[:, kv, kc * 128:kc * 128 + kp], qt[qs][:, 0:T], start=True, stop=True),
                                     reads=[bqt[qs], bkT[kv][kc // 4]], writes=[self.bbuf[sbk]])
                                P.op("act", lambda h, kp=kp, sbk=sbk, ps=ps, T=T: h.activation(out=pt[ps][0:kp, 0:T], in_=self.bank[sbk][0:kp, 0:T], func=AF.Exp, scale=scale),
                                     reads=[self.bbuf[sbk]], writes=[bpt[ps]])
                            if kq >= LOOK:
                                kc = kq - LOOK
                                kp = min(128, L - kc * 128)
                                ps = slots[kc]
                                P.op("pe", lambda h, kc=kc, kp=kp, kv=kv, ps=ps, bo=bo, T=T: h.matmul(self.bank[bo][:, 0:T], vv[0:kp, kc, kv * HD:(kv + 1) * HD], pt[ps][0:kp, 0:T], start=(kc == 0), stop=(kc == nkc - 1)),
                                     reads=[bpt[ps], bvv[kc]], writes=[self.bbuf[bo]])
                                P.op("pe", lambda h, kc=kc, kp=kp, ps=ps, bd=bd, T=T: h.matmul(self.bank[bd][:, 0:T], self.ones1[0:kp, :], pt[ps][0:kp, 0:T], start=(kc == 0), stop=(kc == nkc - 1)),
                                     reads=[bpt[ps], self.bones], writes=[self.bbuf[bd]])
                        P.op("dve", lambda h, o2=o2, bd=bd, T=T: h.reciprocal(out=rd[o2][:, 0:T], in_=self.bank[bd][:, 0:T]), reads=[self.bbuf[bd]], writes=[brd[o2]])
                        P.op("dve", lambda h, o2=o2, bo=bo, T=T: h.tensor_tensor(out=ot[o2][:, 0:T], in0=self.bank[bo][:, 0:T], in1=rd[o2][:, 0:T], op=ALU.mult),
                             reads=[self.bbuf[bo], brd[o2]], writes=[bot[o2]])
                        P.dma(lambda h, o2=o2, hh=hh, t0=t0, T=T: h.dma_start(out=xo[:, hh, t0:t0 + T], in_=ot[o2][:, 0:T]), reads=[bot[o2]])
                P.barrier()
                P.release(bqt + bot)
        self.phase_proj(i, g, "b_at_w_out", j)

    def load_const(self, st, name, shape, dt=F32):
        P = self.P
        t = self.sb(st, "c_" + name, shape)
        b = P.buf()
        P.dma(lambda h: h.dma_start(out=t[:], in_=self.dram[name][:, :]), writes=[b])
        if dt == F32:
            return t, b
        self.const_rel = getattr(self, "const_rel", []) + [b]
        t2 = self.sb(st, "cb_" + name, shape, dt)
        b2 = P.buf()
        P.op("dve", lambda h: h.tensor_copy(out=t2[:], in_=t[:]), reads=[b], writes=[b2])
        return t2, b2

    def fft_pass(self, g, st, C, src, nrows, mode, extra):
        P = self.P
        M, Hh, nk, nin = g.M, g.Hh, g.nk, g.nin
        CG = g.CG
        kf = self.dram["kf_" + g.name]
        yc = self.dram["yc_" + g.name]
        W2 = 2 * nk
        NS = 4
        G = D // CG
        xt = [self.sb(st, "ff_x%d" % s, [M, CG, M], FDT) for s in range(NS)]
        bxc = [P.buf() for _ in range(NS)]
        x32 = [self.sb(st, "ff_x32_%d" % s, [M, CG, M]) for s in range(NS)]
        bxt = [P.buf() for _ in range(NS)]
        bxt2 = [P.buf() for _ in range(NS)]
        B = [self.sb(st, "ff_B%d" % s, [M, CG, W2], FDT) for s in range(NS)]
        bB = [P.buf() for _ in range(NS)]
        t1 = [self.sb(st, "ff_t1%d" % s, [M, CG, W2], FDT) for s in range(NS)]
        bt1 = [P.buf() for _ in range(NS)]
        if mode == "data":
            for s in range(NS):
                P.op("pool", lambda h, s=s: h.memset(x32[s][:], 0.0), writes=[bxt[s], bxt2[s]])
            kt = [self.sb(st, "ff_k%d" % s, [M, CG, W2]) for s in range(NS)]
            bkt = [P.buf() for _ in range(NS)]
            Y = [self.sb(st, "ff_Y%d" % s, [M, CG, W2], FDT) for s in range(NS)]
            bY = [P.buf() for _ in range(NS)]
            Dd = [self.sb(st, "ff_D%d" % s, [nk, CG, 2 * M], FDT) for s in range(NS)]
            bD = [P.buf() for _ in range(NS)]
            t2 = [self.sb(st, "ff_t2%d" % s, [nk, CG, 2 * M], FDT) for s in range(NS)]
            bt2 = [P.buf() for _ in range(NS)]
            yo = [self.sb(st, "ff_yo%d" % s, [nin, CG, M]) for s in range(NS)]
            byo = [P.buf() for _ in range(NS)]
        else:
            sc = extra["sc"]
            bsc = extra["bsc"]
            ko = [self.sb(st, "ff_ko%d" % s, [M, CG, W2]) for s in range(NS)]
            bko = [P.buf() for _ in range(NS)]

        def bc(tab, p, lo, hi):
            a = tab[0:p, lo:hi]
            return a.unsqueeze(1).to_broadcast([p, CG, hi - lo])

        cpb = 512 // (2 * M)
        ib = [4, 5][:CG // cpb]
        assert CG // cpb <= 2 and CG * 2 * Hh <= 512 and CG * M <= 512

        def stage_a(gi):
            s = gi % NS
            c0 = gi * CG
            ba = gi % 2
            if mode == "data":
                P.dma(lambda h: h.dma_start(out=x32[s][0:nin - 1, :, :], in_=src[c0:c0 + CG, 0:(nin - 1) * M].rearrange("c (a b) -> a c b", b=M)), writes=[bxt[s]])
                P.dma(lambda h: h.dma_start(out=x32[s][nin - 1:nin, :, 0:g.tail], in_=src[c0:c0 + CG, (nin - 1) * M:g.L].unsqueeze(0)), writes=[bxt2[s]])
                P.dma(lambda h: h.dma_start(out=kt[s][:], in_=kf[gi].rearrange("m (c w) -> m c w", c=CG)), writes=[bkt[s]])
                P.op("act", lambda h: h.copy(out=xt[s][0:nrows, :, :], in_=x32[s][0:nrows, :, :]), reads=[bxt[s], bxt2[s]], writes=[bxc[s]])
                rdeps = [bxc[s]]
            else:
                P.dma(lambda h: h.dma_start(out=x32[s][:, :, :], in_=src[c0:c0 + CG, :].rearrange("c (a b) -> a c b", b=M)), writes=[bxt[s]])
                P.op("pool", lambda h: h.tensor_tensor(out=xt[s][:], in0=x32[s][:], in1=sc[0:M, c0:c0 + CG].unsqueeze(2).to_broadcast([M, CG, M]), op=ALU.mult),
                     reads=[bxt[s], bsc], writes=[bxc[s]])
                rdeps = [bxc[s]]
            for c in range(CG):
                P.op("pe", lambda h, c=c: h.matmul(self.bank[ba][0:M, c * 2 * Hh:(c + 1) * 2 * Hh], xt[s][0:nrows, c, :], C["f1"][0][0:nrows, :], start=True, stop=True),
                     reads=rdeps + [C["f1"][1]], writes=[self.bbuf[ba]])
            A = self.bank[ba][0:M, 0:CG * 2 * Hh].rearrange("p (c w) -> p c w", c=CG)
            Are = A[:, :, 0:nk]
            Aim = A[:, :, nk:2 * Hh]
            P.op("dve", lambda h: h.tensor_tensor(out=B[s][:, :, 0:nk], in0=Are, in1=bc(C["twc"][0], M, 0, nk), op=ALU.mult),
                 reads=[self.bbuf[ba], C["twc"][1]], writes=[bB[s]])
            P.op("dve", lambda h: h.tensor_tensor(out=B[s][:, :, nk:W2], in0=Are, in1=bc(C["ntws"][0], M, 0, nk), op=ALU.mult),
                 reads=[self.bbuf[ba], C["ntws"][1]], writes=[bB[s]])
            P.op("dve", lambda h: h.tensor_tensor(out=t1[s][:, :, 1:Hh], in0=Aim, in1=bc(C["tws"][0], M, 1, Hh), op=ALU.mult),
                 reads=[self.bbuf[ba], C["tws"][1]], writes=[bt1[s]])
            P.op("dve", lambda h: h.tensor_tensor(out=t1[s][:, :, nk + 1:nk + Hh], in0=Aim, in1=bc(C["twc"][0], M, 1, Hh), op=ALU.mult),
                 reads=[self.bbuf[ba], C["twc"][1]], writes=[bt1[s]])
            P.op("pool", lambda h: h.tensor_tensor(out=B[s][:, :, 1:Hh], in0=B[s][:, :, 1:Hh], in1=t1[s][:, :, 1:Hh], op=ALU.add), reads=[bB[s], bt1[s]], writes=[bB[s]])
            P.op("pool", lambda h: h.tensor_tensor(out=B[s][:, :, nk + 1:nk + Hh], in0=B[s][:, :, nk + 1:nk + Hh], in1=t1[s][:, :, nk + 1:nk + Hh], op=ALU.add), reads=[bB[s], bt1[s]], writes=[bB[s]])

        def stage_b(gi):
            s = gi % NS
            bre, bim = 2, 3
            Bre = B[s][:, :, 0:nk]
            Bim = B[s][:, :, nk:W2]
            Xre = self.bank[bre][0:M, 0:CG * nk].rearrange("p (c w) -> p c w", c=CG)
            Xim = self.bank[bim][0:M, 0:CG * nk].rearrange("p (c w) -> p c w", c=CG)
            P.op("pe", lambda h: h.matmul(Xre, C["c2"][0][:], Bre, start=True, stop=False), reads=[bB[s], C["c2"][1]], writes=[self.bbuf[bre]])
            P.op("pe", lambda h: h.matmul(Xre, C["s2"][0][:], Bim, start=False, stop=True), reads=[bB[s], C["s2"][1]], writes=[self.bbuf[bre]])
            P.op("pe", lambda h: h.matmul(Xim, C["c2"][0][:], Bim, start=True, stop=False), reads=[bB[s], C["c2"][1]], writes=[self.bbuf[bim]])
            P.op("pe", lambda h: h.matmul(Xim, C["ns2"][0][:], Bre, start=False, stop=True), reads=[bB[s], C["ns2"][1]], writes=[self.bbuf[bim]])
            if mode == "filter":
                P.op("act", lambda h: h.copy(out=ko[s][:, :, 0:nk], in_=Xre), reads=[self.bbuf[bre]], writes=[bko[s]])
                P.op("dve", lambda h: h.tensor_copy(out=ko[s][:, :, nk:W2], in_=Xim), reads=[self.bbuf[bim]], writes=[bko[s]])
                P.dma(lambda h: h.dma_start(out=kf[gi].rearrange("m (c w) -> m c w", c=CG), in_=ko[s][:]), reads=[bko[s]])
                return
            Kre = kt[s][:, :, 0:nk]
            Kim = kt[s][:, :, nk:W2]
            P.op("dve", lambda h: h.tensor_tensor(out=Y[s][:, :, 0:nk], in0=Xre, in1=Kre, op=ALU.mult), reads=[self.bbuf[bre], bkt[s]], writes=[bY[s]])
            P.op("dve", lambda h: h.tensor_tensor(out=Y[s][:, :, nk:W2], in0=Xre, in1=Kim, op=ALU.mult), reads=[self.bbuf[bre], bkt[s]], writes=[bY[s]])
            P.op("dve", lambda h: h.tensor_tensor(out=t1[s][:, :, 0:nk], in0=Xim, in1=Kim, op=ALU.mult), reads=[self.bbuf[bim], bkt[s], bB[s]], writes=[bt1[s]])
            P.op("dve", lambda h: h.tensor_tensor(out=t1[s][:, :, nk:W2], in0=Xim, in1=Kre, op=ALU.mult), reads=[self.bbuf[bim], bkt[s]], writes=[bt1[s]])
            P.op("pool", lambda h: h.tensor_tensor(out=Y[s][:, :, 0:nk], in0=Y[s][:, :, 0:nk], in1=t1[s][:, :, 0:nk], op=ALU.subtract), reads=[bY[s], bt1[s]], writes=[bY[s]])
            P.op("pool", lambda h: h.tensor_tensor(out=Y[s][:, :, nk:W2], in0=Y[s][:, :, nk:W2], in1=t1[s][:, :, nk:W2], op=ALU.add), reads=[bY[s], bt1[s]], writes=[bY[s]])

        def stage_c(gi):
            s = gi % NS
            for c in range(CG):
                bi = ib[c // cpb]
                o = self.bank[bi][0:nk, (c % cpb) * 2 * M:(c % cpb + 1) * 2 * M]
                P.op("pe", lambda h, c=c, o=o: h.matmul(o, Y[s][:, c, 0:nk], C["if2a"][0][:], start=True, stop=False), reads=[bY[s], C["if2a"][1]], writes=[self.bbuf[bi]])
                P.op("pe", lambda h, c=c, o=o: h.matmul(o, Y[s][:, c, nk:W2], C["if2b"][0][:], start=False, stop=True), reads=[bY[s], C["if2b"][1]], writes=[self.bbuf[bi]])
            for bi_i, bi in enumerate(ib):
                cs_ = slice(bi_i * cpb, (bi_i + 1) * cpb)
                Cc = self.bank[bi][0:nk, 0:cpb * 2 * M].rearrange("p (c w) -> p c w", c=cpb)
                Cre = Cc[:, :, 0:M]
                Cim = Cc[:, :, M:2 * M]

                def bc2(tab):
                    return tab[0:nk, 0:M].unsqueeze(1).to_broadcast([nk, cpb, M])
                P.op("dve", lambda h: h.tensor_tensor(out=Dd[s][:, cs_, 0:M], in0=Cre, in1=bc2(C["twct"][0]), op=ALU.mult), reads=[self.bbuf[bi], C["twct"][1]], writes=[bD[s]])
                P.op("dve", lambda h: h.tensor_tensor(out=Dd[s][:, cs_, M:2 * M], in0=Cre, in1=bc2(C["twst"][0]), op=ALU.mult), reads=[self.bbuf[bi], C["twst"][1]], writes=[bD[s]])
                P.op("dve", lambda h: h.tensor_tensor(out=t2[s][:, cs_, 0:M], in0=Cim, in1=bc2(C["ntwst"][0]), op=ALU.mult), reads=[self.bbuf[bi], C["ntwst"][1]], writes=[bt2[s]])
                P.op("dve", lambda h: h.tensor_tensor(out=t2[s][:, cs_, M:2 * M], in0=Cim, in1=bc2(C["twct"][0]), op=ALU.mult), reads=[self.bbuf[bi], C["twct"][1]], writes=[bt2[s]])
            P.op("pool", lambda h: h.tensor_tensor(out=Dd[s][:], in0=Dd[s][:], in1=t2[s][:], op=ALU.add), reads=[bD[s], bt2[s]], writes=[bD[s]])

        def stage_d(gi):
            s = gi % NS
            c0 = gi * CG
            yb = 6 + gi % 2
            yv = self.bank[yb][0:nin, 0:CG * M].rearrange("p (c w) -> p c w", c=CG)
            P.op("pe", lambda h: h.matmul(yv, C["gc"][0][:], Dd[s][:, :, 0:M], start=True, stop=False), reads=[bD[s], C["gc"][1]], writes=[self.bbuf[yb]])
            P.op("pe", lambda h: h.matmul(yv, C["gs"][0][:], Dd[s][:, :, M:2 * M], start=False, stop=True), reads=[bD[s], C["gs"][1]], writes=[self.bbuf[yb]])
            P.op("act", lambda h: h.copy(out=yo[s][:], in_=yv), reads=[self.bbuf[yb]], writes=[byo[s]])
            P.dma(lambda h: h.dma_start(out=yc[c0:c0 + CG, :].rearrange("c (a b) -> a c b", b=M), in_=yo[s][:]), reads=[byo[s]])

        nst = 4 if mode == "data" else 2
        P.store_q = "act"
        for t in range(G + nst - 1):
            if mode == "data":
                if 0 <= t - 3 < G:
                    stage_d(t - 3)
                if 0 <= t - 2 < G:
                    stage_c(t - 2)
            if 0 <= t - 1 < G:
                stage_b(t - 1)
            if t < G:
                stage_a(t)
        P.store_q = "pool"
        rel = bxt + bxt2
        if mode == "data":
            rel += bkt + byo
        else:
            rel += bko
        return rel

    def layer_hyena(self, i, g):
        P = self.P
        j = i // 2
        L, M, N, H = g.L, g.M, g.N, g.H
        hbin = self.dram["hb0_" + g.name].rearrange("(k p) t -> p k t", p=128)
        x0d = self.dram["x0_" + g.name].rearrange("(k p) t -> p k t", p=128)
        vgd = self.dram["vg_" + g.name].rearrange("(k p) t -> p k t", p=128)
        ycd = self.dram["yc_" + g.name].rearrange("(k p) t -> p k t", p=128)
        kund = self.dram["kun_" + g.name].rearrange("(k p) t -> p k t", p=128)
        rnd = self.dram["rn_" + g.name]
        zt = self.dram["zt_" + g.name]
        tb = self.dram["tb_" + g.name]
        with contextlib.ExitStack() as sl:
            coef = self.sb(sl, "h_coef", [128, 8, 2, 32])
            bcoef = P.buf()
            with contextlib.ExitStack() as st:
                w1 = self.sb(st, "hf_w1", [33, 64])
                w2 = self.sb(st, "hf_w2", [64, 64])
                w3 = self.sb(st, "hf_w3", [64, 2 * D])
                bw = P.buf()
                bw_2 = P.buf()
                bw_3 = P.buf()
                P.dma(lambda h: h.dma_start(out=w1[:], in_=self.dram["hy_filt_w1"][j]), writes=[bw])
                P.dma(lambda h: h.dma_start(out=w2[:], in_=self.dram["hy_filt_w2"][j]), writes=[bw_2])
                P.dma(lambda h: h.dma_start(out=w3[:], in_=self.dram["hy_filt_w3"][j]), writes=[bw_3])
                fbs = self.sb(st, "hf_fbs", [64, 2])
                bfbs = P.buf()
                for q, nm in enumerate(("fb1_%d" % j, "fb2_%d" % j)):
                    P.op("dve", lambda h, q=q, nm=nm: h.tensor_scalar(fbs[:, q:q + 1], self.vcol(nm, 0, 64), self.vcol("ffr_%d" % j, 0, 64), 0.0, ALU.mult, ALU.add),
                         reads=[self.bvecs], writes=[bfbs])
                asum = self.sb(st, "hf_asum", [128, 2, 8, 40])
                basum = P.buf()
                P.op("dve", lambda h: h.memset(asum[:], 0.0), writes=[basum])
                ev = self.sb(st, "hf_ev", [128, 8, 2, 64])
                bev = P.buf()
                zs = [self.sb(st, "hf_z%d" % s, [33, 512]) for s in range(2)]
                bzs = [P.buf() for _ in range(2)]
                ts = [self.sb(st, "hf_t%d" % s, [128, 512]) for s in range(2)]
                bts = [P.buf() for _ in range(2)]
                a1 = [self.sb(st, "hf_a1%d" % s, [64, 512]) for s in range(2)]
                ba1 = [P.buf() for _ in range(2)]
                wr = [self.sb(st, "hf_wr%d" % s, [64, 512]) for s in range(2)]
                bwr = [P.buf() for _ in range(2)]
                a2 = [self.sb(st, "hf_a2%d" % s, [64, 512]) for s in range(2)]
                ba2 = [P.buf() for _ in range(2)]
                win = [self.sb(st, "hf_win%d" % s, [128, 8, 512]) for s in range(2)]
                bwin = [P.buf() for _ in range(2)]
                ko = [self.sb(st, "hf_ko%d" % s, [128, 8, 512]) for s in range(2)]
                bko = [P.buf() for _ in range(2)]
                ftmp = [self.sb(st, "hf_ftmp%d" % s, [128, 512]) for s in range(2)]
                bftmp = [P.buf() for _ in range(2)]
                jobs = [(c0, 512, 0 if c0 < H else 1) for c0 in range(0, N, 512)] + [(N, 64, 2)]
                for ji, (c0, T, kind) in enumerate(jobs):
                    s = ji % 2
                    P.dma(lambda h, s=s, c0=c0, T=T: h.dma_start(out=zs[s][:, 0:T], in_=zt[:, c0:c0 + T]), writes=[bzs[s]])
                    P.dma(lambda h, s=s, c0=c0, T=T: h.dma_start(out=ts[s][:, 0:T], in_=tb[:, c0:c0 + T]), writes=[bts[s]])
                    bq = ji % 2
                    P.op("pe", lambda h, s=s, T=T, bq=bq: h.matmul(self.bank[bq][0:64, 0:T], w1[:], zs[s][:, 0:T], start=True, stop=True), reads=[bw, bzs[s]], writes=[self.bbuf[bq]])
                    P.op("dve", lambda h, s=s, T=T, bq=bq: h.tensor_scalar(a1[s][:, 0:T], self.bank[bq][0:64, 0:T], self.vcol("ffr_%d" % j, 0, 64), fbs[:, 0:1], ALU.mult, ALU.add),
                         reads=[self.bbuf[bq], bfbs, self.bvecs], writes=[ba1[s]])
                    for _w in range(2):
                        P.op("dve", lambda h, s=s, T=T: h.tensor_scalar(wr[s][:, 0:T], a1[s][:, 0:T], float(PI), float(-2 * PI), ALU.is_gt, ALU.mult), reads=[ba1[s]], writes=[bwr[s]])
                        P.op("dve", lambda h, s=s, T=T: h.tensor_tensor(out=a1[s][:, 0:T], in0=a1[s][:, 0:T], in1=wr[s][:, 0:T], op=ALU.add), reads=[ba1[s], bwr[s]], writes=[ba1[s]])
                        P.op("dve", lambda h, s=s, T=T: h.tensor_scalar(wr[s][:, 0:T], a1[s][:, 0:T], float(-PI), float(2 * PI), ALU.is_lt, ALU.mult), reads=[ba1[s]], writes=[bwr[s]])
                        P.op("dve", lambda h, s=s, T=T: h.tensor_tensor(out=a1[s][:, 0:T], in0=a1[s][:, 0:T], in1=wr[s][:, 0:T], op=ALU.add), reads=[ba1[s], bwr[s]], writes=[ba1[s]])
                    P.op("act", lambda h, s=s, T=T: h.activation(out=a1[s][:, 0:T], in_=a1[s][:, 0:T], func=AF.Sin), reads=[ba1[s]], writes=[ba1[s]])
                    P.op("pe", lambda h, s=s, T=T, bq=bq: h.matmul(self.bank[bq][0:64, 0:T], w2[:], a1[s][:, 0:T], start=True, stop=True), reads=[bw_2, ba1[s]], writes=[self.bbuf[bq]])
                    P.op("dve", lambda h, s=s, T=T, bq=bq: h.tensor_scalar(a2[s][:, 0:T], self.bank[bq][0:64, 0:T], self.vcol("ffr_%d" % j, 0, 64), fbs[:, 1:2], ALU.mult, ALU.add),
                         reads=[self.bbuf[bq], bfbs, self.bvecs], writes=[ba2[s]])
                    for _w in range(2):
                        P.op("dve", lambda h, s=s, T=T: h.tensor_scalar(wr[s][:, 0:T], a2[s][:, 0:T], float(PI), float(-2 * PI), ALU.is_gt, ALU.mult), reads=[ba2[s]], writes=[bwr[s]])
                        P.op("dve", lambda h, s=s, T=T: h.tensor_tensor(out=a2[s][:, 0:T], in0=a2[s][:, 0:T], in1=wr[s][:, 0:T], op=ALU.add), reads=[ba2[s], bwr[s]], writes=[ba2[s]])
                        P.op("dve", lambda h, s=s, T=T: h.tensor_scalar(wr[s][:, 0:T], a2[s][:, 0:T], float(-PI), float(2 * PI), ALU.is_lt, ALU.mult), reads=[ba2[s]], writes=[bwr[s]])
                        P.op("dve", lambda h, s=s, T=T: h.tensor_tensor(out=a2[s][:, 0:T], in0=a2[s][:, 0:T], in1=wr[s][:, 0:T], op=ALU.add), reads=[ba2[s], bwr[s]], writes=[ba2[s]])
                    P.op("act", lambda h, s=s, T=T: h.activation(out=a2[s][:, 0:T], in_=a2[s][:, 0:T], func=AF.Sin), reads=[ba2[s]], writes=[ba2[s]])
                    for k in range(8):
                        P.op("act", lambda h, s=s, k=k, T=T: h.activation(out=win[s][:, k, 0:T], in_=ts[s][:, 0:T], func=AF.Exp, scale=self.vcol("ndelta", k)), reads=[bts[s], self.bvecs], writes=[bwin[s]])
                    fbl = [0] if kind == 0 else ([1] if kind == 1 else [0, 1])
                    for fb in fbl:
                        for k in range(8):
                            bk = 2 + (k % 4)
                            ch = fb * 8 + k
                            P.op("pe", lambda h, s=s, ch=ch, bk=bk, T=T: h.matmul(self.bank[bk][:, 0:T], w3[:, ch * 128:(ch + 1) * 128], a2[s][:, 0:T], start=True, stop=True),
                                 reads=[bw_3, ba2[s]], writes=[self.bbuf[bk]])
                            if kind == 2:
                                dst = ev[:, k, fb, 0:T]
                                bdst = bev
                            else:
                                dst = ko[s][:, k, 0:T]
                                bdst = bko[s]
                            tq = (k + fb) % 2
                            P.op("dve", lambda h, ch=ch, bk=bk, T=T, tq=tq: h.tensor_scalar(ftmp[tq][:, 0:T], self.bank[bk][:, 0:T], self.vcol("fb3_%d" % j, ch), None, ALU.add),
                                 reads=[self.bbuf[bk], self.bvecs], writes=[bftmp[tq]])
                            P.op("dve", lambda h, s=s, k=k, T=T, dst=dst, tq=tq: h.scalar_tensor_tensor(out=dst, in0=win[s][:, k, 0:T], scalar=0.05, in1=ftmp[tq][:, 0:T], op0=ALU.add, op1=ALU.mult),
                                 reads=[bftmp[tq], bwin[s]], writes=[bdst])
                            if kind != 2:
                                P.op("dve", lambda h, s=s, k=k, fb=fb, T=T, dst=dst, ji=ji: h.tensor_reduce(out=asum[:, fb, k, ji:ji + 1], in_=dst, axis=AX.X, op=ALU.add, apply_absolute_value=True),
                                     reads=[bdst], writes=[basum])
                    if kind != 2:
                        P.dma(lambda h, s=s, c0=c0, T=T: h.dma_start(out=kund[:, :, c0:c0 + T], in_=ko[s][:, :, 0:T]), reads=[bko[s]], queue="sp")
                nrm = self.sb(st, "hf_nrm", [128, 2, 8])
                bnrm = P.buf()
                ex = self.sb(st, "hf_ex", [128, 2, 8])
                bex = P.buf()
                P.op("dve", lambda h: h.tensor_reduce(out=nrm[:], in_=asum[:], axis=AX.X, op=ALU.add), reads=[basum], writes=[bnrm])
                exb = self.sb(st, "hf_exb", [128, 8])
                bexb = P.buf()
                P.op("dve", lambda h: h.tensor_reduce(out=ex[:, 0, :], in_=ev[:, :, 0, 16:32], axis=AX.X, op=ALU.add, apply_absolute_value=True), reads=[bev], writes=[bex])
                P.op("dve", lambda h: h.tensor_reduce(out=ex[:, 1, :], in_=ev[:, :, 1, 0:32], axis=AX.X, op=ALU.add, apply_absolute_value=True), reads=[bev], writes=[bex])
                P.op("dve", lambda h: h.tensor_reduce(out=exb[:], in_=ev[:, :, 1, 1:17], axis=AX.X, op=ALU.add, apply_absolute_value=True), reads=[bev], writes=[bexb])
                P.op("dve", lambda h: h.tensor_tensor(out=ex[:, 1, :], in0=ex[:, 1, :], in1=exb[:], op=ALU.subtract), reads=[bex, bexb], writes=[bex])
                P.op("dve", lambda h: h.tensor_tensor(out=nrm[:], in0=nrm[:], in1=ex[:], op=ALU.add), reads=[bnrm, bex], writes=[bnrm])
                P.op("dve", lambda h: h.tensor_scalar(nrm[:], nrm[:], 1e-6, None, ALU.add), reads=[bnrm], writes=[bnrm])
                P.op("dve", lambda h: h.reciprocal(out=nrm[:], in_=nrm[:]), reads=[bnrm], writes=[bnrm])
                for k in range(8):
                    for fb in range(2):
                        P.op("dve", lambda h, k=k, fb=fb: h.tensor_scalar(ev[:, k, fb, :], ev[:, k, fb, :], nrm[:, fb, k:k + 1], None, ALU.mult), reads=[bev, bnrm], writes=[bev])
                    P.op("dve", lambda h, k=k: h.tensor_tensor(out=coef[:, k, 0, :], in0=ev[:, k, 0, 0:32], in1=ev[:, k, 1, 32:64], op=ALU.subtract), reads=[bev], writes=[bcoef])
                    P.op("dve", lambda h, k=k: h.tensor_tensor(out=coef[:, k, 1, :], in0=ev[:, k, 1, 0:32], in1=ev[:, k, 0, 32:64], op=ALU.subtract), reads=[bev], writes=[bcoef])
                P.dma(lambda h: h.dma_start(out=rnd.rearrange("f (k p) -> p f k", p=128), in_=nrm[:], allow_slow_non_contiguous=True), reads=[bnrm])
                P.barrier()
                P.release([bw, bw_2, bw_3] + bzs + bts + bko + [bnrm])
            with contextlib.ExitStack() as st:
                C = {}
                for k in ("f1", "c2", "s2", "ns2", "if2a", "if2b", "twc", "tws", "ntws", "twct", "twst", "ntwst", "gc", "gs"):
                    nm = "%s_%s" % (k, g.name)
                    shp = list(self.dram[nm].shape)
                    C[k] = self.load_const(st, nm, shp, FDT if k in ("f1", "c2", "s2", "ns2", "if2a", "if2b", "gc", "gs") else F32)
                crel = [v[1] for v in C.values()] + getattr(self, "const_rel", [])
                self.const_rel = []
                with contextlib.ExitStack() as s1:
                    sc = self.sb(s1, "hf_sc", [128, D])
                    bsc = P.buf()
                    bsc2 = P.buf()
                    P.dma(lambda h: h.dma_start(out=sc[0:M // 2, :], in_=rnd[0:1, :].to_broadcast([M // 2, D])), writes=[bsc])
                    P.dma(lambda h: h.dma_start(out=sc[M // 2:M, :], in_=rnd[1:2, :].to_broadcast([M // 2, D])), writes=[bsc2])
                    bscj = P.buf()
                    P.op("pool", lambda h: h.tensor_copy(out=sc[0:1, 0:1], in_=sc[0:1, 0:1]), reads=[bsc, bsc2], writes=[bscj])
                    rel = self.fft_pass(g, s1, C, self.dram["kun_" + g.name], M, "filter", {"sc": sc, "bsc": bscj})
                    P.barrier()
                    P.release(rel + [bsc, bsc2])
                if self.dbg.get("stop_h") == "filter":
                    P.release(crel)
                    return
                with contextlib.ExitStack() as s1:
                    wd = self.dram["b_hy_w_in"]
                    w = self.sb(s1, "h_w", [128, 8, 3 * D], BF16)
                    bw = P.buf()
                    P.dma(lambda h: h.dma_start(out=w[:], in_=wd[j].rearrange("(k p) d -> p k d", p=128)), writes=[bw])
                    hb = [self.sb(s1, "h_hb%d" % s, [128, 8, 512], BF16) for s in range(2)]
                    bhb = [P.buf() for _ in range(2)]
                    cv = [self.sb(s1, "h_cv%d" % s, [128, 512]) for s in range(3)]
                    bcv = [P.buf() for _ in range(3)]
                    x0o = [self.sb(s1, "h_x0o%d" % s, [128, 8, 512]) for s in range(2)]
                    bx0o = [P.buf() for _ in range(2)]
                    vgo = [self.sb(s1, "h_vgo%d" % s, [128, 8, 512]) for s in range(2)]
                    bvgo = [P.buf() for _ in range(2)]
                    tiles = tiles_even(L, 510)
                    bc_ = 0
                    for ti, (t0, T) in enumerate(tiles):
                        s = ti % 2
                        P.dma(lambda h, s=s, t0=t0, T=T: h.dma_start(out=hb[s][:, :, 0:T + 2], in_=hbin[:, :, t0:t0 + T + 2]), writes=[bhb[s]])
                        for k in range(8):
                            for part in (1, 2, 0):
                                n = part * 8 + k
                                bk = bc_ % 4
                                bc_ += 1
                                for kk in range(8):
                                    P.op("pe", lambda h, s=s, kk=kk, n=n, bk=bk, T=T: h.matmul(self.bank[bk][:, 0:T + 2], w[:, kk, n * 128:(n + 1) * 128], hb[s][:, kk, 0:T + 2], start=(kk == 0), stop=(kk == 7)),
                                         reads=[bw, bhb[s]], writes=[self.bbuf[bk]])
                                if part == 1:
                                    dst, bd_ = cv[0][:, 0:T], bcv[0]
                                elif part == 2:
                                    dst, bd_ = cv[1][:, 0:T], bcv[1]
                                else:
                                    dst, bd_ = x0o[s][:, k, 0:T], bx0o[s]
                                e = "dve"
                                P.op("act", lambda h, n=n, bk=bk, T=T, dst=dst: h.activation(out=dst, in_=self.bank[bk][:, 1:T + 1], func=AF.Identity, scale=self.vcol("hcw%d_1" % j, n), bias=self.vcol("hcb%d" % j, n)),
                                     reads=[self.bbuf[bk], self.bvecs], writes=[bd_])
                                P.op(e, lambda h, n=n, bk=bk, T=T, dst=dst: h.scalar_tensor_tensor(out=dst, in0=self.bank[bk][:, 0:T], scalar=self.vcol("hcw%d_0" % j, n), in1=dst, op0=ALU.mult, op1=ALU.add),
                                     reads=[self.bbuf[bk], bd_], writes=[bd_])
                                P.op(e, lambda h, n=n, bk=bk, T=T, dst=dst: h.scalar_tensor_tensor(out=dst, in0=self.bank[bk][:, 2:T + 2], scalar=self.vcol("hcw%d_2" % j, n), in1=dst, op0=ALU.mult, op1=ALU.add),
                                     reads=[self.bbuf[bk], bd_], writes=[bd_])
                                if part == 2:
                                    P.op("pool", lambda h, s=s, k=k, T=T: h.tensor_tensor(out=vgo[s][:, k, 0:T], in0=cv[0][:, 0:T], in1=cv[1][:, 0:T], op=ALU.mult), reads=[bcv[0], bcv[1]], writes=[bvgo[s]])
                        P.dma(lambda h, s=s, t0=t0, T=T: h.dma_start(out=x0d[:, :, t0:t0 + T], in_=x0o[s][:, :, 0:T]), reads=[bx0o[s]])
                        P.dma(lambda h, s=s, t0=t0, T=T: h.dma_start(out=vgd[:, :, t0:t0 + T], in_=vgo[s][:, :, 0:T]), reads=[bvgo[s]])
                    P.barrier()
                    P.release([bw] + bhb + bx0o + bvgo)
                if self.dbg.get("stop_h") == "inproj":
                    P.release(crel)
                    return
                with contextlib.ExitStack() as s1:
                    rel = self.fft_pass(g, s1, C, self.dram["vg_" + g.name], g.nin, "data", None)
                    P.barrier()
                    P.release(rel)
                P.release(crel)
            if self.dbg.get("stop_h") == "fft":
                return
            def prologue(what, arg):
                if what == "alloc":
                    st = arg
                    E = {}
                    E["ycs"] = [self.sb(st, "hp_yc%d" % s, [128, 8, 512]) for s in range(2)]
                    E["bycs"] = [P.buf() for _ in range(2)]
                    E["vgs"] = [self.sb(st, "hp_vg%d" % s, [128, 8, 512]) for s in range(2)]
                    E["bvgs"] = [P.buf() for _ in range(2)]
                    E["x0s"] = [self.sb(st, "hp_x0%d" % s, [128, 8, 512]) for s in range(2)]
                    E["bx0s"] = [P.buf() for _ in range(2)]
                    E["vh"] = self.sb(st, "hp_vh", [128, 8, 16])
                    E["vt"] = self.sb(st, "hp_vt", [128, 8, 16])
                    E["bvh"] = P.buf()
                    E["bvt"] = P.buf()
                    P.dma(lambda h: h.dma_start(out=E["vh"][:], in_=vgd[:, :, 0:16]), writes=[E["bvh"]])
                    P.dma(lambda h: h.dma_start(out=E["vt"][:], in_=vgd[:, :, L - 16:L]), writes=[E["bvt"]])
                    return E
                if what == "bufs":
                    E = arg
                    return E["bycs"] + E["bvgs"] + E["bx0s"] + [E["bvh"], E["bvt"]]
                ti, t0, T, xin, bxin, E = arg
                s = ti % 2
                yt, byt = E["ycs"][s], E["bycs"][s]
                P.dma(lambda h: h.dma_start(out=yt[:, :, 0:T], in_=ycd[:, :, t0:t0 + T]), writes=[byt])
                P.dma(lambda h: h.dma_start(out=E["vgs"][s][:, :, 0:T], in_=vgd[:, :, t0:t0 + T]), writes=[E["bvgs"][s]])
                P.dma(lambda h: h.dma_start(out=E["x0s"][s][:, :, 0:T], in_=x0d[:, :, t0:t0 + T]), writes=[E["bx0s"][s]])
                for k in range(8):
                    e = "dve"
                    if t0 == 0:
                        for m in range(H + 1, H + 16):
                            ee = m - H + 16
                            nt = L - m
                            so = m - (L - 16)
                            P.op(e, lambda h, k=k, ee=ee, nt=nt, so=so: h.scalar_tensor_tensor(out=yt[:, k, 0:nt], in0=E["vt"][:, k, so:so + nt], scalar=coef[:, k, 1, ee:ee + 1], in1=yt[:, k, 0:nt], op0=ALU.mult, op1=ALU.add),
                                 reads=[E["bvt"], bcoef, byt], writes=[byt])
                    if t0 + T == L and T == 16:
                        for d in range(H, H + 16):
                            ee = d - H + 16
                            nt = L - d
                            to = d - t0
                            P.op(e, lambda h, k=k, ee=ee, nt=nt, to=to: h.scalar_tensor_tensor(out=yt[:, k, to:to + nt], in0=E["vh"][:, k, 0:nt], scalar=coef[:, k, 0, ee:ee + 1], in1=yt[:, k, to:to + nt], op0=ALU.mult, op1=ALU.add),
                                 reads=[E["bvh"], bcoef, byt], writes=[byt])
                    P.op(e, lambda h, k=k: h.scalar_tensor_tensor(out=yt[:, k, 0:T], in0=E["vgs"][s][:, k, 0:T], scalar=self.vcol("skip%d" % j, k), in1=yt[:, k, 0:T], op0=ALU.mult, op1=ALU.add),
                         reads=[E["bvgs"][s], byt, self.bvecs], writes=[byt])
                    P.op("pool", lambda h, k=k: h.tensor_tensor(out=xin[:, k, 0:T], in0=yt[:, k, 0:T], in1=E["x0s"][s][:, k, 0:T], op=ALU.mult),
                         reads=[byt, E["bx0s"][s]], writes=[bxin])
            self.phase_proj(i, g, "b_hy_w_out", j, prologue=prologue)

    def phase_output(self, g):
        P = self.P
        hf = self.dram["hf0_" + g.name].rearrange("(k p) t -> p k t", p=128)
        y = self.dram["y_" + g.name]
        with contextlib.ExitStack() as st:
            xi = [self.sb(st, "o_x%d" % s, [128, 8, 512]) for s in range(2)]
            bxi = [P.buf() for _ in range(2)]
            yo = [self.sb(st, "o_y%d" % s, [128, 4, D]) for s in range(2)]
            byo = [P.buf() for _ in range(2)]
            for ti, t0 in enumerate(range(0, g.n, 512)):
                s = ti % 2
                P.dma(lambda h, s=s, t0=t0: h.dma_start(out=xi[s][:], in_=hf[:, :, NMETA + t0:NMETA + t0 + 512]), writes=[bxi[s]])
                for jj in range(4):
                    for k in range(8):
                        bk = (jj % 2) * 2 + k // 4
                        P.op("pe", lambda h, s=s, jj=jj, k=k, bk=bk: h.transpose(self.bank[bk][:, (k % 4) * 128:(k % 4 + 1) * 128], xi[s][:, k, jj * 128:(jj + 1) * 128], self.ident[:]),
                             reads=[bxi[s], self.bident], writes=[self.bbuf[bk]])
                        if k % 4 == 3:
                            e = "dve" if k == 3 else "act"
                            if e == "dve":
                                P.op("dve", lambda h, s=s, jj=jj, k=k, bk=bk: h.tensor_copy(out=yo[s][:, jj, (k // 4) * 512:(k // 4 + 1) * 512], in_=self.bank[bk][:]), reads=[self.bbuf[bk]], writes=[byo[s]])
                            else:
                                P.op("act", lambda h, s=s, jj=jj, k=k, bk=bk: h.copy(out=yo[s][:, jj, (k // 4) * 512:(k // 4 + 1) * 512], in_=self.bank[bk][:]), reads=[self.bbuf[bk]], writes=[byo[s]])
                P.dma(lambda h, s=s, t0=t0: h.dma_start(out=y[t0:t0 + 512, :].rearrange("(j p) d -> p j d", p=128), in_=yo[s][:]), reads=[byo[s]])
            P.barrier()
            P.release(bxi + byo)


GROUPS = None


def make_consts(groups):
    c = {}
    for g in groups:
        for k, v in fft_consts(g).items():
            c["%s_%s" % (k, g.name)] = v
        zt, tb = filter_grids(g)
        c["zt_" + g.name] = zt
        c["tb_" + g.name] = tb
        cs, sn = rope_tables(g)
        c["cos_" + g.name] = cs
        c["sin_" + g.name] = sn
    c["ident"] = np.eye(128, dtype=np.float32)
    return c


WNAMES = ("meta_tokens", "hy_w_in", "hy_w_out", "hy_filt_w1", "hy_filt_w2", "hy_filt_w3", "at_w_qkv", "at_w_out", "ffn_w_in", "ffn_w_out")


def kernel(**inputs):
    inp = {k: np.asarray(v) for k, v in inputs.items()}
    groups = [Group("p", inp["x_prompt"].shape[1]), Group("s", inp["x_sample"].shape[1])]
    vm, vecs = build_vecs(inp)
    consts = make_consts(groups)
    b = Builder(groups, vm)
    nc = b.build()
    ncore = 8
    in_maps = []
    for c in range(ncore):
        m = dict(consts)
        m["vecs"] = vecs
        for w in WNAMES:
            m[w] = np.ascontiguousarray(inp[w], dtype=np.float32)
        m["x_p"] = np.ascontiguousarray(inp["x_prompt"][c], dtype=np.float32)
        m["x_s"] = np.ascontiguousarray(inp["x_sample"][c], dtype=np.float32)
        in_maps.append(m)
    res = run_bass_kernel_spmd(nc, in_maps, core_ids=list(range(ncore)))
    yp = np.stack([np.asarray(r["y_p"]) for r in res.results], axis=0).astype(np.float32)
    ys = np.stack([np.asarray(r["y_s"]) for r in res.results], axis=0).astype(np.float32)
    return (yp, ys)
```
